# Optimizing a Trainium2 kernel written in Bass

```python
import math
import jax
import jax.numpy as jnp
from jax import lax
import numpy as np

D_MODEL = 1024
BATCH = 32
SEQ = 2048
DEPTH = 2

CTX_LEN = 256
GRID_W = 64
NORM_EPS = 1e-6
N_BRANCH = 3
N_MOD = 6
BRANCH_DIM = D_MODEL // 2

SGU_GROUP_DIM = 128
SGU_GROUPS = BRANCH_DIM // SGU_GROUP_DIM
SGU_DIM = SGU_GROUPS * SGU_GROUP_DIM
SGU_CHUNK = 128

NA_HEAD_DIM = 64
NA_HEADS = BRANCH_DIM // NA_HEAD_DIM
NA_DIM = NA_HEADS * NA_HEAD_DIM
NA_WIN_ROWS = 8
NA_WIN_COLS = 16
ROPE_THETA = 10000.0
NEG_INF = -1e30

HY_DIM = BRANCH_DIM
HY_ORDER = 2
HY_EMB = 33
HY_FILTER_HIDDEN = 64
HY_SHORT_CONV = 3
HY_FAST_DECAY_PCT = 0.3
HY_SLOW_DECAY_PCT = 1.5
HY_DECAY_TARGET = 1e-2

D_FF = 128 * ((8 * D_MODEL + 3 * 128 - 1) // (3 * 128))
FFN_CONV = 3

OFF_SGU = 0
OFF_QKV = OFF_SGU + 2 * SGU_DIM
OFF_HY = OFF_QKV + 3 * NA_DIM
OFF_GATE = OFF_HY + (HY_ORDER + 1) * HY_DIM
IN_COLS = OFF_GATE + N_BRANCH * D_MODEL

kernel_name = 'hybrid_diffusion_trunk'


def rmsnorm(x, g):
    x32 = x.astype(jnp.float32)
    y = x32 * lax.rsqrt(jnp.mean(x32 * x32, axis=-1, keepdims=True) + NORM_EPS)
    return y.astype(x.dtype) * g


def modulate(x, shift, scale):
    return x * (1 + scale) + shift


def depthwise_conv_centered(x, w, b):
    k_w = w.shape[0]
    L = x.shape[1]
    left = (k_w - 1) // 2
    xp = jnp.pad(x, ((0, 0), (left, k_w - 1 - left), (0, 0)))
    y = xp[:, 0:L] * w[0]
    for j in range(1, k_w):
        y = y + xp[:, j:j + L] * w[j]
    return y + b


def spatial_gating_mixer(uv, norm_g, w_s, b_s):
    B, L, _ = uv.shape
    u, v = jnp.split(jax.nn.gelu(uv, approximate=False), 2, axis=-1)
    v = rmsnorm(v, norm_g).reshape(B, L // SGU_CHUNK, SGU_CHUNK, SGU_GROUPS, SGU_GROUP_DIM)
    v = jnp.einsum('gqp,bnpgc->bnqgc', w_s, v) + b_s.T[None, None, :, :, None]
    return u * v.reshape(B, L, SGU_DIM)


def hyena_filter_spectrum(L, w1, b1, w2, b2, w3, sin_freq):
    f32 = jnp.float32
    t = jnp.linspace(0.0, 1.0, L, dtype=f32)[:, None]
    n_bands = (HY_EMB - 1) // 2
    bands = jnp.linspace(1e-4, n_bands - 1, n_bands, dtype=f32)[None, :]
    ang = (2.0 * math.pi) * jnp.arange(L, dtype=f32)[:, None] / L * bands
    z = jnp.concatenate([t, jnp.cos(ang), -jnp.sin(ang)], axis=-1)
    freq = sin_freq.astype(f32)
    h = jnp.sin(freq * (z @ w1.astype(f32) + b1.astype(f32)))
    h = jnp.sin(freq * (h @ w2.astype(f32) + b2.astype(f32)))
    h = (h @ w3.astype(f32)).reshape(L, HY_ORDER, 2, HY_DIM)
    max_decay = math.log(HY_DECAY_TARGET) / HY_FAST_DECAY_PCT
    min_decay = math.log(HY_DECAY_TARGET) / HY_SLOW_DECAY_PCT
    deltas = jnp.abs(jnp.linspace(min_decay, max_decay, HY_DIM, dtype=f32))
    h = h * jnp.exp(-t * deltas)[:, None, None, :]
    k = jnp.concatenate([h[:, :, 0], jnp.zeros((1, HY_ORDER, HY_DIM), f32), h[:0:-1, :, 1]], axis=0)
    k = k / jnp.sum(jnp.abs(k), axis=0, keepdims=True)
    return jnp.fft.rfft(k, axis=0)


def long_conv(z, kf, skip):
    L = z.shape[1]
    zf = jnp.fft.rfft(z, n=2 * L, axis=1)
    y = jnp.fft.irfft(zf * kf[None], n=2 * L, axis=1)[:, :L]
    return y + z * skip


def hyena_mixer(p, lp):
    L = p.shape[1]
    pc = depthwise_conv_centered(p, lp['hy_conv_w'], lp['hy_conv_b']).astype(jnp.float32)
    parts = jnp.split(pc, HY_ORDER + 1, axis=-1)
    kf = hyena_filter_spectrum(L, lp['hy_filt_w1'], lp['hy_filt_b1'], lp['hy_filt_w2'],
                               lp['hy_filt_b2'], lp['hy_filt_w3'], lp['hy_sin_freq'])
    skip = lp['hy_skip'].astype(jnp.float32)
    z = parts[0]
    for n in range(HY_ORDER):
        z = parts[n + 1] * long_conv(z, kf[:, n], skip[n])
    return z.astype(p.dtype)


def axial_rope(x, rows, cols):
    f32 = jnp.float32
    half = NA_HEAD_DIM // 2
    quarter = half // 2
    inv_freq = ROPE_THETA ** (-jnp.arange(quarter, dtype=f32) * 2.0 / half)

    def rotate(xa, pos):
        ang = pos.astype(f32)[:, None] * inv_freq[None, :]
        cos = jnp.cos(ang)[None, :, None, :]
        sin = jnp.sin(ang)[None, :, None, :]
        a, b = xa[..., :quarter], xa[..., quarter:]
        return jnp.concatenate([a * cos - b * sin, a * sin + b * cos], axis=-1)

    x32 = x.astype(f32)
    out = jnp.concatenate([rotate(x32[..., :half], rows), rotate(x32[..., half:], cols)], axis=-1)
    return out.astype(x.dtype)


def neighborhood_attention(q, k, v, kc, vc, rpb):
    B, L, H, hd = q.shape
    rows = L // GRID_W
    kr = min(NA_WIN_ROWS, rows)
    scale = hd ** -0.5
    qb = q.reshape(B, rows, GRID_W, H, hd)
    kb = k.reshape(B, rows, GRID_W, H, hd)
    vb = v.reshape(B, rows, GRID_W, H, hd)
    col = jnp.arange(GRID_W)
    cstart = jnp.clip(col - NA_WIN_COLS // 2, 0, GRID_W - NA_WIN_COLS)
    col_mask = (col[None, :] >= cstart[:, None]) & (col[None, :] < cstart[:, None] + NA_WIN_COLS)
    dc_idx = jnp.clip(col[None, :] - col[:, None] + NA_WIN_COLS - 1, 0, 2 * NA_WIN_COLS - 2)

    def row_block(r):
        r0 = jnp.clip(r - kr // 2, 0, rows - kr)
        q_r = lax.dynamic_index_in_dim(qb, r, axis=1, keepdims=False)
        k_r = lax.dynamic_slice_in_dim(kb, r0, kr, axis=1)
        v_r = lax.dynamic_slice_in_dim(vb, r0, kr, axis=1)
        dr_idx = r0 + jnp.arange(kr) - r + NA_WIN_ROWS - 1
        bias = rpb[:, dr_idx][:, :, dc_idx].transpose(0, 2, 1, 3)
        s_loc = jnp.einsum('bqhd,brkhd->bhqrk', q_r, k_r).astype(jnp.float32) * scale
        s_loc = s_loc + bias[None].astype(jnp.float32)
        s_loc = jnp.where(col_mask[None, None, :, None, :], s_loc, NEG_INF)
        s_ctx = jnp.einsum('bqhd,bchd->bhqc', q_r, kc).astype(jnp.float32) * scale
        s = jnp.concatenate([s_loc.reshape(B, H, GRID_W, kr * GRID_W), s_ctx], axis=-1)
        p = jax.nn.softmax(s, axis=-1).astype(v.dtype)
        p_loc = p[..., :kr * GRID_W].reshape(B, H, GRID_W, kr, GRID_W)
        p_ctx = p[..., kr * GRID_W:]
        return (jnp.einsum('bhqrk,brkhd->bqhd', p_loc, v_r)
                + jnp.einsum('bhqc,bchd->bqhd', p_ctx, vc))

    out = lax.map(row_block, jnp.arange(rows))
    return out.transpose(1, 0, 2, 3, 4).reshape(B, L, H * hd)


def context_attention(qc, kc, vc):
    B, Lc, H, hd = qc.shape
    s = jnp.einsum('bqhd,bkhd->bhqk', qc, kc).astype(jnp.float32) * (hd ** -0.5)
    p = jax.nn.softmax(s, axis=-1).astype(vc.dtype)
    return jnp.einsum('bhqk,bkhd->bqhd', p, vc).reshape(B, Lc, H * hd)


def split_heads(t):
    B, L, _ = t.shape
    return t.reshape(B, L, NA_HEADS, NA_HEAD_DIM)


def merge_branches(p, attn, lp):
    a = spatial_gating_mixer(p[..., OFF_SGU:OFF_QKV], lp['sgu_norm_g'], lp['sgu_w'], lp['sgu_b'])
    hy = hyena_mixer(p[..., OFF_HY:OFF_GATE], lp)
    g_a, g_b, g_c = jnp.split(jax.nn.sigmoid(p[..., OFF_GATE:]), N_BRANCH, axis=-1)
    wb = lp['w_branch']
    m = g_a * (a @ wb[0]) + g_b * (attn @ wb[1]) + g_c * (hy @ wb[2])
    return m @ lp['w_out']


def conv_ffn(h, lp):
    a, b = jnp.split(h @ lp['ffn_w_up'], 2, axis=-1)
    a = depthwise_conv_centered(a, lp['ffn_conv_w'], lp['ffn_conv_b'])
    return (jax.nn.gelu(a, approximate=False) * b) @ lp['ffn_w_down']


def layer_forward(x, xc, mod, mod_c, lp, update_ctx):
    B, L, _ = x.shape
    sh1, sc1, g1, sh2, sc2, g2 = jnp.split(mod, N_MOD, axis=-1)
    csh1, csc1, cg1, csh2, csc2, cg2 = jnp.split(mod_c, N_MOD, axis=-1)

    hc = modulate(rmsnorm(xc, lp['norm1_g']), csh1, csc1)
    if update_ctx:
        pc = hc @ lp['w_in']
        qc = split_heads(pc[..., OFF_QKV:OFF_QKV + NA_DIM])
        kc = split_heads(pc[..., OFF_QKV + NA_DIM:OFF_QKV + 2 * NA_DIM])
        vc = split_heads(pc[..., OFF_QKV + 2 * NA_DIM:OFF_HY])
    else:
        kvc = hc @ lp['w_in'][:, OFF_QKV + NA_DIM:OFF_HY]
        kc = split_heads(kvc[..., :NA_DIM])
        vc = split_heads(kvc[..., NA_DIM:])

    h = modulate(rmsnorm(x, lp['norm1_g']), sh1, sc1)
    p = h @ lp['w_in']
    pos = jnp.arange(L)
    rows, cols = pos // GRID_W, pos % GRID_W
    q = axial_rope(split_heads(p[..., OFF_QKV:OFF_QKV + NA_DIM]), rows, cols)
    k = axial_rope(split_heads(p[..., OFF_QKV + NA_DIM:OFF_QKV + 2 * NA_DIM]), rows, cols)
    v = split_heads(p[..., OFF_QKV + 2 * NA_DIM:OFF_HY])
    attn = neighborhood_attention(q, k, v, kc, vc, lp['na_rpb'])
    x = x + g1 * merge_branches(p, attn, lp)
    h = modulate(rmsnorm(x, lp['norm2_g']), sh2, sc2)
    x = x + g2 * conv_ffn(h, lp)

    if update_ctx:
        xc = xc + cg1 * merge_branches(pc, context_attention(qc, kc, vc), lp)
        hc2 = modulate(rmsnorm(xc, lp['norm2_g']), csh2, csc2)
        xc = xc + cg2 * conv_ffn(hc2, lp)
    return x, xc


def setup_inputs(seed: int = 0) -> dict:
    key = jax.random.key(seed)
    ks = jax.random.split(key, 32)
    f32 = jnp.float32
    D = D_MODEL

    def nrm(k, shape, scale):
        return jax.random.normal(k, shape, f32) * scale

    return {
        'x': nrm(ks[0], (BATCH, SEQ, D), 1.0),
        'c': nrm(ks[1], (BATCH, D), 1.0),
        'ctx': nrm(ks[2], (BATCH, CTX_LEN, D), 1.0),
        'c_ctx': nrm(ks[3], (D,), 1.0),
        'ada_w': nrm(ks[4], (DEPTH, D, N_MOD * D), D ** -0.5),
        'ada_b': nrm(ks[5], (DEPTH, N_MOD * D), 0.02),
        'norm1_g': 1.0 + nrm(ks[6], (DEPTH, D), 0.02),
        'norm2_g': 1.0 + nrm(ks[7], (DEPTH, D), 0.02),
        'w_in': nrm(ks[8], (DEPTH, D, IN_COLS), D ** -0.5),
        'sgu_norm_g': 1.0 + nrm(ks[9], (DEPTH, SGU_DIM), 0.02),
        'sgu_w': nrm(ks[10], (DEPTH, SGU_GROUPS, SGU_CHUNK, SGU_CHUNK), 0.5 * SGU_CHUNK ** -0.5),
        'sgu_b': 1.0 + nrm(ks[11], (DEPTH, SGU_GROUPS, SGU_CHUNK), 0.02),
        'na_rpb': nrm(ks[12], (DEPTH, NA_HEADS, 2 * NA_WIN_ROWS - 1, 2 * NA_WIN_COLS - 1), 0.1),
        'hy_conv_w': nrm(ks[13], (DEPTH, HY_SHORT_CONV, (HY_ORDER + 1) * HY_DIM), HY_SHORT_CONV ** -0.5),
        'hy_conv_b': nrm(ks[14], (DEPTH, (HY_ORDER + 1) * HY_DIM), 0.02),
        'hy_filt_w1': nrm(ks[15], (DEPTH, HY_EMB, HY_FILTER_HIDDEN), HY_EMB ** -0.5),
        'hy_filt_b1': nrm(ks[16], (DEPTH, HY_FILTER_HIDDEN), 0.02),
        'hy_filt_w2': nrm(ks[17], (DEPTH, HY_FILTER_HIDDEN, HY_FILTER_HIDDEN), HY_FILTER_HIDDEN ** -0.5),
        'hy_filt_b2': nrm(ks[18], (DEPTH, HY_FILTER_HIDDEN), 0.02),
        'hy_filt_w3': nrm(ks[19], (DEPTH, HY_FILTER_HIDDEN, HY_ORDER * 2 * HY_DIM), HY_FILTER_HIDDEN ** -0.5),
        'hy_sin_freq': 1.0 + nrm(ks[20], (DEPTH, HY_FILTER_HIDDEN), 0.02),
        'hy_skip': nrm(ks[21], (DEPTH, HY_ORDER, HY_DIM), 1.0),
        'w_branch': nrm(ks[22], (DEPTH, N_BRANCH, BRANCH_DIM, D), BRANCH_DIM ** -0.5),
        'w_out': nrm(ks[23], (DEPTH, D, D), D ** -0.5),
        'ffn_w_up': nrm(ks[24], (DEPTH, D, 2 * D_FF), D ** -0.5),
        'ffn_conv_w': nrm(ks[25], (DEPTH, FFN_CONV, D_FF), FFN_CONV ** -0.5),
        'ffn_conv_b': nrm(ks[26], (DEPTH, D_FF), 0.02),
        'ffn_w_down': nrm(ks[27], (DEPTH, D_FF, D), D_FF ** -0.5),
        'final_norm_g': 1.0 + nrm(ks[28], (D,), 0.02),
    }


def reference(x, c, ctx, c_ctx, ada_w, ada_b, norm1_g, norm2_g, w_in, sgu_norm_g, sgu_w, sgu_b,
              na_rpb, hy_conv_w, hy_conv_b, hy_filt_w1, hy_filt_b1, hy_filt_w2, hy_filt_b2,
              hy_filt_w3, hy_sin_freq, hy_skip, w_branch, w_out, ffn_w_up, ffn_conv_w, ffn_conv_b,
              ffn_w_down, final_norm_g):
    xc = ctx
    for i in range(DEPTH):
        lp = {
            'norm1_g': norm1_g[i], 'norm2_g': norm2_g[i], 'w_in': w_in[i],
            'sgu_norm_g': sgu_norm_g[i], 'sgu_w': sgu_w[i], 'sgu_b': sgu_b[i],
            'na_rpb': na_rpb[i], 'hy_conv_w': hy_conv_w[i], 'hy_conv_b': hy_conv_b[i],
            'hy_filt_w1': hy_filt_w1[i], 'hy_filt_b1': hy_filt_b1[i], 'hy_filt_w2': hy_filt_w2[i],
            'hy_filt_b2': hy_filt_b2[i], 'hy_filt_w3': hy_filt_w3[i], 'hy_sin_freq': hy_sin_freq[i],
            'hy_skip': hy_skip[i], 'w_branch': w_branch[i], 'w_out': w_out[i],
            'ffn_w_up': ffn_w_up[i], 'ffn_conv_w': ffn_conv_w[i], 'ffn_conv_b': ffn_conv_b[i],
            'ffn_w_down': ffn_w_down[i],
        }
        mod = (jax.nn.silu(c) @ ada_w[i] + ada_b[i])[:, None, :]
        mod_c = (jax.nn.silu(c_ctx) @ ada_w[i] + ada_b[i])[None, None, :]
        x, xc = layer_forward(x, xc, mod, mod_c, lp, i < DEPTH - 1)
    return rmsnorm(x, final_norm_g)
```

```python
import math
from contextlib import ExitStack

import numpy as np
import ml_dtypes

import concourse.bass as bass
import concourse.mybir as mybir
from concourse.bass_utils import run_bass_kernel_spmd

F32 = mybir.dt.float32
BF16 = mybir.dt.bfloat16
AF = mybir.ActivationFunctionType
ALU = mybir.AluOpType

D = 1024
L = 2048
LC = 256
DFF = 2816
NH = 8
HD = 64
GRID_W = 64
NROWS = 32
EPS = 1e-6
NCORES = 8
NEG8 = -240000.0


class Sched:
    ENG = ('pe', 'act', 'dve', 'pool', 'sp')
    DMA_BW = 140.0
    DMA_LAT = 2000.0
    frozen = False
    reorder = True

    def __init__(self, nc, nsem=4, epoch=4096):
        self.nc = nc
        self.e = {'pe': nc.tensor, 'act': nc.scalar, 'dve': nc.vector, 'pool': nc.gpsimd, 'sp': nc.sync}
        self.ops = []
        self.labels = []
        self.costs = []
        self.phase = 'init'
        self.nsem = nsem
        self.epoch = epoch

    def op(self, eng, fn, reads=(), writes=(), dma=None, cost=200.0, nbytes=0):
        if self.frozen:
            return
        self.ops.append((eng, fn, tuple(reads), tuple(writes), dma))
        self.labels.append(self.phase)
        self.costs.append((cost, nbytes))

    def barrier(self):
        if self.frozen:
            return
        self.ops.append(('barrier', None, (), (), None))
        self.labels.append(self.phase)
        self.costs.append((0.0, 0))

    def _schedule(self, seg, deps_all):
        import heapq
        ops = self.ops
        succ = {i: [] for i in seg}
        indeg = {i: 0 for i in seg}
        for i in seg:
            for j in deps_all[i]:
                succ[j].append(i)
                indeg[i] += 1
        eng_free = {e: 0.0 for e in self.ENG}
        dma_free = 0.0
        dep_ready = {i: 0.0 for i in seg}
        ready = []
        for i in seg:
            if indeg[i] == 0:
                heapq.heappush(ready, (0.0, i))
        out = []
        while ready:
            est, i = heapq.heappop(ready)
            eng = ops[i][0]
            t0 = max(eng_free[eng], dep_ready[i])
            if t0 > est + 1e-6:
                heapq.heappush(ready, (t0, i))
                continue
            cost, nbytes = self.costs[i]
            if ops[i][4] is not None:
                eng_free[eng] = t0 + 60.0
                ts = max(t0 + 60.0, dma_free)
                dma_free = ts + nbytes / self.DMA_BW
                fin = dma_free + self.DMA_LAT
            else:
                eng_free[eng] = t0 + cost
                fin = t0 + cost
            out.append(i)
            for s_ in succ[i]:
                lat = 40.0 if (ops[s_][0] == eng and ops[i][4] is None) else 160.0
                if eng == 'pe' and ops[s_][0] == 'pe' and ops[i][4] is None:
                    lat = 0.0
                r_ = fin + lat
                if r_ > dep_ready[s_]:
                    dep_ready[s_] = r_
                indeg[s_] -= 1
                if indeg[s_] == 0:
                    heapq.heappush(ready, (max(dep_ready[s_], eng_free[ops[s_][0]]), s_))
        assert len(out) == len(seg)
        return out, max(max(eng_free.values()), dma_free)

    def emit(self):
        nc = self.nc
        ops = self.ops
        n = len(ops)
        last_w = {}
        readers = {}
        deps_all = [None] * n
        segments = []
        cur = []
        for i, (eng, fn, reads, writes, dma) in enumerate(ops):
            if eng == 'barrier':
                segments.append((cur, i))
                cur = []
                last_w = {}
                readers = {}
                continue
            d = set()
            for k in reads:
                w = last_w.get(k)
                if w is not None:
                    d.add(w)
            for k in writes:
                w = last_w.get(k)
                if w is not None:
                    d.add(w)
                rr = readers.get(k)
                if rr:
                    d.update(rr)
            d.discard(i)
            deps_all[i] = d
            for k in writes:
                last_w[k] = i
                readers[k] = []
            for k in reads:
                readers.setdefault(k, []).append(i)
            cur.append(i)
        assert not cur, "program must end with a barrier"
        order = []
        model_ns = 0.0
        for seg, bi in segments:
            if self.reorder and seg:
                seg2, t = self._schedule(seg, deps_all)
                model_ns += t
            else:
                seg2 = seg
            order.extend(seg2)
            order.append(bi)
        need_sig = [False] * n
        fdeps = [None] * n
        has_dep = [False] * n
        segops = []
        for i in order:
            eng, fn, reads, writes, dma = ops[i]
            if eng == 'barrier':
                lastc = {}
                out = []
                for j in segops:
                    if ops[j][4] is None:
                        lastc[ops[j][0]] = j
                    elif not has_dep[j]:
                        out.append(j)
                out.extend(lastc.values())
                for j in out:
                    need_sig[j] = True
                fdeps[i] = sorted(out)
                segops = []
                continue
            out = []
            for j in deps_all[i]:
                has_dep[j] = True
                eng_j, _, _, _, dma_j = ops[j]
                if dma_j is None and dma is None and eng_j == eng and eng == 'pe':
                    continue
                out.append(j)
                need_sig[j] = True
            fdeps[i] = out
            segops.append(i)
        ticket = [None] * n
        cnt = {e: 0 for e in self.ENG}
        semc = {}

        def getsem(name):
            if name not in semc:
                semc[name] = nc.alloc_semaphore(name)
            return semc[name]
        semval = {}
        for i in order:
            eng, fn, reads, writes, dma = ops[i]
            if eng == 'barrier':
                continue
            if dma is not None:
                nm = 'd_' + str(dma)
                semval[nm] = semval.get(nm, 0) + 16
                ticket[i] = (nm, semval[nm])
            elif need_sig[i]:
                ep = (cnt[eng] // self.epoch) % self.nsem
                cnt[eng] += 1
                nm = 'c_%s%d' % (eng, ep)
                semval[nm] = semval.get(nm, 0) + 1
                ticket[i] = (nm, semval[nm])
        waited = {e: {} for e in self.ENG}
        nwait = 0
        for i in order:
            eng, fn, reads, writes, dma = ops[i]
            need = {}
            for j in fdeps[i]:
                nm, v = ticket[j]
                if need.get(nm, 0) < v:
                    need[nm] = v
            if eng == 'barrier':
                for en in self.ENG:
                    E = self.e[en]
                    for nm, v in need.items():
                        if waited[en].get(nm, 0) < v:
                            E.wait_ge(getsem(nm), v)
                            waited[en][nm] = v
                            nwait += 1
                continue
            E = self.e[eng]
            for nm, v in need.items():
                if waited[eng].get(nm, 0) < v:
                    E.wait_ge(getsem(nm), v)
                    waited[eng][nm] = v
                    nwait += 1
            ins = fn(E)
            if ticket[i] is not None:
                nm, v = ticket[i]
                ins.then_inc(getsem(nm), 16 if dma is not None else 1)
        self.order = order
        self.stats = dict(ops=n, waits=nwait, sems=len(semc), maxval=max(semval.values()) if semval else 0,
                          model_us=model_ns / 1e3)


def _rope_tables():
    t = np.arange(L)
    rows = t // GRID_W
    cols = t % GRID_W
    inv = 10000.0 ** (-np.arange(16, dtype=np.float64) * 2.0 / 32.0)
    cosT = np.zeros((128, L), np.float64)
    sinT = np.zeros((128, L), np.float64)
    for p in range(128):
        d = p % 64
        i = d % 16
        isb = (d % 32) >= 16
        pos = rows if d < 32 else cols
        ang = pos.astype(np.float32).astype(np.float64) * np.float32(inv[i]).astype(np.float64)
        cosT[p] = np.cos(ang)
        sinT[p] = np.sin(ang) * (1.0 if isb else -1.0)
    return cosT.astype(np.float32), sinT.astype(np.float32)


def _swap_perm():
    perm = np.zeros(512, np.int64)
    for h in range(NH):
        for d in range(HD):
            dd = d + 16 if (d % 32) < 16 else d - 16
            perm[h * 64 + d] = h * 64 + dd
    return perm


def _mask_tables():
    m8 = np.zeros((128, 64), np.float32)
    mn8 = np.zeros((128, 64), np.float32)
    for p in range(128):
        kc = p % 64
        for qc in range(64):
            cs = min(max(qc - 8, 0), 48)
            ok = (kc >= cs) and (kc < cs + 16)
            m8[p, qc] = 8.0 if ok else 0.0
            mn8[p, qc] = 0.0 if ok else NEG8
    return m8, mn8


def _rpb_gather(rpb):
    kc = np.arange(64)[:, None]
    qc = np.arange(64)[None, :]
    dc = np.clip(kc - qc + 15, 0, 30)
    out = np.zeros((2, 128, NH, 14, 64), np.float32)
    for half in range(2):
        for o in range(14):
            g = rpb[:, :, o + half, :][:, :, dc]
            out[:, half * 64:(half + 1) * 64, :, o, :] = g.transpose(0, 2, 1, 3)
    return out


def _hy_consts(Ls):
    nt = Ls // 128
    f32 = np.float32
    t = np.linspace(0.0, 1.0, Ls, dtype=f32)[:, None]
    bands = np.linspace(1e-4, 15, 16, dtype=f32)[None, :]
    ang = (f32(2.0 * math.pi) * np.arange(Ls, dtype=f32)[:, None] / f32(Ls)) * bands
    z = np.concatenate([t, np.cos(ang), -np.sin(ang)], axis=-1).astype(f32)
    zT = np.ascontiguousarray(z.T)
    max_decay = math.log(1e-2) / 0.3
    min_decay = math.log(1e-2) / 1.5
    deltas = np.abs(np.linspace(min_decay, max_decay, 512, dtype=f32))
    decay = np.exp(-t * deltas[None, :]).astype(f32)
    decay_t = np.ascontiguousarray(decay.reshape(nt, 128, 512).transpose(1, 0, 2))
    tt = np.arange(Ls, dtype=np.int64)[:, None]
    ff = np.arange(Ls, dtype=np.int64)[None, :]
    nph = (tt * (2 * ff + 1)) % (4 * Ls)
    ang2 = 2.0 * math.pi * nph.astype(np.float64) / (4 * Ls)
    C = np.cos(ang2)
    Sn = np.sin(ang2)

    def tile_fwd(M):
        return np.ascontiguousarray(M.reshape(nt, 128, nt, 128).transpose(2, 1, 0, 3)).astype(ml_dtypes.bfloat16)

    def tile_inv(M):
        return np.ascontiguousarray(M.reshape(nt, 128, nt, 128).transpose(0, 3, 2, 1)).astype(ml_dtypes.bfloat16)
    return dict(zT=zT, decay=decay_t, cf=tile_fwd(C), sf=tile_fwd(Sn), ci=tile_inv(C), si=tile_inv(Sn))


_CONST_CACHE = {}


def _consts():
    if not _CONST_CACHE:
        cosT, sinT = _rope_tables()
        m8, mn8 = _mask_tables()
        _CONST_CACHE.update(cosT=cosT, sinT=sinT, m8=m8, mn8=mn8, hyL=_hy_consts(L), hyC=_hy_consts(LC))
    return _CONST_CACHE


class StopBuild(Exception):
    pass


class Prog:
    def __init__(self, NB=4, nlayers=2, debug=(), stop=None):
        self.stop = stop
        self.NB = NB
        self.nlayers = nlayers
        self.debug = set(debug)
        self.nc = bass.Bass("TRN2", target_bir_lowering=False)
        self.S = Sched(self.nc)
        self.uid = 0
        self.dbg_out = {}

    def din(self, name, shape, dt=F32):
        return self.nc.dram_tensor(name, list(shape), dt, kind="ExternalInput").ap()

    def dscr(self, name, shape, dt):
        return self.nc.dram_tensor(name, list(shape), dt, kind="Internal").ap()

    def dout(self, name, shape, dt=F32):
        return self.nc.dram_tensor(name, list(shape), dt, kind="ExternalOutput").ap()

    def sb(self, es, name, shape, dt):
        self.uid += 1
        return es.enter_context(self.nc.sbuf_tensor("%s_%d" % (name, self.uid), list(shape), dt))

    @staticmethod
    def _fs(ap):
        n = 1
        for d in ap.shape[1:]:
            n *= int(d)
        return n

    def mm(self, out, lhsT, rhs, start, stop, r, w):
        nn = self._fs(rhs)
        c = max(nn, 48) / 2.2 + 16.0
        if lhsT.dtype == F32:
            c *= 4.0
        self.S.op('pe', lambda e: e.matmul(out, lhsT=lhsT, rhs=rhs, start=start, stop=stop), r, w, cost=c)

    def tr(self, out, in_, ident, r, w):
        c = 64.0 * (4.0 if in_.dtype == F32 else 1.0)
        self.S.op('pe', lambda e: e.transpose(out=out, in_=in_, identity=ident), r, w, cost=c)

    def act(self, out, in_, func, r, w, bias=None, scale=None, accum=None):
        kw = {}
        if bias is not None:
            kw['bias'] = bias
        if scale is not None:
            kw['scale'] = scale
        if accum is not None:
            kw['accum_out'] = accum
        self.S.op('act', lambda e: e.activation(out=out, in_=in_, func=func, **kw), r, w, cost=220.0 + self._fs(in_) / 1.1)

    def _vcost(self, eng, ap):
        nn = self._fs(ap)
        if eng == 'pool':
            return 250.0 + nn / 0.45
        if eng == 'act':
            return 220.0 + nn / 1.1
        return 130.0 + nn / 0.9

    def tt(self, eng, out, in0, in1, op, r, w):
        self.S.op(eng, lambda e: e.tensor_tensor(out=out, in0=in0, in1=in1, op=op), r, w, cost=self._vcost(eng, out))

    def ts(self, eng, out, in0, s1, s2, op0, op1, r, w):
        c = self._vcost(eng, out)
        if op1 is None:
            self.S.op(eng, lambda e: e.tensor_scalar(out=out, in0=in0, scalar1=s1, scalar2=None, op0=op0), r, w, cost=c)
        else:
            self.S.op(eng, lambda e: e.tensor_scalar(out=out, in0=in0, scalar1=s1, scalar2=s2, op0=op0, op1=op1), r, w, cost=c)

    def stt(self, eng, out, in0, scalar, in1, op0, op1, r, w):
        self.S.op(eng, lambda e: e.scalar_tensor_tensor(out=out, in0=in0, scalar=scalar, in1=in1, op0=op0, op1=op1), r, w,
                  cost=self._vcost(eng, out))

    def cp(self, eng, out, in_, r, w):
        c = self._vcost(eng, out)
        if eng == 'act':
            self.S.op('act', lambda e: e.copy(out=out, in_=in_), r, w, cost=c)
        else:
            self.S.op(eng, lambda e: e.tensor_copy(out=out, in_=in_), r, w, cost=c)

    def memset(self, eng, ap, val, w, r=()):
        self.S.op(eng, lambda e: e.memset(ap, val), r, w, cost=self._vcost(eng, ap))

    def recip(self, out, in_, r, w):
        self.S.op('dve', lambda e: e.reciprocal(out=out, in_=in_), r, w, cost=self._vcost('dve', out))

    def dma(self, out, in_, r, w, key):
        nb = int(out.shape[0]) * self._fs(out) * (2 if out.dtype == BF16 else 4)
        self.S.op('sp', lambda e: e.dma_start(out=out, in_=in_), r, w, dma=key, nbytes=nb)

    def chk(self, name):
        if self.stop == name:
            self.S.frozen = True

    def dump(self, tag, sb_ap, rkeys, shape, dt):
        if tag not in self.debug:
            return
        o = self.dout("dbg_" + tag, shape, dt)
        self.dbg_out[tag] = "dbg_" + tag
        self.dma(o, sb_ap, list(rkeys), [('dbg', tag)], 'dbg_' + tag)

    def build(self):
        nc = self.nc
        NB = self.NB
        NL = self.nlayers
        I = {}
        I['x'] = self.din('x', [NB, L, D])
        I['ctx'] = self.din('ctx', [NB, LC, D])
        I['c5'] = self.din('c5', [NB + 1, D])
        I['ada_w'] = self.din('ada_w', [2, D, 6 * D])
        I['ada_b'] = self.din('ada_b', [2, 6 * D])
        I['norm1_g'] = self.din('norm1_g', [2, D])
        I['norm2_g'] = self.din('norm2_g', [2, D])
        I['w_in'] = self.din('w_in', [2, D, 8192])
        I['sgu_norm_g'] = self.din('sgu_norm_g', [2, 512])
        I['sgu_w'] = self.din('sgu_w', [2, 4, 128, 128])
        I['sgu_b'] = self.din('sgu_b', [2, 512])
        I['rpbg'] = self.din('rpbg', [2, 128, NH * 14 * 64])
        I['hy_conv'] = self.din('hy_conv', [2, 4, 1536])
        I['hy_p3'] = self.din('hy_p3', [2, 3, 64])
        I['hy_w1'] = self.din('hy_w1', [2, 33, 64])
        I['hy_w2'] = self.din('hy_w2', [2, 64, 64])
        I['hy_w3'] = self.din('hy_w3', [2, 64, 2048])
        I['hy_skip'] = self.din('hy_skip', [2, 2, 512])
        I['w_branch'] = self.din('w_branch', [2, 3, 512, D])
        I['w_out'] = self.din('w_out', [2, D, D])
        I['ffn_up'] = self.din('ffn_up', [2, D, 2 * DFF])
        I['ffn_conv'] = self.din('ffn_conv', [2, 4, DFF])
        I['ffn_down'] = self.din('ffn_down', [2, DFF, D])
        I['final_g'] = self.din('final_g', [1, D])
        I['cosT'] = self.din('cosT', [128, L])
        I['sinT'] = self.din('sinT', [128, L])
        I['m8'] = self.din('m8', [128, 64])
        I['mn8'] = self.din('mn8', [128, 64])
        for kind, Ls in (('L', L), ('C', LC)):
            nt = Ls // 128
            I['zT' + kind] = self.din('zT' + kind, [33, Ls])
            I['decay' + kind] = self.din('decay' + kind, [128, nt, 512])
            for nm in ('cf', 'sf', 'ci', 'si'):
                I[nm + kind] = self.din(nm + kind, [nt, 128, nt, 128], BF16)
        self.I = I
        self.out = self.dout('out', [NB, L, D])
        R = {}
        R['wA'] = self.dscr('s_wA', [2, 16, 128, 8, 512], BF16)
        R['wB'] = self.dscr('s_wB', [2, 3, 2, 128, 4, 512], BF16)
        R['wO'] = self.dscr('s_wO', [2, 2, 128, 8, 512], BF16)
        R['wU'] = self.dscr('s_wU', [2, 11, 128, 8, 512], BF16)
        R['wD'] = self.dscr('s_wD', [2, 2, 128, 22, 512], BF16)
        R['modrow'] = self.dscr('s_modrow', [2, NB + 1, 6 * D], F32)
        R['tbl'] = self.dscr('s_tbl', [2, 128, NH * 27 * 64], BF16)
        R['specL'] = self.dscr('s_specL', [2, 2, L // 128, 128, 1024], F32)
        R['specC'] = self.dscr('s_specC', [2, 2, LC // 128, 128, 1024], F32)
        R['xs'] = self.dscr('s_xs', [NB, L, D], F32)
        R['xcs'] = self.dscr('s_xcs', [NB, LC, D], F32)
        R['saL'] = self.dscr('s_saL', [128, 4, L], BF16)
        R['sattL'] = self.dscr('s_sattL', [128, 4, L], BF16)
        R['saC'] = self.dscr('s_saC', [128, 4, LC], BF16)
        R['sattC'] = self.dscr('s_sattC', [128, 4, LC], BF16)
        self.R = R

        with ExitStack() as es:
            self.psw = [es.enter_context(nc.psum_tensor("psw%d" % i, [128, 1024], F32)) for i in range(4)]
            self.ps = [self.psw[i // 2][:, (i % 2) * 512:(i % 2 + 1) * 512] for i in range(8)]
            P = {}
            P['identb'] = self.sb(es, 'identb', [128, 128], BF16)
            P['identf'] = self.sb(es, 'identf', [128, 128], F32)
            P['onesf'] = self.sb(es, 'onesf', [128, 128], F32)
            P['ss'] = self.sb(es, 'ss', [128, 8], F32)
            for l in range(2):
                P['g1T%d' % l] = self.sb(es, 'g1T', [128, 8], F32)
                P['g2T%d' % l] = self.sb(es, 'g2T', [128, 8], F32)
                P['gsnT%d' % l] = self.sb(es, 'gsnT', [128, 4], F32)
                P['wsT%d' % l] = self.sb(es, 'wsT', [128, 4, 128], BF16)
                P['bsb%d' % l] = self.sb(es, 'bsb', [128, 4, 128], F32)
                P['hcw%d' % l] = self.sb(es, 'hcw', [128, 12, 4], F32)
                P['fcw%d' % l] = self.sb(es, 'fcw', [128, 22, 4], F32)
                P['modT%d' % l] = self.sb(es, 'modT', [128, 48, NB + 1], F32)
            P['fgbc'] = self.sb(es, 'fgbc', [128, D], F32)
            P['KcT'] = self.sb(es, 'KcT', [128, 4, LC], BF16)
            P['Vc1'] = self.sb(es, 'Vc1', [128, 2, NH, 65], BF16)
            P['hT'] = self.sb(es, 'hT', [128, 8, L + 2], BF16)
            P['hTc'] = self.sb(es, 'hTc', [128, 8, LC + 2], BF16)
            self.P = P
            try:
                self.prologue(es)
                for b in range(NB):
                    for l in range(NL):
                        self.seq_pass(b, l, 'C')
                        self.seq_pass(b, l, 'L')
            except StopBuild:
                pass
            self.S.frozen = False
            self.S.barrier()
            self.S.emit()
        return nc

    def prologue(self, es0):
        P = self.P
        I = self.I
        R = self.R
        nc = self.nc
        NB = self.NB
        self.memset('pool', P['identf'][:], 0.0, ['identf'])
        self.S.op('pool', lambda e: e.affine_select(out=P['identf'][:], in_=P['identf'][:], pattern=[[-1, 128]],
                                                    compare_op=ALU.not_equal, fill=1.0, base=0, channel_multiplier=1),
                  ['identf'], ['identf'])
        self.cp('dve', P['identb'][:], P['identf'][:], ['identf'], ['identb'])
        self.memset('pool', P['onesf'][:], 1.0, ['onesf'])
        self.memset('pool', P['hT'][:], 0.0, [('hT', i) for i in range(L // 128)])
        self.memset('pool', P['hTc'][:], 0.0, [('hTc', i) for i in range(LC // 128)])
        self.memset('pool', P['Vc1'][:], 1.0, ['Vc1'])
        self.dma(P['fgbc'][:], I['final_g'][0:1, :].partition_broadcast(128), [], ['fgbc'], 'fgbc')
        self.S.phase = 'params'
        with ExitStack() as es:
            stage = self.sb(es, 'stage', [8, 6 * D], F32)
            s5 = self.sb(es, 's5', [8, D], F32)
            s5T = self.sb(es, 's5T', [128, 8, NB + 1], F32)
            adw = [self.sb(es, 'adw', [128, 8, 512], F32) for _ in range(2)]
            adb = self.sb(es, 'adb', [1, 6 * D], F32)
            sgw = self.sb(es, 'sgw', [128, 4, 128], F32)
            gst = self.sb(es, 'gst', [128, 8 * 14 * 64 // 2], F32)
            tblb = self.sb(es, 'tblb', [128, NH, 27, 64], BF16)
            m8 = self.sb(es, 'm8', [128, 64], F32)
            mn8 = self.sb(es, 'mn8', [128, 64], F32)
            nb1 = NB + 1
            self.dma(m8[:], I['m8'], [], ['m8'], 'm8')
            self.dma(mn8[:], I['mn8'], [], ['mn8'], 'mn8')

            def rows_to_cols(src_rows, Rr, N, dst, tag):
                nch = N // 128
                self.dma(stage[0:Rr, 0:N], src_rows, [], ['stage'], 'stage')
                pb = self.ps[0]
                for c in range(nch):
                    self.tr(pb[:, c * Rr:(c + 1) * Rr], stage[0:Rr, c * 128:(c + 1) * 128], P['identf'][0:Rr, 0:Rr],
                            ['stage', 'identf'], ['ps0'])
                self.cp('dve', dst, pb[:, 0:nch * Rr].rearrange("p (c r) -> p c r", r=Rr), ['ps0'], [tag])

            self.dma(s5[0:nb1, :], I['c5'], [], ['s5'], 's5')
            self.act(s5[0:nb1, :], s5[0:nb1, :], AF.Silu, ['s5'], ['s5'])
            pb = self.ps[1]
            for k in range(8):
                self.tr(pb[:, k * nb1:(k + 1) * nb1], s5[0:nb1, k * 128:(k + 1) * 128], P['identf'][0:nb1, 0:nb1],
                        ['s5', 'identf'], ['ps1'])
            self.cp('dve', s5T[:], pb[:, 0:8 * nb1].rearrange("p (c r) -> p c r", r=nb1), ['ps1'], ['s5T'])
            self.chk('p_s5')
            for l in range(self.nlayers):
                rows_to_cols(I['norm1_g'][l:l + 1, :], 1, D, P['g1T%d' % l][:].unsqueeze(2), 'g1T%d' % l)
                rows_to_cols(I['norm2_g'][l:l + 1, :], 1, D, P['g2T%d' % l][:].unsqueeze(2), 'g2T%d' % l)
                rows_to_cols(I['sgu_norm_g'][l:l + 1, :], 1, 512, P['gsnT%d' % l][:].unsqueeze(2), 'gsnT%d' % l)
                rows_to_cols(I['hy_conv'][l], 4, 1536, P['hcw%d' % l][:], 'hcw%d' % l)
                rows_to_cols(I['ffn_conv'][l], 4, DFF, P['fcw%d' % l][:], 'fcw%d' % l)
                self.chk('p_rows')
                self.dma(P['bsb%d' % l][:].rearrange("p g q -> p (g q)"), I['sgu_b'][l:l + 1, :].partition_broadcast(128), [],
                         ['bsb%d' % l], 'bsb%d' % l)
                self.dma(sgw[:], I['sgu_w'][l].rearrange("g q p -> q g p"), [], ['sgw'], 'sgw')
                pb2 = self.ps[2]
                for g in range(4):
                    self.tr(pb2[:, g * 128:(g + 1) * 128], sgw[:, g, :], P['identf'][:], ['sgw', 'identf'], ['ps2'])
                self.cp('dve', P['wsT%d' % l][:].rearrange("p g q -> p (g q)"), pb2[:, :], ['ps2'], ['wsT%d' % l])
                self.chk('p_sgw')
                self.dma(adb[:], I['ada_b'][l:l + 1, :], [], ['adb'], 'adb')
                for n in range(12):
                    a = adw[n % 2]
                    self.dma(a[:], I['ada_w'][l][:, n * 512:(n + 1) * 512].rearrange("(kc p) c -> p kc c", p=128), [],
                             [('adw', n % 2)], 'adw%d' % (n % 2))
                    pm = self.ps[3 + (n % 2)]
                    pk = 'ps%d' % (3 + (n % 2))
                    for k in range(8):
                        self.mm(pm[0:nb1, :], s5T[:, k, :], a[:, k, :], k == 0, False, ['s5T', ('adw', n % 2)], [pk])
                    self.mm(pm[0:nb1, :], P['onesf'][0:1, 0:nb1], adb[0:1, n * 512:(n + 1) * 512], False, True,
                            ['onesf', 'adb'], [pk])
                    self.cp('dve', stage[0:nb1, n * 512:(n + 1) * 512], pm[0:nb1, :], [pk], ['stage'])
                self.dma(R['modrow'][l], stage[0:nb1, :], ['stage'], [('modrow', l)], 'modrow_st')
                self.chk('p_mod')
                pb = self.ps[5]
                for c in range(48):
                    self.tr(pb[:, c * nb1:(c + 1) * nb1], stage[0:nb1, c * 128:(c + 1) * 128], P['identf'][0:nb1, 0:nb1],
                            ['stage', 'identf'], ['ps5'])
                self.cp('dve', P['modT%d' % l][:], pb[:, 0:48 * nb1].rearrange("p (c r) -> p c r", r=nb1), ['ps5'],
                        ['modT%d' % l])
                self.chk('p_modT')
                for hh in range(2):
                    self.dma(gst[:], I['rpbg'][l][:, hh * 3584:(hh + 1) * 3584], [], ['gst'], 'gst')
                    g3 = gst[:].rearrange("p (a q) -> p a q", q=64)
                    self.tt('dve', g3, g3, m8[:].unsqueeze(1).to_broadcast([128, 56, 64]), ALU.mult, ['gst', 'm8'], ['gst'])
                    for h4 in range(4):
                        self.tt('dve', tblb[:, hh * 4 + h4, 0:14, :], g3[:, h4 * 14:(h4 + 1) * 14, :],
                                mn8[:].unsqueeze(1).to_broadcast([128, 14, 64]), ALU.add, ['gst', 'mn8'], ['tblb'])
                self.cp('dve', tblb[:, :, 14, :], tblb[:, :, 2, :], ['tblb'], ['tblb'])
                self.cp('dve', tblb[:, :, 15, :], tblb[:, :, 10, :], ['tblb'], ['tblb'])
                self.memset('dve', tblb[0:64, :, 14, :], NEG8, ['tblb'], ['tblb'])
                self.memset('dve', tblb[64:128, :, 15, :], NEG8, ['tblb'], ['tblb'])
                self.memset('dve', tblb[:, :, 16, :], NEG8, ['tblb'], ['tblb'])
                for ci, src_slot in enumerate((3, 14, 5, 4, 7, 6, 9, 8, 16, 15)):
                    self.cp('dve', tblb[:, :, 17 + ci, :], tblb[:, :, src_slot, :], ['tblb'], ['tblb'])
                self.dma(R['tbl'][l], tblb[:].rearrange("p h o q -> p (h o q)"), ['tblb'], [('tbl', l)], 'tbl_st')
        self.S.barrier()
        self.chk('params')
        self.S.phase = 'filter'
        with ExitStack() as es:
            st = [self.sb(es, 'cst', [128, 2048], F32) for _ in range(2)]
            sbf = [self.sb(es, 'csb', [128, 2048], BF16) for _ in range(2)]
            cnt = [0]
            engs = ['pool', 'act', 'pool']

            def cast_units(src, dst, KC, N):
                for kc in range(KC):
                    for n0 in range(0, N, 2048):
                        yield (src, dst, kc, n0, min(2048, N - n0))

            def all_units():
                for l in range(self.nlayers if not getattr(self, 'skip_casts', False) else 0):
                    yield from cast_units(I['w_in'][l], R['wA'][l], 8, 8192)
                    for i in range(3):
                        yield from cast_units(I['w_branch'][l, i], R['wB'][l, i], 4, D)
                    yield from cast_units(I['w_out'][l], R['wO'][l], 8, D)
                    yield from cast_units(I['ffn_up'][l], R['wU'][l], 8, 2 * DFF)
                    yield from cast_units(I['ffn_down'][l], R['wD'][l], 22, D)
            gen = all_units()

            def pump(k):
                ph = self.S.phase
                self.S.phase = 'casts'
                for _ in range(k):
                    u = next(gen, None)
                    if u is None:
                        break
                    src, dst, kc, n0, w = u
                    i = cnt[0] % 2
                    eng = engs[cnt[0] % 3]
                    cnt[0] += 1
                    self.dma(st[i][:, 0:w], src[kc * 128:(kc + 1) * 128, n0:n0 + w], [], [('cst', i)], 'cst%d' % i)
                    self.cp(eng, sbf[i][:, 0:w], st[i][:, 0:w], [('cst', i)], [('csb', i)])
                    g0 = n0 // 512
                    ng = w // 512
                    self.dma(dst[g0:g0 + ng, :, kc, :].rearrange("g p c -> p g c"),
                             sbf[i][:, 0:w].rearrange("p (g c) -> p g c", g=ng), [('csb', i)], [('wscr',)], 'csb%d' % i)
                self.S.phase = ph
            self.pump = pump
            for l in range(self.nlayers):
                self.hy_filter(l, 'L')
                if l == 0:
                    self.hy_filter(l, 'C')
            pump(10000)
            self.pump = lambda k: None
        self.S.barrier()
        self.chk('filter')

    def range_reduce_sin(self, u, tmp, keys_u, keys_tmp):
        for _ in range(2):
            self.ts('dve', tmp, u, math.pi, 2 * math.pi, ALU.is_gt, ALU.mult, keys_u, keys_tmp)
            self.tt('dve', u, u, tmp, ALU.subtract, keys_u + keys_tmp, keys_u)
            self.ts('dve', tmp, u, -math.pi, 2 * math.pi, ALU.is_lt, ALU.mult, keys_u, keys_tmp)
            self.tt('dve', u, u, tmp, ALU.add, keys_u + keys_tmp, keys_u)
        self.act(u, u, AF.Sin, keys_u, keys_u)

    def hy_filter(self, l, kind):
        P = self.P
        I = self.I
        R = self.R
        Ls = L if kind == 'L' else LC
        nt = Ls // 128
        TW = min(512, Ls)
        spec = R['spec' + kind][l]
        with ExitStack() as es:
            zT = self.sb(es, 'zT', [33, Ls], F32)
            w1 = self.sb(es, 'w1', [33, 64], F32)
            w2 = self.sb(es, 'w2', [64, 64], F32)
            w3 = self.sb(es, 'w3', [64, 2048], F32)
            p3 = self.sb(es, 'p3', [64, 4], F32)
            fb = self.sb(es, 'fb', [64, 2], F32)
            st3 = self.sb(es, 'st3', [4, 64], F32)
            h1T = self.sb(es, 'h1T', [64, Ls], F32)
            h2T = self.sb(es, 'h2T', [64, Ls], F32)
            tmp = self.sb(es, 'tmp', [64, TW], F32)
            he = self.sb(es, 'he', [128, nt, 512], BF16)
            ho = self.sb(es, 'ho', [128, nt, 512], BF16)
            dec = [self.sb(es, 'dec', [128, 512], F32) for _ in range(2)]
            hd = [self.sb(es, 'hd', [128, 512], F32) for _ in range(2)]
            ab = [self.sb(es, 'ab', [128, 512], F32) for _ in range(2)]
            rn = self.sb(es, 'rn', [128, 512], F32)
            skb = self.sb(es, 'skb', [128, 512], F32)
            cfb = [self.sb(es, 'cfb', [128, nt, 128], BF16) for _ in range(2)]
            sfb = [self.sb(es, 'sfb', [128, nt, 128], BF16) for _ in range(2)]
            ko = [self.sb(es, 'ko', [128, 2, 512], F32) for _ in range(2)]
            self.dma(zT[:], I['zT' + kind], [], [('hsrc', 0)], 'zT')
            self.dma(w1[:], I['hy_w1'][l], [], ['w1'], 'w1')
            self.dma(w2[:], I['hy_w2'][l], [], ['w2'], 'w2')
            self.dma(w3[:], I['hy_w3'][l], [], ['w3'], 'w3')
            self.dma(st3[0:3, :], I['hy_p3'][l], [], ['st3'], 'st3')
            pb = self.ps[0]
            self.tr(pb[0:64, 0:3], st3[0:3, :], P['identf'][0:3, 0:3], ['st3', 'identf'], ['ps0'])
            self.cp('dve', p3[:, 0:3], pb[0:64, 0:3], ['ps0'], ['p3'])
            self.tt('dve', fb[:, 0:1], p3[:, 0:1], p3[:, 2:3], ALU.mult, ['p3'], ['fb'])
            self.tt('dve', fb[:, 1:2], p3[:, 1:2], p3[:, 2:3], ALU.mult, ['p3', 'fb'], ['fb'])
            for li, (wsrc, src, dst, kk) in enumerate(((w1, zT, h1T, 33), (w2, h1T, h2T, 64))):
                for t0 in range(0, Ls, TW):
                    pi = 1 + ((t0 // TW) % 2)
                    pk = 'ps%d' % pi
                    self.mm(self.ps[pi][0:64, 0:TW], wsrc[0:kk, :], src[0:kk, t0:t0 + TW], True, True,
                            ['w%d' % (li + 1), ('hsrc', li)], [pk])
                    self.ts('dve', dst[:, t0:t0 + TW], self.ps[pi][0:64, 0:TW], p3[:, 2:3], fb[:, li:li + 1], ALU.mult, ALU.add,
                            [pk, 'p3', 'fb'], [('hsrc', li + 1)])
                    self.range_reduce_sin(dst[:, t0:t0 + TW], tmp[:, 0:TW], [('hsrc', li + 1)], ['rrtmp'])
            for o in range(2):
                pn = self.ps[3]
                first = True
                for i in range(nt):
                    self.pump(2)
                    self.dma(dec[i % 2][:], I['decay' + kind][:, i, :], [], [('dec', i % 2)], 'dec%d' % (i % 2))
                    for dr in range(2):
                        n = o * 2 + dr
                        pi = 4 + dr
                        pk = 'ps%d' % pi
                        self.mm(self.ps[pi][:, :], h2T[:, i * 128:(i + 1) * 128], w3[:, n * 512:(n + 1) * 512], True, True,
                                [('hsrc', 2), 'w3'], [pk])
                        self.tt('dve', hd[dr][:], self.ps[pi][:, :], dec[i % 2][:], ALU.mult, [pk, ('dec', i % 2)], [('hd', dr)])
                        if dr == 1 and i == 0:
                            self.memset('dve', hd[dr][0:1, :], 0.0, [('hd', dr)], [('hd', dr)])
                        self.act(ab[dr][:], hd[dr][:], AF.Abs, [('hd', dr)], [('ab', dr)])
                        last = (i == nt - 1 and dr == 1)
                        self.mm(pn[:, :], P['onesf'][:], ab[dr][:], first, last, ['onesf', ('ab', dr)], ['ps3'])
                        first = False
                    self.tt('dve', he[:, i, :], hd[0][:], hd[1][:], ALU.add, [('hd', 0), ('hd', 1)], [('he', i)])
                    self.tt('pool', ho[:, i, :], hd[0][:], hd[1][:], ALU.subtract, [('hd', 0), ('hd', 1)], [('ho', i)])
                self.recip(rn[:], pn[:, :], ['ps3'], ['rn'])
                self.ts('dve', rn[:], rn[:], 1.0 / Ls, None, ALU.mult, None, ['rn'], ['rn'])
                self.dma(skb[:], I['hy_skip'][l, o:o + 1, :].partition_broadcast(128), [], ['skb'], 'skb')
                self.ts('dve', skb[:], skb[:], 1.0 / Ls, None, ALU.mult, None, ['skb'], ['skb'])
                hek = [('he', i) for i in range(nt)]
                hok = [('ho', i) for i in range(nt)]
                for m in range(nt):
                    self.pump(2)
                    s = m % 2
                    self.dma(cfb[s][:], I['cf' + kind][m], [], [('cfb', s)], 'cfb%d' % s)
                    self.dma(sfb[s][:], I['sf' + kind][m], [], [('sfb', s)], 'sfb%d' % s)
                    pr, ps_ = self.ps[6], self.ps[7]
                    for tt_ in range(nt):
                        self.mm(pr[:, :], cfb[s][:, tt_, :], he[:, tt_, :], tt_ == 0, tt_ == nt - 1, [('cfb', s)] + hek, ['ps6'])
                    for tt_ in range(nt):
                        self.mm(ps_[:, :], sfb[s][:, tt_, :], ho[:, tt_, :], tt_ == 0, tt_ == nt - 1, [('sfb', s)] + hok, ['ps7'])
                    self.tt('dve', ko[s][:, 0, :], pr[:, :], rn[:], ALU.mult, ['ps6', 'rn'], [('ko', s)])
                    self.tt('dve', ko[s][:, 0, :], ko[s][:, 0, :], skb[:], ALU.add, [('ko', s), 'skb'], [('ko', s)])
                    self.tt('dve', ko[s][:, 1, :], ps_[:, :], rn[:], ALU.mult, ['ps7', 'rn', ('ko', s)], [('ko', s)])
                    self.dma(spec[o, m], ko[s][:].rearrange("p a c -> p (a c)"), [('ko', s)], [('spec', kind, l)], 'ko%d' % s)
        self.S.barrier()

    def norm_tile(self, es_bufs, x_sb, xk, A, B, Akeys, hT, hkey, i):
        P = self.P
        junks, xns, _unused, pbi = es_bufs
        par = i % 2
        junk, xn = junks[par], xns[par]
        ss = P['ss'][:, 4 * par:4 * par + 4]
        sk = ('ss', par)
        pb = self.ps[pbi]
        pk = 'ps%d' % pbi
        self.memset('pool', ss[:, 0:1], 0.0, [sk])
        self.act(junk[:], x_sb, AF.Square, [xk, sk], [('njunk', par), sk], accum=ss[:, 0:1])
        self.ts('dve', ss[:, 1:2], ss[:, 0:1], 1.0 / D, EPS, ALU.mult, ALU.add, [sk], [sk])
        self.act(ss[:, 2:3], ss[:, 1:2], AF.Sqrt, [sk], [sk])
        self.recip(ss[:, 3:4], ss[:, 2:3], [sk], [sk])
        self.act(xn[:], x_sb, AF.Copy, [xk, sk], [('nxn', par)], scale=ss[:, 3:4])
        pbb = pb[:].bitcast(BF16)
        for c in range(8):
            self.tr(pbb[:, c * 128:(c + 1) * 128], xn[:, c * 128:(c + 1) * 128], P['identb'][:], [('nxn', par), 'identb'], [pk])
        for c in range(8):
            self.ts('dve', hT[:, c, 1 + 128 * i:1 + 128 * (i + 1)], pbb[:, c * 128:(c + 1) * 128], A[:, c:c + 1], B[:, c:c + 1],
                    ALU.mult, ALU.add, [pk] + Akeys, [(hkey, i, c)])

    def seq_pass(self, b, l, kind):
        P = self.P
        I = self.I
        R = self.R
        NB = self.NB
        isL = (kind == 'L')
        Ls = L if isL else LC
        nt = Ls // 128
        TW = min(512, Ls)
        nT = Ls // TW
        spt = TW // 128
        last_layer = (l == self.nlayers - 1)
        full = isL or (not last_layer)
        bcol = b if isL else NB
        hT = P['hT'] if isL else P['hTc']
        hkey = 'hT' if isL else 'hTc'
        modT = P['modT%d' % l]
        wA = R['wA'][l]
        if isL:
            xsrc = I['x'][b] if l == 0 else R['xs'][b]
            xsk = ('xin',) if l == 0 else ('xs', b)
            xdst = R['xs'][b]
            xdk = ('xs', b)
        else:
            xsrc = I['ctx'][b] if l == 0 else R['xcs'][b]
            xsk = ('xin',) if l == 0 else ('xcs', b)
            xdst = R['xcs'][b]
            xdk = ('xcs', b)

        def hkeys(T, halo=False, k=0):
            ks = [(hkey, T * spt + s, k) for s in range(spt)]
            if halo:
                if T * spt - 1 >= 0:
                    ks.append((hkey, T * spt - 1, k))
                if (T + 1) * spt < nt:
                    ks.append((hkey, (T + 1) * spt, k))
            return ks

        def hcols(T):
            return slice(1 + T * TW, 1 + (T + 1) * TW)

        def halo_ap(kc, T):
            base = hT[:, kc, T * TW:T * TW + 1]
            return bass.AP(base.tensor, base.offset, [list(base.ap[0]), [TW + 1, 2]])

        with ExitStack() as esP:
            AB = self.sb(esP, 'AB', [128, 4, 8], F32)
            PF = [self.sb(esP, 'PF', [128, 8, 512], BF16) for _ in range(2)]
            self.PF = PF

            def pf_fill(i, src):
                self.dma(PF[i][:], src, [('wscr',)], [('PF', i)], 'PF%d' % i)
            self.pf_fill = pf_fill
            A1 = AB[:, 0, :]
            A2 = AB[:, 1, :]
            B1 = modT[:, 0:8, bcol]
            B2 = modT[:, 24:32, bcol]
            self.stt('dve', A1, modT[:, 8:16, bcol], 1.0, P['g1T%d' % l][:], ALU.add, ALU.mult, ['modT%d' % l, 'g1T%d' % l], ['AB'])
            self.stt('dve', A2, modT[:, 32:40, bcol], 1.0, P['g2T%d' % l][:], ALU.add, ALU.mult, ['modT%d' % l, 'g2T%d' % l, 'AB'],
                     ['AB'])
            ABk = ['AB', 'modT%d' % l]
            self.S.phase = kind + '_N1'
            with ExitStack() as es:
                xt = [self.sb(es, 'xt', [128, D], F32) for _ in range(3)]
                junk = [self.sb(es, 'junk', [128, D], BF16) for _ in range(2)]
                xn = [self.sb(es, 'xn', [128, D], BF16) for _ in range(2)]
                for i in range(nt):
                    s = i % 3
                    self.dma(xt[s][:], xsrc[i * 128:(i + 1) * 128, :], [xsk + (i,)], [('xt', s)], 'xt%d' % s)
                    self.norm_tile((junk, xn, None, i % 2), xt[s][:], ('xt', s), A1, B1, ABk, hT, hkey, i)
                if isL and b == 0 and l == 0:
                    self.dump('hT', hT[:], [(hkey, i, c) for i in range(nt) for c in range(8)], [128, 8, L + 2], BF16)
                if isL:
                    pf_fill(0, wA[2])
                    pf_fill(1, wA[14])
                elif full:
                    pf_fill(0, wA[2])
                    pf_fill(1, wA[3])
                else:
                    pf_fill(0, wA[3])
                    pf_fill(1, wA[4])
                if full:
                    self.S.phase = kind + '_sgu'
                    self.phase_sgu(b, l, kind, hT, hkeys, hcols)
            self.S.barrier()
            if isL:
                self.chk('sgu')
            self.S.phase = kind + '_attn'
            self.phase_attn(b, l, kind, hT, hkeys, hcols, full)
            self.S.barrier()
            if isL:
                self.chk('attn')
            if not full:
                return
            with ExitStack() as esH:
                hyT = self.sb(esH, 'hyT', [128, 4, Ls], BF16)
                self.S.phase = kind + '_hyena'
                self.phase_hyena(b, l, kind, hT, hkeys, hcols, halo_ap, hyT)
                self.S.barrier()
                if isL:
                    self.chk('hyena')
                if isL and b == 0 and l == 0:
                    self.dump('hyT', hyT[:], [], [128, 4, L], BF16)
                mT = self.sb(esH, 'mT', [128, 8, Ls], BF16)
                self.S.phase = kind + '_merge1'
                self.phase_merge1(b, l, kind, hT, hkeys, hcols, hyT, mT)
                self.S.barrier()
                if isL:
                    self.chk('merge1')
                if isL and b == 0 and l == 0:
                    self.dump('mT', mT[:], [], [128, 8, L], BF16)
                self.S.phase = kind + '_merge2'
                self.phase_merge2(b, l, kind, hT, hkey, mT, xsrc, xsk, xdst, xdk, A2, B2, ABk, bcol)
                self.S.barrier()
                if isL:
                    self.chk('merge2')
            if isL and b == 0 and l == 0:
                self.dump('h2T', hT[:], [], [128, 8, L + 2], BF16)
            self.S.phase = kind + '_ffn'
            self.phase_ffn(b, l, kind, hT, hkeys, hcols, halo_ap, xdst, xdk, bcol, last_layer and isL)
            self.S.barrier()

    def phase_sgu(self, b, l, kind, hT, hkeys, hcols):
        P = self.P
        R = self.R
        isL = kind == 'L'
        Ls = L if isL else LC
        nt = Ls // 128
        TW = min(512, Ls)
        nT = Ls // TW
        spt = TW // 128
        wA = R['wA'][l]
        with ExitStack() as es:
            aT = self.sb(es, 'aT', [128, 4, Ls], BF16)
            wu = self.sb(es, 'wu', [128, 8, 512], BF16)
            wv = self.sb(es, 'wv', [128, 8, 512], BF16)
            vg = [self.sb(es, 'vg', [128, 512], F32) for _ in range(2)]
            vjs = [self.sb(es, 'vj', [128, 512], BF16) for _ in range(2)]
            vs = [self.sb(es, 'vs', [128, 512], BF16) for _ in range(2)]
            t1 = [self.sb(es, 't1', [128, 4, 128], F32) for _ in range(2)]
            svs = self.sb(es, 'sv', [128, 8], F32)
            self.dma(wu[:], wA[0], [('wscr',)], ['wu'], 'wu')
            self.dma(wv[:], wA[1], [('wscr',)], ['wv'], 'wv')
            cnt = 0
            for T in range(nT):
                for fc in range(4):
                    pi = cnt % 3
                    cnt += 1
                    pk = 'ps%d' % pi
                    for k in range(8):
                        self.mm(self.ps[pi][:, 0:TW], wu[:, k, fc * 128:(fc + 1) * 128], hT[:, k, hcols(T)], k == 0, k == 7,
                                ['wu'] + hkeys(T, k=k), [pk])
                    self.act(aT[:, fc, T * TW:(T + 1) * TW], self.ps[pi][:, 0:TW], AF.Gelu, [pk], [('aT', T, fc)])
                for s in range(spt):
                    i = T * spt + s
                    pi = 3 + (i % 2)
                    pk = 'ps%d' % pi
                    q = i % 2
                    for k in range(8):
                        self.mm(self.ps[pi][:, :], hT[:, k, 1 + i * 128:1 + (i + 1) * 128], wv[:, k, :], k == 0, k == 7,
                                ['wv', (hkeys(T, k=k)[s])], [pk])
                    self.act(vg[q][:], self.ps[pi][:, :], AF.Gelu, [pk], [('vg', q)])
                    sv = svs[:, 4 * q:4 * q + 4]
                    svk = ('sv', q)
                    vj = vjs[q]
                    self.memset('pool', sv[:, 0:1], 0.0, [svk])
                    self.act(vj[:], vg[q][:], AF.Square, [('vg', q), svk], [('vj', q), svk], accum=sv[:, 0:1])
                    self.ts('dve', sv[:, 1:2], sv[:, 0:1], 1.0 / 512, EPS, ALU.mult, ALU.add, [svk], [svk])
                    self.act(sv[:, 2:3], sv[:, 1:2], AF.Sqrt, [svk], [svk])
                    self.recip(sv[:, 3:4], sv[:, 2:3], [svk], [svk])
                    self.ts('dve', vs[q][:], vg[q][:], sv[:, 3:4], None, ALU.mult, None, [('vg', q), svk], [('vs', q)])
                    pj = 5 + (i % 2)
                    pjk = 'ps%d' % pj
                    for g in range(4):
                        self.mm(self.ps[pj][:, g * 128:(g + 1) * 128], vs[q][:, g * 128:(g + 1) * 128], P['wsT%d' % l][:, g, :],
                                True, True, [('vs', q), 'wsT%d' % l], [pjk])
                    self.tt('dve', t1[q][:], self.ps[pj][:, :].rearrange("p (g q) -> p g q", g=4),
                            P['gsnT%d' % l][:].unsqueeze(2).to_broadcast([128, 4, 128]), ALU.mult, [pjk, 'gsnT%d' % l], [('t1', q)])
                    self.tt('pool', t1[q][:], t1[q][:], P['bsb%d' % l][:], ALU.add, [('t1', q), 'bsb%d' % l], [('t1', q)])
                    akeys = [('aT', T, fc) for fc in range(4)]
                    self.tt('dve', aT[:, :, i * 128:(i + 1) * 128], aT[:, :, i * 128:(i + 1) * 128], t1[q][:], ALU.mult,
                            [('t1', q)] + akeys, [('aT2', i)])
            allk = [('aT', T, fc) for T in range(nT) for fc in range(4)] + [('aT2', i) for i in range(nt)]
            self.dma(R['sa' + kind], aT[:], allk, [('sa', kind)], 'sa_st')
            if isL and b == 0 and l == 0:
                self.dump('aT', aT[:], allk, [128, 4, L], BF16)

    def phase_attn(self, b, l, kind, hT, hkeys, hcols, full):
        P = self.P
        I = self.I
        R = self.R
        isL = kind == 'L'
        Ls = L if isL else LC
        nt = Ls // 128
        TW = min(512, Ls)
        nT = Ls // TW
        spt = TW // 128
        wA = R['wA'][l]
        with ExitStack() as es:
            es1 = es.enter_context(ExitStack())
            if isL:
                QT = self.sb(es, 'QTz', [128, NH, Ls], BF16)
                KT = self.sb(es, 'KT', [128, 4, Ls], BF16)
                V1 = self.sb(es, 'V1', [128, nt, NH, 65], BF16)
                tbl = self.sb(es, 'tbl', [128, NH, 27, 64], BF16)
                cosT = self.sb(es1, 'cosT', [128, L], F32)
                sinT = self.sb(es1, 'sinT', [128, L], F32)
                self.dma(cosT[:], I['cosT'], [], ['cosT'], 'cosT')
                self.dma(sinT[:], I['sinT'], [], ['sinT'], 'sinT')
                self.dma(tbl[:].rearrange("p h o q -> p (h o q)"), R['tbl'][l], [('tbl', l)], ['tbl'], 'tbl')
                self.memset('pool', V1[:], 1.0, ['V1'])
                self.memset('dve', QT[:], 0.0, ['QTz0'])
            else:
                KT = P['KcT']
                V1 = P['Vc1']
                if full:
                    QT = self.sb(es, 'QTc', [128, 4, Ls], BF16)
            wq = [self.sb(es1, 'wq', [128, 8, 512], BF16) for _ in range(2)]
            r1 = [self.sb(es1, 'r1', [128, 512], F32) for _ in range(2)]
            r2 = [self.sb(es1, 'r2', [128, 512], F32) for _ in range(2)]
            wcnt = [0]

            pfn = [0]

            def wload(g):
                if pfn[0] < 2:
                    i_ = pfn[0]
                    pfn[0] += 1
                    return self.PF[i_], ('PF', i_)
                s = wcnt[0] % 2
                wcnt[0] += 1
                self.dma(wq[s][:], wA[g], [('wscr',)], [('wq', s)], 'wq%d' % s)
                return wq[s], ('wq', s)

            pcnt = [0]

            def proj_fm(wb, wk, T, fc, pi):
                pk = 'ps%d' % pi
                for k in range(8):
                    self.mm(self.ps[pi][:, 0:TW], wb[:, k, fc * 128:(fc + 1) * 128], hT[:, k, hcols(T)], k == 0, k == 7,
                            [wk] + hkeys(T, k=k), [pk])
                return self.ps[pi][:, 0:TW], pk

            kname = 'KT' if isL else 'KcT'
            vname = 'V1' if isL else 'Vc1'
            for name, g, gsw, dstT in (('q', 2, 14, 'QT'), ('k', 3, 15, kname)):
                if name == 'q' and not full:
                    continue
                dst = QT if name == 'q' else KT
                wb, wk = wload(g)
                if isL:
                    wb2, wk2 = wload(gsw)
                for T in range(nT):
                    for fc in range(4):
                        pa, pak = proj_fm(wb, wk, T, fc, (pcnt[0] % 2) * 2)
                        if isL:
                            pb_, pbk = proj_fm(wb2, wk2, T, fc, (pcnt[0] % 2) * 2 + 1)
                            q = pcnt[0] % 2
                            self.tt('dve', r1[q][:], pa, cosT[:, T * TW:(T + 1) * TW], ALU.mult, [pak, 'cosT'], [('r1', q)])
                            self.tt('dve', r2[q][:], pb_, sinT[:, T * TW:(T + 1) * TW], ALU.mult, [pbk, 'sinT'], [('r2', q)])
                            if name == 'q':
                                for hp in range(2):
                                    self.tt('pool', dst[hp * 64:(hp + 1) * 64, 2 * fc + hp, T * TW:(T + 1) * TW],
                                            r1[q][hp * 64:(hp + 1) * 64, :], r2[q][hp * 64:(hp + 1) * 64, :], ALU.add,
                                            [('r1', q), ('r2', q), 'QTz0'], [(dstT, T, fc, hp)])
                            else:
                                self.tt('pool', dst[:, fc, T * TW:(T + 1) * TW], r1[q][:], r2[q][:], ALU.add, [('r1', q), ('r2', q)],
                                        [(dstT, T, fc)])
                        else:
                            self.cp('act', dst[:, fc, T * TW:(T + 1) * TW], pa, [pak], [(dstT, T, fc)])
                        pcnt[0] += 1
            wb, wk = wload(4)
            for i in range(nt):
                pi = 4 + (i % 2)
                pk = 'ps%d' % pi
                for k in range(8):
                    self.mm(self.ps[pi][:, :], hT[:, k, 1 + i * 128:1 + (i + 1) * 128], wb[:, k, :], k == 0, k == 7,
                            [wk, hkeys(i // spt, k=k)[i % spt]], [pk])
                self.cp('act', V1[:, i, :, 0:64], self.ps[pi][:, :].rearrange("p (h d) -> p h d", h=NH), [pk, vname], [(vname, i)])
            if not full:
                return
            if isL:
                QTk = [('QT', T, fc, hp) for T in range(nT) for fc in range(4) for hp in range(2)]
            else:
                QTk = [('QT', T, fc) for T in range(nT) for fc in range(4)]
            KTk = [(kname, T, fc) for T in range(nT) for fc in range(4)]
            if isL and b == 0 and l == 0:
                self.dump('QT', QT[:], QTk, [128, NH, L], BF16)
                self.dump('KT', KT[:], KTk, [128, 4, L], BF16)
            self.S.barrier()
            es1.close()
            self.pf_fill(0, wA[5])
            self.pf_fill(1, wA[6])
            self.S.phase = kind + '_attn2'
            attnT = self.sb(es, 'attnT', [128, 4, Ls], BF16)
            Pb = [self.sb(es, 'Pb', [128, 896], BF16) for _ in range(3)]
            atok = [self.sb(es, 'atok', [128, 512], BF16) for _ in range(2)]
            rc = [self.sb(es, 'rc', [128, 8], F32) for _ in range(2)]
            KcT = P['KcT']
            Vc1 = P['Vc1']
            nqb = Ls // 128
            sc = 0
            pc_ = 0
            for pr in range(nqb):
                r = 2 * pr
                tiles = []
                interior = False
                if isL:
                    if 4 <= r <= 26:
                        interior = True
                        for t in range(5):
                            tiles.append(('loc', (r - 4) // 2 + t, None))
                    else:
                        j0 = 0 if r < 4 else 12
                        for t in range(4):
                            j = j0 + t
                            tiles.append(('loc', j, (2 * j - r + 7, 2 * j - (r + 1) + 7)))
                tiles.append(('ctx', 0, None))
                tiles.append(('ctx', 1, None))
                ntl = len(tiles)
                rp = pr % 2
                OA, OB = self.ps[4 + 2 * rp], self.ps[5 + 2 * rp]
                OAk, OBk = 'ps%d' % (4 + 2 * rp), 'ps%d' % (5 + 2 * rp)
                for h in range(NH):
                    c = h // 2
                    p0 = (h % 2) * 64
                    st_ = sc % 2
                    sc += 1
                    SB_ = (self.ps[2 * st_], self.ps[2 * st_ + 1])
                    SK_ = ('ps%d' % (2 * st_), 'ps%d' % (2 * st_ + 1))
                    Tq = (r * 64) // TW
                    if isL:
                        qap = QT[:, h, r * 64:r * 64 + 128]
                        qk = [('QT', Tq, c, h % 2)]
                        ksl = slice(0, 128)
                    else:
                        qap = QT[p0:p0 + 64, c, r * 64:r * 64 + 128]
                        qk = [('QT', Tq, c)]
                        ksl = slice(p0, p0 + 64)
                    if interior:
                        self.mm(SB_[0][:, 0:512], P['identb'][:], tbl[:, h, 17:25, :].rearrange("p a q -> p (a q)"), True, False,
                                ['identb', 'tbl'], [SK_[0]])
                        self.mm(SB_[1][:, 0:128], P['identb'][:], tbl[:, h, 25:27, :].rearrange("p a q -> p (a q)"), True, False,
                                ['identb', 'tbl'], [SK_[1]])
                    for ti, (ty, j, bspec) in enumerate(tiles):
                        Sb, Sk = SB_[ti // 4], SK_[ti // 4]
                        co = (ti % 4) * 128
                        if ty == 'loc':
                            kap = KT[ksl, c, j * 128:(j + 1) * 128]
                            kk = [('KT', (j * 128) // TW, c)]
                            if interior:
                                self.mm(Sb[:, co:co + 128], kap, qap, False, (ti == 3 or ti == 4), kk + qk, [Sk])
                            else:
                                self.mm(Sb[:, co:co + 128], kap, qap, True, False, kk + qk, [Sk])
                                for a_ in range(2):
                                    self.mm(Sb[:, co + a_ * 64:co + (a_ + 1) * 64], P['identb'][:], tbl[:, h, bspec[a_], :], False, a_ == 1,
                                            ['identb', 'tbl'], [Sk])
                        else:
                            kap = KcT[ksl, c, j * 128:(j + 1) * 128]
                            kk = [('KcT', 0, c)]
                            self.mm(Sb[:, co:co + 128], kap, qap, True, True, kk + qk, [Sk])
                    pq = pc_ % 3
                    pc_ += 1
                    na = min(ntl, 4) * 128
                    pkeys = [('Pb', pq, 0)]
                    if ntl > 4:
                        self.act(Pb[pq][:, 0:ntl * 128], self.psw[st_][:, 0:ntl * 128], AF.Exp, [SK_[0], SK_[1]],
                                 [('Pb', pq, 0), ('Pb', pq, 1)], scale=0.125)
                        pkeys.append(('Pb', pq, 1))
                    else:
                        self.act(Pb[pq][:, 0:na], SB_[0][:, 0:na], AF.Exp, [SK_[0]], [('Pb', pq, 0)], scale=0.125)
                    O = OA if h < 4 else OB
                    Ok = OAk if h < 4 else OBk
                    hh = h % 4
                    for ti, (ty, j, bspec) in enumerate(tiles):
                        if ty == 'loc':
                            vap = V1[:, j, h, :]
                            vk = [('V1', j)]
                        else:
                            vap = Vc1[:, j, h, :]
                            vk = [('Vc1', j)]
                        self.mm(O[:, hh * 65:(hh + 1) * 65], Pb[pq][:, ti * 128:(ti + 1) * 128], vap, ti == 0, ti == ntl - 1,
                                [pkeys[ti // 4]] + vk, [Ok])
                q = pr % 2
                for half, (O, Ok) in enumerate(((OA, OAk), (OB, OBk))):
                    o3 = O[:, 0:260].rearrange("p (h e) -> p h e", h=4)
                    self.recip(rc[q][:, half * 4:(half + 1) * 4].unsqueeze(2), o3[:, :, 64:65], [Ok], [('rc', q, half)])
                    self.tt('dve', atok[q][:, half * 256:(half + 1) * 256].rearrange("p (h d) -> p h d", h=4), o3[:, :, 0:64],
                            rc[q][:, half * 4:(half + 1) * 4].unsqueeze(2).to_broadcast([128, 4, 64]), ALU.mult,
                            [Ok, ('rc', q, half)], [('atok', q, half)])
                st_ = sc % 2
                sc += 1
                Tb = self.ps[2 * st_][:].bitcast(BF16)
                Tk = 'ps%d' % (2 * st_)
                for c in range(4):
                    self.tr(Tb[:, c * 128:(c + 1) * 128], atok[q][:, c * 128:(c + 1) * 128], P['identb'][:],
                            [('atok', q, 0), ('atok', q, 1), 'identb'], [Tk])
                self.cp('act', attnT[:, :, pr * 128:(pr + 1) * 128], Tb[:, 0:512].rearrange("p (c t) -> p c t", c=4), [Tk],
                        [('attnT', pr)])
            allk = [('attnT', r) for r in range(nqb)]
            self.dma(R['satt' + kind], attnT[:], allk, [('satt', kind)], 'satt_st')
            if isL and b == 0 and l == 0:
                self.dump('attnT', attnT[:], allk, [128, 4, L], BF16)

    def phase_hyena(self, b, l, kind, hT, hkeys, hcols, halo_ap, hyT):
        P = self.P
        I = self.I
        R = self.R
        isL = kind == 'L'
        Ls = L if isL else LC
        nt = Ls // 128
        TW = min(512, Ls)
        nT = Ls // TW
        spt = TW // 128
        wA = R['wA'][l]
        hcw = P['hcw%d' % l]
        spec = R['spec' + kind][l]
        with ExitStack() as es:
            ztok = self.sb(es, 'ztok', [128, nt, 512], BF16)
            x1tok = self.sb(es, 'x1tok', [128, nt, 512], BF16)
            Yr = self.sb(es, 'Yr', [128, nt, 512], BF16)
            Ys = self.sb(es, 'Ys', [128, nt, 512], BF16)
            cb = [self.sb(es, 'cb', [128, nt, 128], BF16) for _ in range(2)]
            sbf_ = [self.sb(es, 'sbf', [128, nt, 128], BF16) for _ in range(2)]
            kk_ = [self.sb(es, 'kk', [128, 2, 512], F32) for _ in range(2)]
            with ExitStack() as es2:
                wh = [self.sb(es2, 'wh', [128, 8, 512], BF16) for _ in range(2)]
                pbuf = [self.sb(es2, 'pbuf', [128, 516], F32) for _ in range(2)]
                cacc = [self.sb(es2, 'cacc', [128, 512], F32) for _ in range(2)]
                cfm = [self.sb(es2, 'cfm', [128, 512], BF16) for _ in range(2)]
                cnt = 0
                for gi, g in enumerate((5, 6, 7)):
                    s = gi % 2
                    if gi < 2:
                        whb, whk = self.PF[gi], ('PF', gi)
                    else:
                        self.dma(wh[0][:], wA[g], [('wscr',)], [('wh', 0)], 'wh0')
                        whb, whk = wh[0], ('wh', 0)
                    for T in range(nT):
                        for fc in range(4):
                            ch = gi * 4 + fc
                            q = cnt % 2
                            cnt += 1
                            pi = q * 2
                            pk, pk2 = 'ps%d' % pi, 'ps%d' % (pi + 1)
                            for k in range(8):
                                self.mm(self.ps[pi][:, 0:TW], whb[:, k, fc * 128:(fc + 1) * 128], hT[:, k, hcols(T)], k == 0, k == 7,
                                        [whk] + hkeys(T, k=k), [pk])
                            for k in range(8):
                                self.mm(self.ps[pi + 1][:, 0:2], whb[:, k, fc * 128:(fc + 1) * 128], halo_ap(k, T), k == 0, k == 7,
                                        [whk] + hkeys(T, True, k), [pk2])
                            self.cp('act', pbuf[q][:, 1:1 + TW], self.ps[pi][:, 0:TW], [pk], [('pbuf', q)])
                            hdst = bass.AP(pbuf[q][:, 0:1].tensor, pbuf[q][:, 0:1].offset,
                                           [list(pbuf[q][:, 0:1].ap[0]), [TW + 1, 2]])
                            self.cp('act', hdst, self.ps[pi + 1][:, 0:2], [pk2, ('pbuf', q)], [('pbuf', q)])
                            self.ts('dve', cacc[q][:, 0:TW], pbuf[q][:, 1:1 + TW], hcw[:, ch, 1:2], hcw[:, ch, 3:4], ALU.mult, ALU.add,
                                    [('pbuf', q), 'hcw%d' % l], [('cacc', q)])
                            self.stt('dve', cacc[q][:, 0:TW], pbuf[q][:, 0:TW], hcw[:, ch, 0:1], cacc[q][:, 0:TW], ALU.mult, ALU.add,
                                     [('pbuf', q), 'hcw%d' % l, ('cacc', q)], [('cacc', q)])
                            if gi == 2:
                                self.stt('dve', hyT[:, fc, T * TW:(T + 1) * TW], pbuf[q][:, 2:2 + TW], hcw[:, ch, 2:3], cacc[q][:, 0:TW],
                                         ALU.mult, ALU.add, [('pbuf', q), 'hcw%d' % l, ('cacc', q)], [('x2T', T, fc)])
                            else:
                                self.stt('dve', cfm[q][:, 0:TW], pbuf[q][:, 2:2 + TW], hcw[:, ch, 2:3], cacc[q][:, 0:TW],
                                         ALU.mult, ALU.add, [('pbuf', q), 'hcw%d' % l, ('cacc', q)], [('cfm', q)])
                                pt = 4 + q
                                ptk = 'ps%d' % pt
                                Tb = self.ps[pt][:].bitcast(BF16)
                                for s4 in range(spt):
                                    self.tr(Tb[:, s4 * 128:(s4 + 1) * 128], cfm[q][:, s4 * 128:(s4 + 1) * 128], P['identb'][:],
                                            [('cfm', q), 'identb'], [ptk])
                                dstt = ztok if gi == 0 else x1tok
                                dk = 'ztok' if gi == 0 else 'x1tok'
                                self.cp('act', dstt[:, T * spt:(T + 1) * spt, fc * 128:(fc + 1) * 128],
                                        Tb[:, 0:spt * 128].rearrange("p (s c) -> p s c", s=spt), [ptk],
                                        [(dk, T * spt + s4, fc) for s4 in range(spt)])
            self.S.barrier()
            self.pf_fill(0, wA[8])
            self.pf_fill(1, wA[10])
            self.S.phase = kind + '_hyena2'
            zk = [('ztok', i, fc) for i in range(nt) for fc in range(4)]
            x1k = [('x1tok', i, fc) for i in range(nt) for fc in range(4)]
            if isL and b == 0 and l == 0:
                self.dump('ztok', ztok[:], zk, [128, nt, 512], BF16)
                self.dump('x1tok', x1tok[:], x1k, [128, nt, 512], BF16)
                self.dump('x2T', hyT[:], [('x2T', T, fc) for T in range(nT) for fc in range(4)], [128, 4, L], BF16)
            with ExitStack() as es3:
                zs = [self.sb(es3, 'zs', [128, 512], F32) for _ in range(2)]
                ta = [self.sb(es3, 'ta', [128, 512], F32) for _ in range(2)]
                tb_ = [self.sb(es3, 'tb', [128, 512], F32) for _ in range(2)]
                tc_ = [self.sb(es3, 'tc', [128, 512], F32) for _ in range(2)]
                td = [self.sb(es3, 'td', [128, 512], F32) for _ in range(2)]
                ytok = [self.sb(es3, 'ytok', [128, 512], BF16) for _ in range(2)]
                cntl = 0
                for o in range(2):
                    zin_keys = zk if o == 0 else [('z1', i) for i in range(nt)]
                    Yk = []
                    for m in range(nt):
                        s = cntl % 2
                        cntl += 1
                        self.dma(cb[s][:], I['cf' + kind][m], [], [('cb', s)], 'cb%d' % s)
                        self.dma(sbf_[s][:], I['sf' + kind][m], [], [('sbf', s)], 'sbf%d' % s)
                        self.dma(kk_[s][:].rearrange("p a c -> p (a c)"), spec[o, m], [('spec', kind, l)], [('kk', s)], 'kk%d' % s)
                        pr, psn = self.ps[s * 2], self.ps[s * 2 + 1]
                        prk, psk = 'ps%d' % (s * 2), 'ps%d' % (s * 2 + 1)
                        for t_ in range(nt):
                            self.mm(pr[:, :], cb[s][:, t_, :], ztok[:, t_, :], t_ == 0, t_ == nt - 1, [('cb', s)] + zin_keys, [prk])
                        for t_ in range(nt):
                            self.mm(psn[:, :], sbf_[s][:, t_, :], ztok[:, t_, :], t_ == 0, t_ == nt - 1, [('sbf', s)] + zin_keys, [psk])
                        self.cp('act', zs[s][:], psn[:, :], [psk], [('zs', s)])
                        self.tt('dve', ta[s][:], pr[:, :], kk_[s][:, 0, :], ALU.mult, [prk, ('kk', s)], [('ta', s)])
                        self.tt('pool', tb_[s][:], zs[s][:], kk_[s][:, 1, :], ALU.mult, [('zs', s), ('kk', s)], [('tb', s)])
                        self.tt('dve', Yr[:, m, :], ta[s][:], tb_[s][:], ALU.subtract, [('ta', s), ('tb', s)], [('Yr', o, m)])
                        self.tt('dve', tc_[s][:], pr[:, :], kk_[s][:, 1, :], ALU.mult, [prk, ('kk', s)], [('tc', s)])
                        self.tt('pool', td[s][:], zs[s][:], kk_[s][:, 0, :], ALU.mult, [('zs', s), ('kk', s)], [('td', s)])
                        self.tt('pool', Ys[:, m, :], tc_[s][:], td[s][:], ALU.add, [('tc', s), ('td', s)], [('Ys', o, m)])
                        Yk.append(('Yr', o, m))
                        Yk.append(('Ys', o, m))
                    for m in range(nt):
                        s = cntl % 2
                        cntl += 1
                        self.dma(cb[s][:], I['ci' + kind][m], [], [('cb', s)], 'cb%d' % s)
                        self.dma(sbf_[s][:], I['si' + kind][m], [], [('sbf', s)], 'sbf%d' % s)
                        py = self.ps[4 + s]
                        pyk = 'ps%d' % (4 + s)
                        for t_ in range(nt):
                            self.mm(py[:, :], cb[s][:, t_, :], Yr[:, t_, :], t_ == 0, False, [('cb', s)] + Yk, [pyk])
                        for t_ in range(nt):
                            self.mm(py[:, :], sbf_[s][:, t_, :], Ys[:, t_, :], False, t_ == nt - 1, [('sbf', s)] + Yk, [pyk])
                        if o == 0:
                            self.tt('dve', ztok[:, m, :], py[:, :], x1tok[:, m, :], ALU.mult, [pyk] + x1k + zk, [('z1', m)] + [('ztok', m, fc) for fc in range(4)])
                        else:
                            self.cp('act', ytok[s][:], py[:, :], [pyk], [('ytok', s)])
                            pt = 6 + s
                            ptk = 'ps%d' % pt
                            Tb = self.ps[pt][:].bitcast(BF16)
                            for cc in range(4):
                                self.tr(Tb[:, cc * 128:(cc + 1) * 128], ytok[s][:, cc * 128:(cc + 1) * 128], P['identb'][:],
                                        [('ytok', s), 'identb'], [ptk])
                            T_ = (m * 128) // TW
                            self.tt('dve', hyT[:, :, m * 128:(m + 1) * 128], Tb[:, 0:512].rearrange("p (c t) -> p c t", c=4),
                                    hyT[:, :, m * 128:(m + 1) * 128], ALU.mult, [ptk] + [('x2T', T_, fc) for fc in range(4)],
                                    [('hyT', m)] + [('x2T', T_, fc) for fc in range(4)])
                    if o == 0 and isL and b == 0 and l == 0:
                        self.dump('z1tok', ztok[:], [('z1', i) for i in range(nt)], [128, nt, 512], BF16)

    def phase_merge1(self, b, l, kind, hT, hkeys, hcols, hyT, mT):
        P = self.P
        R = self.R
        isL = kind == 'L'
        Ls = L if isL else LC
        TW = min(512, Ls)
        nT = Ls // TW
        wA = R['wA'][l]
        wB = R['wB'][l]
        with ExitStack() as es:
            aT = self.sb(es, 'aTm', [128, 4, Ls], BF16)
            attnT = self.sb(es, 'attnTm', [128, 4, Ls], BF16)
            wg = [self.sb(es, 'wg', [128, 8, 512], BF16) for _ in range(3)]
            wb = [self.sb(es, 'wbr', [128, 4, 512], BF16) for _ in range(3)]
            sg = [self.sb(es, 'sg', [128, 512], BF16) for _ in range(3)]
            m1 = [self.sb(es, 'm1', [128, 512], F32) for _ in range(2)]
            m2 = [self.sb(es, 'm2', [128, 512], F32) for _ in range(2)]
            self.dma(aT[:], R['sa' + kind], [('sa', kind)], ['aTm'], 'aTm')
            self.dma(attnT[:], R['satt' + kind], [('satt', kind)], ['attnTm'], 'attnTm')
            srcs = ((aT, 'aTm'), (attnT, 'attnTm'), (hyT, None))
            cnt = 0
            for fq in range(2):
                wgb = []
                for gi in range(3):
                    if fq == 0 and gi < 2:
                        wgb.append((self.PF[gi], ('PF', gi)))
                    else:
                        self.dma(wg[gi][:], wA[8 + 2 * gi + fq], [('wscr',)], [('wg', gi)], 'wg%d' % gi)
                        wgb.append((wg[gi], ('wg', gi)))
                    self.dma(wb[gi][:], wB[gi, fq], [('wscr',)], [('wbr', gi)], 'wbr%d' % gi)
                if fq == 1:
                    self.pf_fill(0, R['wO'][l][0])
                    self.pf_fill(1, R['wO'][l][1])
                for f4 in range(4):
                    fc = fq * 4 + f4
                    for T in range(nT):
                        q = cnt % 2
                        cnt += 1
                        for gi in range(3):
                            pg = gi
                            pgk = 'ps%d' % pg
                            for k in range(8):
                                self.mm(self.ps[pg][:, 0:TW], wgb[gi][0][:, k, f4 * 128:(f4 + 1) * 128], hT[:, k, hcols(T)], k == 0, k == 7,
                                        [wgb[gi][1]] + hkeys(T, k=k), [pgk])
                            self.act(sg[gi][:, 0:TW], self.ps[pg][:, 0:TW], AF.Sigmoid, [pgk], [('sg', gi)])
                        for gi in range(3):
                            pbn = 3 + gi
                            pbk = 'ps%d' % pbn
                            src, sk = srcs[gi]
                            if sk is None:
                                skeys = [('hyT', i) for i in range(Ls // 128)]
                            else:
                                skeys = [sk]
                            for k in range(4):
                                self.mm(self.ps[pbn][:, 0:TW], wb[gi][:, k, f4 * 128:(f4 + 1) * 128], src[:, k, T * TW:(T + 1) * TW],
                                        k == 0, k == 3, [('wbr', gi)] + skeys, [pbk])
                        self.tt('dve', m1[q][:, 0:TW], self.ps[3][:, 0:TW], sg[0][:, 0:TW], ALU.mult, ['ps3', ('sg', 0)], [('m1', q)])
                        self.tt('dve', m2[q][:, 0:TW], self.ps[4][:, 0:TW], sg[1][:, 0:TW], ALU.mult, ['ps4', ('sg', 1)], [('m2', q)])
                        self.tt('pool', m1[q][:, 0:TW], m1[q][:, 0:TW], m2[q][:, 0:TW], ALU.add, [('m1', q), ('m2', q)], [('m1', q)])
                        self.tt('dve', m2[q][:, 0:TW], self.ps[5][:, 0:TW], sg[2][:, 0:TW], ALU.mult, ['ps5', ('sg', 2), ('m2', q)],
                                [('m2', q)])
                        self.tt('pool', mT[:, fc, T * TW:(T + 1) * TW], m1[q][:, 0:TW], m2[q][:, 0:TW], ALU.add, [('m1', q), ('m2', q)],
                                [('mT', T, fc)])

    def phase_merge2(self, b, l, kind, hT, hkey, mT, xsrc, xsk, xdst, xdk, A2, B2, ABk, bcol):
        P = self.P
        R = self.R
        isL = kind == 'L'
        Ls = L if isL else LC
        nt = Ls // 128
        TW = min(512, Ls)
        wO = R['wO'][l]
        with ExitStack() as es:
            g1bc = self.sb(es, 'g1bc', [128, D], F32)
            xt = [self.sb(es, 'xtm', [128, D], F32) for _ in range(3)]
            tmp = [self.sb(es, 'tmpm', [128, 512], F32) for _ in range(2)]
            junk = [self.sb(es, 'junk2', [128, D], BF16) for _ in range(2)]
            xn = [self.sb(es, 'xn2', [128, D], BF16) for _ in range(2)]
            tmpf = None
            self.dma(g1bc[:], R['modrow'][l][bcol:bcol + 1, 2 * D:3 * D].partition_broadcast(128), [('modrow', l)], ['g1bc'], 'g1bc')
            for i in range(nt):
                s = i % 3
                T = (i * 128) // TW
                self.dma(xt[s][:], xsrc[i * 128:(i + 1) * 128, :], [xsk + (i,)], [('xtm', s)], 'xtm%d' % s)
                for half in range(2):
                    pi = 2 + half + 2 * (i % 2)
                    pk = 'ps%d' % pi
                    for k in range(8):
                        self.mm(self.ps[pi][:, :], mT[:, k, i * 128:(i + 1) * 128], self.PF[half][:, k, :], k == 0, k == 7,
                                [('PF', half)] + [('mT', T, k)], [pk])
                    self.tt('dve', tmp[half][:], self.ps[pi][:, :], g1bc[:, half * 512:(half + 1) * 512], ALU.mult, [pk, 'g1bc'],
                            [('tmpm', half)])
                    self.tt('pool', xt[s][:, half * 512:(half + 1) * 512], xt[s][:, half * 512:(half + 1) * 512], tmp[half][:], ALU.add,
                            [('tmpm', half), ('xtm', s)], [('xtm', s)])
                self.dma(xdst[i * 128:(i + 1) * 128, :], xt[s][:], [('xtm', s)], [xdk + (i,)], 'xtm_st%d' % s)
                self.norm_tile((junk, xn, tmpf, i % 2), xt[s][:], ('xtm', s), A2, B2, ABk, hT, hkey, i)
            self.pf_fill(0, R['wU'][l][0])
            self.pf_fill(1, R['wU'][l][5])

    def phase_ffn(self, b, l, kind, hT, hkeys, hcols, halo_ap, xdst, xdk, bcol, final):
        P = self.P
        R = self.R
        isL = kind == 'L'
        Ls = L if isL else LC
        nt = Ls // 128
        TW = min(512, Ls)
        nT = Ls // TW
        spt = TW // 128
        wU = R['wU'][l]
        wD = R['wD'][l]
        fcw = P['fcw%d' % l]
        with ExitStack() as es:
            wd = self.sb(es, 'wd', [128, 2, 22, 512], BF16)
            g2bc = self.sb(es, 'g2bc', [128, D], F32)
            G = self.sb(es, 'G', [128, 22, TW], BF16)
            wa = [self.sb(es, 'wa', [128, 8, 512], BF16) for _ in range(2)]
            wbb = [self.sb(es, 'wbb', [128, 8, 512], BF16) for _ in range(2)]
            pbuf = [self.sb(es, 'pbuf2', [128, 516], F32) for _ in range(2)]
            cacc = [self.sb(es, 'cacc2', [128, 512], F32) for _ in range(2)]
            ga = [self.sb(es, 'ga', [128, 512], F32) for _ in range(2)]
            xt = [self.sb(es, 'xtf', [128, D], F32) for _ in range(2)]
            tmp = [self.sb(es, 'tmpf3', [128, 512], F32) for _ in range(2)]
            sf_ = self.sb(es, 'sfin', [128, 8], F32)
            junk = self.sb(es, 'junk3', [128, D], BF16)
            for half in range(2):
                self.dma(wd[:, half], wD[half], [('wscr',)], [('wd', half)], 'wd%d' % half)
            self.dma(g2bc[:], R['modrow'][l][bcol:bcol + 1, 5 * D:6 * D].partition_broadcast(128), [('modrow', l)], ['g2bc'], 'g2bc')
            cur = {'a': (None, None), 'b': (None, None)}
            lc = {'a': 0, 'b': 0}

            pfused = {'a': False, 'b': False}

            def getw(which, g):
                bufs = wa if which == 'a' else wbb
                if not pfused[which]:
                    pi_ = 0 if which == 'a' else 1
                    if cur[which][0] is None:
                        cur[which] = (g, 'pf')
                    if cur[which] == (g, 'pf'):
                        return self.PF[pi_], ('PF', pi_)
                    pfused[which] = True
                if cur[which][0] != g or cur[which][1] == 'pf':
                    s = lc[which] % 2
                    lc[which] += 1
                    self.dma(bufs[s][:], wU[g], [('wscr',)], [('w' + which, s)], 'wu%s%d' % (which, s))
                    cur[which] = (g, s)
                s = cur[which][1]
                return bufs[s], ('w' + which, s)
            cnt = 0
            xcnt = 0
            for T in range(nT):
                for fc in range(22):
                    q = cnt % 2
                    cnt += 1
                    ba, bak = getw('a', fc // 4)
                    bb_, bbk = getw('b', (22 + fc) // 4)
                    ca = (fc % 4) * 128
                    cbo = ((22 + fc) % 4) * 128
                    pa, ph, pb_ = self.ps[q * 3], self.ps[q * 3 + 1], self.ps[q * 3 + 2]
                    pak, phk, pbk = 'ps%d' % (q * 3), 'ps%d' % (q * 3 + 1), 'ps%d' % (q * 3 + 2)
                    for k in range(8):
                        self.mm(pa[:, 0:TW], ba[:, k, ca:ca + 128], hT[:, k, hcols(T)], k == 0, k == 7, [bak] + hkeys(T, k=k), [pak])
                    for k in range(8):
                        self.mm(ph[:, 0:2], ba[:, k, ca:ca + 128], halo_ap(k, T), k == 0, k == 7, [bak] + hkeys(T, True, k), [phk])
                    for k in range(8):
                        self.mm(pb_[:, 0:TW], bb_[:, k, cbo:cbo + 128], hT[:, k, hcols(T)], k == 0, k == 7, [bbk] + hkeys(T, k=k), [pbk])
                    self.cp('act', pbuf[q][:, 1:1 + TW], pa[:, 0:TW], [pak], [('pbuf2', q)])
                    hdst = bass.AP(pbuf[q][:, 0:1].tensor, pbuf[q][:, 0:1].offset, [list(pbuf[q][:, 0:1].ap[0]), [TW + 1, 2]])
                    self.cp('act', hdst, ph[:, 0:2], [phk, ('pbuf2', q)], [('pbuf2', q)])
                    self.ts('dve', cacc[q][:, 0:TW], pbuf[q][:, 1:1 + TW], fcw[:, fc, 1:2], fcw[:, fc, 3:4], ALU.mult, ALU.add,
                            [('pbuf2', q), 'fcw%d' % l], [('cacc2', q)])
                    self.stt('dve', cacc[q][:, 0:TW], pbuf[q][:, 0:TW], fcw[:, fc, 0:1], cacc[q][:, 0:TW], ALU.mult, ALU.add,
                             [('pbuf2', q), 'fcw%d' % l, ('cacc2', q)], [('cacc2', q)])
                    self.stt('dve', cacc[q][:, 0:TW], pbuf[q][:, 2:2 + TW], fcw[:, fc, 2:3], cacc[q][:, 0:TW], ALU.mult, ALU.add,
                             [('pbuf2', q), 'fcw%d' % l, ('cacc2', q)], [('cacc2', q)])
                    self.act(ga[q][:, 0:TW], cacc[q][:, 0:TW], AF.Gelu, [('cacc2', q)], [('ga', q)])
                    self.tt('dve', G[:, fc, :], ga[q][:, 0:TW], pb_[:, 0:TW], ALU.mult, [('ga', q), pbk], [('G', fc)])
                Gk = [('G', fc) for fc in range(22)]
                for s4 in range(spt):
                    i = T * spt + s4
                    s = xcnt % 2
                    xcnt += 1
                    self.dma(xt[s][:], xdst[i * 128:(i + 1) * 128, :], [xdk + (i,)], [('xtf', s)], 'xtf%d' % s)
                    for half in range(2):
                        pi = 6 + half
                        pk = 'ps%d' % pi
                        for k in range(22):
                            self.mm(self.ps[pi][:, :], G[:, k, s4 * 128:(s4 + 1) * 128], wd[:, half, k, :], k == 0, k == 21,
                                    [('wd', half)] + Gk, [pk])
                        self.tt('dve', tmp[half][:], self.ps[pi][:, :], g2bc[:, half * 512:(half + 1) * 512], ALU.mult, [pk, 'g2bc'],
                                [('tmpf3', half)])
                        self.tt('pool', xt[s][:, half * 512:(half + 1) * 512], xt[s][:, half * 512:(half + 1) * 512], tmp[half][:],
                                ALU.add, [('tmpf3', half), ('xtf', s)], [('xtf', s)])
                    if final:
                        self.memset('pool', sf_[:, 0:1], 0.0, ['sfin'])
                        self.act(junk[:], xt[s][:], AF.Square, [('xtf', s), 'sfin'], ['junk3', 'sfin'], accum=sf_[:, 0:1])
                        self.ts('dve', sf_[:, 1:2], sf_[:, 0:1], 1.0 / D, EPS, ALU.mult, ALU.add, ['sfin'], ['sfin'])
                        self.act(sf_[:, 2:3], sf_[:, 1:2], AF.Sqrt, ['sfin'], ['sfin'])
                        self.recip(sf_[:, 3:4], sf_[:, 2:3], ['sfin'], ['sfin'])
                        self.stt('dve', xt[s][:], xt[s][:], sf_[:, 3:4], P['fgbc'][:], ALU.mult, ALU.mult, [('xtf', s), 'sfin', 'fgbc'],
                                 [('xtf', s)])
                        self.dma(self.out[b, i * 128:(i + 1) * 128, :], xt[s][:], [('xtf', s)], [('out', b, i)], 'xtf_st%d' % s)
                    else:
                        self.dma(xdst[i * 128:(i + 1) * 128, :], xt[s][:], [('xtf', s)], [xdk + (i,)], 'xtf_st%d' % s)


def _prep_inputs(inp, NB, core):
    C = _consts()
    f32 = np.float32
    b0 = core * NB
    m = {}
    m['x'] = np.ascontiguousarray(inp['x'][b0:b0 + NB])
    m['ctx'] = np.ascontiguousarray(inp['ctx'][b0:b0 + NB])
    m['c5'] = np.ascontiguousarray(np.concatenate([inp['c'][b0:b0 + NB], inp['c_ctx'][None, :]], axis=0))
    return m


def _shared_inputs(inp):
    C = _consts()
    perm = _swap_perm()
    w_in = inp['w_in']
    w_ext = np.concatenate([w_in, w_in[:, :, 1024 + perm], w_in[:, :, 1536 + perm]], axis=2)
    m = {}
    m['ada_w'] = inp['ada_w']
    m['ada_b'] = inp['ada_b']
    m['norm1_g'] = inp['norm1_g']
    m['norm2_g'] = inp['norm2_g']
    m['w_in'] = np.ascontiguousarray(w_ext)
    m['sgu_norm_g'] = inp['sgu_norm_g']
    m['sgu_w'] = inp['sgu_w']
    m['sgu_b'] = np.ascontiguousarray(inp['sgu_b'].reshape(2, 512))
    m['rpbg'] = np.ascontiguousarray(_rpb_gather(inp['na_rpb']).reshape(2, 128, NH * 14 * 64))
    m['hy_conv'] = np.ascontiguousarray(np.concatenate([inp['hy_conv_w'], inp['hy_conv_b'][:, None, :]], axis=1))
    m['hy_p3'] = np.ascontiguousarray(np.stack([inp['hy_filt_b1'], inp['hy_filt_b2'], inp['hy_sin_freq']], axis=1))
    m['hy_w1'] = inp['hy_filt_w1']
    m['hy_w2'] = inp['hy_filt_w2']
    m['hy_w3'] = inp['hy_filt_w3']
    m['hy_skip'] = inp['hy_skip']
    m['w_branch'] = inp['w_branch']
    m['w_out'] = inp['w_out']
    m['ffn_up'] = inp['ffn_w_up']
    m['ffn_conv'] = np.ascontiguousarray(np.concatenate([inp['ffn_conv_w'], inp['ffn_conv_b'][:, None, :]], axis=1))
    m['ffn_down'] = inp['ffn_w_down']
    m['final_g'] = np.ascontiguousarray(inp['final_norm_g'][None, :])
    m['cosT'] = C['cosT']
    m['sinT'] = C['sinT']
    m['m8'] = C['m8']
    m['mn8'] = C['mn8']
    for kind, key in (('L', 'hyL'), ('C', 'hyC')):
        h = C[key]
        m['zT' + kind] = h['zT']
        m['decay' + kind] = h['decay']
        for nm in ('cf', 'sf', 'ci', 'si'):
            m[nm + kind] = h[nm]
    return {k: np.ascontiguousarray(v) for k, v in m.items()}


def run(inputs, NB=4, nlayers=2, debug=(), ncores=NCORES, stop=None):
    inp = {k: np.asarray(v) for k, v in inputs.items()}
    prog = Prog(NB=NB, nlayers=nlayers, debug=debug, stop=stop)
    nc = prog.build()
    shared = _shared_inputs(inp)
    in_maps = []
    for c in range(ncores):
        m = dict(shared)
        m.update(_prep_inputs(inp, NB, c))
        in_maps.append(m)
    res = run_bass_kernel_spmd(nc, in_maps, core_ids=list(range(ncores)))
    return prog, res


def kernel(**inputs):
    prog, res = run(inputs, NB=4, nlayers=2)
    out = np.concatenate([np.asarray(r['out']) for r in res.results], axis=0)
    return out.astype(np.float32)
```

```python
import math
from contextlib import ExitStack

import numpy as np
import ml_dtypes

import concourse.bass as bass
import concourse.mybir as mybir
from concourse.bass_utils import run_bass_kernel_spmd

F32 = mybir.dt.float32
BF16 = mybir.dt.bfloat16
AF = mybir.ActivationFunctionType
ALU = mybir.AluOpType

D = 1024
L = 2048
LC = 256
DFF = 2816
NH = 8
HD = 64
GRID_W = 64
NROWS = 32
EPS = 1e-6
NCORES = 8
NEG8 = -240000.0


class Sched:
    ENG = ('pe', 'act', 'dve', 'pool', 'sp')
    DMA_BW = 140.0
    DMA_LAT = 2000.0
    frozen = False
    reorder = True

    def __init__(self, nc, nsem=4, epoch=4096):
        self.nc = nc
        self.e = {'pe': nc.tensor, 'act': nc.scalar, 'dve': nc.vector, 'pool': nc.gpsimd, 'sp': nc.sync}
        self.ops = []
        self.labels = []
        self.costs = []
        self.phase = 'init'
        self.nsem = nsem
        self.epoch = epoch

    def op(self, eng, fn, reads=(), writes=(), dma=None, cost=200.0, nbytes=0):
        if self.frozen:
            return
        self.ops.append((eng, fn, tuple(reads), tuple(writes), dma))
        self.labels.append(self.phase)
        self.costs.append((cost, nbytes))

    def barrier(self):
        if self.frozen:
            return
        self.ops.append(('barrier', None, (), (), None))
        self.labels.append(self.phase)
        self.costs.append((0.0, 0))

    def _schedule(self, seg, deps_all):
        import heapq
        ops = self.ops
        succ = {i: [] for i in seg}
        indeg = {i: 0 for i in seg}
        for i in seg:
            for j in deps_all[i]:
                succ[j].append(i)
                indeg[i] += 1
        eng_free = {e: 0.0 for e in self.ENG}
        dma_free = 0.0
        dep_ready = {i: 0.0 for i in seg}
        ready = []
        for i in seg:
            if indeg[i] == 0:
                heapq.heappush(ready, (0.0, i))
        out = []
        while ready:
            est, i = heapq.heappop(ready)
            eng = ops[i][0]
            t0 = max(eng_free[eng], dep_ready[i])
            if t0 > est + 1e-6:
                heapq.heappush(ready, (t0, i))
                continue
            cost, nbytes = self.costs[i]
            if ops[i][4] is not None:
                eng_free[eng] = t0 + 60.0
                ts = max(t0 + 60.0, dma_free)
                dma_free = ts + nbytes / self.DMA_BW
                fin = dma_free + self.DMA_LAT
            else:
                eng_free[eng] = t0 + cost
                fin = t0 + cost
            out.append(i)
            for s_ in succ[i]:
                lat = 40.0 if (ops[s_][0] == eng and ops[i][4] is None) else 160.0
                if eng == 'pe' and ops[s_][0] == 'pe' and ops[i][4] is None:
                    lat = 0.0
                r_ = fin + lat
                if r_ > dep_ready[s_]:
                    dep_ready[s_] = r_
                indeg[s_] -= 1
                if indeg[s_] == 0:
                    heapq.heappush(ready, (max(dep_ready[s_], eng_free[ops[s_][0]]), s_))
        assert len(out) == len(seg)
        return out, max(max(eng_free.values()), dma_free)

    def emit(self):
        nc = self.nc
        ops = self.ops
        n = len(ops)
        last_w = {}
        readers = {}
        deps_all = [None] * n
        segments = []
        cur = []
        for i, (eng, fn, reads, writes, dma) in enumerate(ops):
            if eng == 'barrier':
                segments.append((cur, i))
                cur = []
                last_w = {}
                readers = {}
                continue
            d = set()
            for k in reads:
                w = last_w.get(k)
                if w is not None:
                    d.add(w)
            for k in writes:
                w = last_w.get(k)
                if w is not None:
                    d.add(w)
                rr = readers.get(k)
                if rr:
                    d.update(rr)
            d.discard(i)
            deps_all[i] = d
            for k in writes:
                last_w[k] = i
                readers[k] = []
            for k in reads:
                readers.setdefault(k, []).append(i)
            cur.append(i)
        assert not cur, "program must end with a barrier"
        order = []
        model_ns = 0.0
        for seg, bi in segments:
            if self.reorder and seg:
                seg2, t = self._schedule(seg, deps_all)
                model_ns += t
            else:
                seg2 = seg
            order.extend(seg2)
            order.append(bi)
        need_sig = [False] * n
        fdeps = [None] * n
        has_dep = [False] * n
        segops = []
        for i in order:
            eng, fn, reads, writes, dma = ops[i]
            if eng == 'barrier':
                lastc = {}
                out = []
                for j in segops:
                    if ops[j][4] is None:
                        lastc[ops[j][0]] = j
                    elif not has_dep[j]:
                        out.append(j)
                out.extend(lastc.values())
                for j in out:
                    need_sig[j] = True
                fdeps[i] = sorted(out)
                segops = []
                continue
            out = []
            for j in deps_all[i]:
                has_dep[j] = True
                eng_j, _, _, _, dma_j = ops[j]
                if dma_j is None and dma is None and eng_j == eng and eng == 'pe':
                    continue
                out.append(j)
                need_sig[j] = True
            fdeps[i] = out
            segops.append(i)
        ticket = [None] * n
        cnt = {e: 0 for e in self.ENG}
        semc = {}

        def getsem(name):
            if name not in semc:
                semc[name] = nc.alloc_semaphore(name)
            return semc[name]
        semval = {}
        for i in order:
            eng, fn, reads, writes, dma = ops[i]
            if eng == 'barrier':
                continue
            if dma is not None:
                nm = 'd_' + str(dma)
                semval[nm] = semval.get(nm, 0) + 16
                ticket[i] = (nm, semval[nm])
            elif need_sig[i]:
                ep = (cnt[eng] // self.epoch) % self.nsem
                cnt[eng] += 1
                nm = 'c_%s%d' % (eng, ep)
                semval[nm] = semval.get(nm, 0) + 1
                ticket[i] = (nm, semval[nm])
        waited = {e: {} for e in self.ENG}
        nwait = 0
        for i in order:
            eng, fn, reads, writes, dma = ops[i]
            need = {}
            for j in fdeps[i]:
                nm, v = ticket[j]
                if need.get(nm, 0) < v:
                    need[nm] = v
            if eng == 'barrier':
                for en in self.ENG:
                    E = self.e[en]
                    for nm, v in need.items():
                        if waited[en].get(nm, 0) < v:
                            E.wait_ge(getsem(nm), v)
                            waited[en][nm] = v
                            nwait += 1
                continue
            E = self.e[eng]
            for nm, v in need.items():
                if waited[eng].get(nm, 0) < v:
                    E.wait_ge(getsem(nm), v)
                    waited[eng][nm] = v
                    nwait += 1
            ins = fn(E)
            if ticket[i] is not None:
                nm, v = ticket[i]
                ins.then_inc(getsem(nm), 16 if dma is not None else 1)
        self.order = order
        self.stats = dict(ops=n, waits=nwait, sems=len(semc), maxval=max(semval.values()) if semval else 0,
                          model_us=model_ns / 1e3)


def _rope_tables():
    t = np.arange(L)
    rows = t // GRID_W
    cols = t % GRID_W
    inv = 10000.0 ** (-np.arange(16, dtype=np.float64) * 2.0 / 32.0)
    cosT = np.zeros((128, L), np.float64)
    sinT = np.zeros((128, L), np.float64)
    for p in range(128):
        d = p % 64
        i = d % 16
        isb = (d % 32) >= 16
        pos = rows if d < 32 else cols
        ang = pos.astype(np.float32).astype(np.float64) * np.float32(inv[i]).astype(np.float64)
        cosT[p] = np.cos(ang)
        sinT[p] = np.sin(ang) * (1.0 if isb else -1.0)
    return cosT.astype(np.float32), sinT.astype(np.float32)


def _swap_perm():
    perm = np.zeros(512, np.int64)
    for h in range(NH):
        for d in range(HD):
            dd = d + 16 if (d % 32) < 16 else d - 16
            perm[h * 64 + d] = h * 64 + dd
    return perm


def _mask_tables():
    m8 = np.zeros((128, 64), np.float32)
    mn8 = np.zeros((128, 64), np.float32)
    for p in range(128):
        kc = p % 64
        for qc in range(64):
            cs = min(max(qc - 8, 0), 48)
            ok = (kc >= cs) and (kc < cs + 16)
            m8[p, qc] = 8.0 if ok else 0.0
            mn8[p, qc] = 0.0 if ok else NEG8
    return m8, mn8


def _rpb_gather(rpb):
    kc = np.arange(64)[:, None]
    qc = np.arange(64)[None, :]
    dc = np.clip(kc - qc + 15, 0, 30)
    out = np.zeros((2, 128, NH, 14, 64), np.float32)
    for half in range(2):
        for o in range(14):
            g = rpb[:, :, o + half, :][:, :, dc]
            out[:, half * 64:(half + 1) * 64, :, o, :] = g.transpose(0, 2, 1, 3)
    return out


def _hy_consts(Ls):
    nt = Ls // 128
    f32 = np.float32
    t = np.linspace(0.0, 1.0, Ls, dtype=f32)[:, None]
    bands = np.linspace(1e-4, 15, 16, dtype=f32)[None, :]
    ang = (f32(2.0 * math.pi) * np.arange(Ls, dtype=f32)[:, None] / f32(Ls)) * bands
    z = np.concatenate([t, np.cos(ang), -np.sin(ang)], axis=-1).astype(f32)
    zT = np.ascontiguousarray(z.T)
    max_decay = math.log(1e-2) / 0.3
    min_decay = math.log(1e-2) / 1.5
    deltas = np.abs(np.linspace(min_decay, max_decay, 512, dtype=f32))
    decay = np.exp(-t * deltas[None, :]).astype(f32)
    decay_t = np.ascontiguousarray(decay.reshape(nt, 128, 512).transpose(1, 0, 2))
    tt = np.arange(Ls, dtype=np.int64)[:, None]
    ff = np.arange(Ls, dtype=np.int64)[None, :]
    nph = (tt * (2 * ff + 1)) % (4 * Ls)
    ang2 = 2.0 * math.pi * nph.astype(np.float64) / (4 * Ls)
    C = np.cos(ang2)
    Sn = np.sin(ang2)

    def tile_fwd(M):
        return np.ascontiguousarray(M.reshape(nt, 128, nt, 128).transpose(2, 1, 0, 3)).astype(ml_dtypes.bfloat16)

    def tile_inv(M):
        return np.ascontiguousarray(M.reshape(nt, 128, nt, 128).transpose(0, 3, 2, 1)).astype(ml_dtypes.bfloat16)
    return dict(zT=zT, decay=decay_t, cf=tile_fwd(C), sf=tile_fwd(Sn), ci=tile_inv(C), si=tile_inv(Sn))


_CONST_CACHE = {}


def _consts():
    if not _CONST_CACHE:
        cosT, sinT = _rope_tables()
        m8, mn8 = _mask_tables()
        _CONST_CACHE.update(cosT=cosT, sinT=sinT, m8=m8, mn8=mn8, hyL=_hy_consts(L), hyC=_hy_consts(LC))
    return _CONST_CACHE


class StopBuild(Exception):
    pass


class Prog:
    def __init__(self, NB=4, nlayers=2, debug=(), stop=None):
        self.stop = stop
        self.NB = NB
        self.nlayers = nlayers
        self.debug = set(debug)
        self.nc = bass.Bass("TRN2", target_bir_lowering=False)
        self.S = Sched(self.nc)
        self.uid = 0
        self.dbg_out = {}

    def din(self, name, shape, dt=F32):
        return self.nc.dram_tensor(name, list(shape), dt, kind="ExternalInput").ap()

    def dscr(self, name, shape, dt):
        return self.nc.dram_tensor(name, list(shape), dt, kind="Internal").ap()

    def dout(self, name, shape, dt=F32):
        return self.nc.dram_tensor(name, list(shape), dt, kind="ExternalOutput").ap()

    def sb(self, es, name, shape, dt):
        self.uid += 1
        return es.enter_context(self.nc.sbuf_tensor("%s_%d" % (name, self.uid), list(shape), dt))

    @staticmethod
    def _fs(ap):
        n = 1
        for d in ap.shape[1:]:
            n *= int(d)
        return n

    def mm(self, out, lhsT, rhs, start, stop, r, w):
        nn = self._fs(rhs)
        c = max(nn, 48) / 2.2 + 16.0
        if lhsT.dtype == F32:
            c *= 4.0
        self.S.op('pe', lambda e: e.matmul(out, lhsT=lhsT, rhs=rhs, start=start, stop=stop), r, w, cost=c)

    def tr(self, out, in_, ident, r, w):
        c = 64.0 * (4.0 if in_.dtype == F32 else 1.0)
        self.S.op('pe', lambda e: e.transpose(out=out, in_=in_, identity=ident), r, w, cost=c)

    def act(self, out, in_, func, r, w, bias=None, scale=None, accum=None):
        kw = {}
        if bias is not None:
            kw['bias'] = bias
        if scale is not None:
            kw['scale'] = scale
        if accum is not None:
            kw['accum_out'] = accum
        self.S.op('act', lambda e: e.activation(out=out, in_=in_, func=func, **kw), r, w, cost=220.0 + self._fs(in_) / 1.1)

    def _vcost(self, eng, ap):
        nn = self._fs(ap)
        if eng == 'pool':
            return 250.0 + nn / 0.45
        if eng == 'act':
            return 220.0 + nn / 1.1
        return 130.0 + nn / 0.9

    def tt(self, eng, out, in0, in1, op, r, w):
        self.S.op(eng, lambda e: e.tensor_tensor(out=out, in0=in0, in1=in1, op=op), r, w, cost=self._vcost(eng, out))

    def ts(self, eng, out, in0, s1, s2, op0, op1, r, w):
        c = self._vcost(eng, out)
        if op1 is None:
            self.S.op(eng, lambda e: e.tensor_scalar(out=out, in0=in0, scalar1=s1, scalar2=None, op0=op0), r, w, cost=c)
        else:
            self.S.op(eng, lambda e: e.tensor_scalar(out=out, in0=in0, scalar1=s1, scalar2=s2, op0=op0, op1=op1), r, w, cost=c)

    def stt(self, eng, out, in0, scalar, in1, op0, op1, r, w):
        self.S.op(eng, lambda e: e.scalar_tensor_tensor(out=out, in0=in0, scalar=scalar, in1=in1, op0=op0, op1=op1), r, w,
                  cost=self._vcost(eng, out))

    def cp(self, eng, out, in_, r, w):
        c = self._vcost(eng, out)
        if eng == 'act':
            self.S.op('act', lambda e: e.copy(out=out, in_=in_), r, w, cost=c)
        else:
            self.S.op(eng, lambda e: e.tensor_copy(out=out, in_=in_), r, w, cost=c)

    def memset(self, eng, ap, val, w, r=()):
        self.S.op(eng, lambda e: e.memset(ap, val), r, w, cost=self._vcost(eng, ap))

    def recip(self, out, in_, r, w):
        self.S.op('dve', lambda e: e.reciprocal(out=out, in_=in_), r, w, cost=self._vcost('dve', out))

    def dma(self, out, in_, r, w, key):
        nb = int(out.shape[0]) * self._fs(out) * (2 if out.dtype == BF16 else 4)
        self.S.op('sp', lambda e: e.dma_start(out=out, in_=in_), r, w, dma=key, nbytes=nb)

    def chk(self, name):
        if self.stop == name:
            self.S.frozen = True

    def dump(self, tag, sb_ap, rkeys, shape, dt):
        if tag not in self.debug:
            return
        o = self.dout("dbg_" + tag, shape, dt)
        self.dbg_out[tag] = "dbg_" + tag
        self.dma(o, sb_ap, list(rkeys), [('dbg', tag)], 'dbg_' + tag)

    def build(self):
        nc = self.nc
        NB = self.NB
        NL = self.nlayers
        I = {}
        I['x'] = self.din('x', [NB, L, D])
        I['ctx'] = self.din('ctx', [NB, LC, D])
        I['c5'] = self.din('c5', [NB + 1, D])
        I['ada_w'] = self.din('ada_w', [2, D, 6 * D])
        I['ada_b'] = self.din('ada_b', [2, 6 * D])
        I['norm1_g'] = self.din('norm1_g', [2, D])
        I['norm2_g'] = self.din('norm2_g', [2, D])
        I['w_in'] = self.din('w_in', [2, D, 8192])
        I['sgu_norm_g'] = self.din('sgu_norm_g', [2, 512])
        I['sgu_w'] = self.din('sgu_w', [2, 4, 128, 128])
        I['sgu_b'] = self.din('sgu_b', [2, 512])
        I['rpbg'] = self.din('rpbg', [2, 128, NH * 14 * 64])
        I['hy_conv'] = self.din('hy_conv', [2, 4, 1536])
        I['hy_p3'] = self.din('hy_p3', [2, 3, 64])
        I['hy_w1'] = self.din('hy_w1', [2, 33, 64])
        I['hy_w2'] = self.din('hy_w2', [2, 64, 64])
        I['hy_w3'] = self.din('hy_w3', [2, 64, 2048])
        I['hy_skip'] = self.din('hy_skip', [2, 2, 512])
        I['w_branch'] = self.din('w_branch', [2, 3, 512, D])
        I['w_out'] = self.din('w_out', [2, D, D])
        I['ffn_up'] = self.din('ffn_up', [2, D, 2 * DFF])
        I['ffn_conv'] = self.din('ffn_conv', [2, 4, DFF])
        I['ffn_down'] = self.din('ffn_down', [2, DFF, D])
        I['final_g'] = self.din('final_g', [1, D])
        I['cosT'] = self.din('cosT', [128, L])
        I['sinT'] = self.din('sinT', [128, L])
        I['m8'] = self.din('m8', [128, 64])
        I['mn8'] = self.din('mn8', [128, 64])
        for kind, Ls in (('L', L), ('C', LC)):
            nt = Ls // 128
            I['zT' + kind] = self.din('zT' + kind, [33, Ls])
            I['decay' + kind] = self.din('decay' + kind, [128, nt, 512])
            for nm in ('cf', 'sf', 'ci', 'si'):
                I[nm + kind] = self.din(nm + kind, [nt, 128, nt, 128], BF16)
        self.I = I
        self.out = self.dout('out', [NB, L, D])
        R = {}
        R['wA'] = self.dscr('s_wA', [2, 16, 128, 8, 512], BF16)
        R['wB'] = self.dscr('s_wB', [2, 3, 2, 128, 4, 512], BF16)
        R['wO'] = self.dscr('s_wO', [2, 2, 128, 8, 512], BF16)
        R['wU'] = self.dscr('s_wU', [2, 11, 128, 8, 512], BF16)
        R['wD'] = self.dscr('s_wD', [2, 2, 128, 22, 512], BF16)
        R['modrow'] = self.dscr('s_modrow', [2, NB + 1, 6 * D], F32)
        R['tbl'] = self.dscr('s_tbl', [2, 128, NH * 27 * 64], BF16)
        R['specL'] = self.dscr('s_specL', [2, 2, L // 128, 128, 1024], F32)
        R['specC'] = self.dscr('s_specC', [2, 2, LC // 128, 128, 1024], F32)
        R['xs'] = self.dscr('s_xs', [NB, L, D], F32)
        R['xcs'] = self.dscr('s_xcs', [NB, LC, D], F32)
        R['saL'] = self.dscr('s_saL', [128, 4, L], BF16)
        R['sattL'] = self.dscr('s_sattL', [128, 4, L], BF16)
        R['saC'] = self.dscr('s_saC', [128, 4, LC], BF16)
        R['sattC'] = self.dscr('s_sattC', [128, 4, LC], BF16)
        self.R = R

        with ExitStack() as es:
            self.ps = [es.enter_context(nc.psum_tensor("ps%d" % i, [128, 512], F32)) for i in range(8)]
            P = {}
            P['identb'] = self.sb(es, 'identb', [128, 128], BF16)
            P['identf'] = self.sb(es, 'identf', [128, 128], F32)
            P['onesf'] = self.sb(es, 'onesf', [128, 128], F32)
            P['ss'] = self.sb(es, 'ss', [128, 8], F32)
            for l in range(2):
                P['g1T%d' % l] = self.sb(es, 'g1T', [128, 8], F32)
                P['g2T%d' % l] = self.sb(es, 'g2T', [128, 8], F32)
                P['gsnT%d' % l] = self.sb(es, 'gsnT', [128, 4], F32)
                P['wsT%d' % l] = self.sb(es, 'wsT', [128, 4, 128], BF16)
                P['bsb%d' % l] = self.sb(es, 'bsb', [128, 4, 128], F32)
                P['hcw%d' % l] = self.sb(es, 'hcw', [128, 12, 4], F32)
                P['fcw%d' % l] = self.sb(es, 'fcw', [128, 22, 4], F32)
                P['modT%d' % l] = self.sb(es, 'modT', [128, 48, NB + 1], F32)
            P['fgbc'] = self.sb(es, 'fgbc', [128, D], F32)
            P['KcT'] = self.sb(es, 'KcT', [128, 4, LC], BF16)
            P['Vc1'] = self.sb(es, 'Vc1', [128, 2, NH, 65], BF16)
            P['hT'] = self.sb(es, 'hT', [128, 8, L + 2], BF16)
            P['hTc'] = self.sb(es, 'hTc', [128, 8, LC + 2], BF16)
            self.P = P
            try:
                self.prologue(es)
                for b in range(NB):
                    for l in range(NL):
                        self.seq_pass(b, l, 'C')
                        self.seq_pass(b, l, 'L')
            except StopBuild:
                pass
            self.S.frozen = False
            self.S.barrier()
            self.S.emit()
        return nc

    def prologue(self, es0):
        P = self.P
        I = self.I
        R = self.R
        nc = self.nc
        NB = self.NB
        self.memset('pool', P['identf'][:], 0.0, ['identf'])
        self.S.op('pool', lambda e: e.affine_select(out=P['identf'][:], in_=P['identf'][:], pattern=[[-1, 128]],
                                                    compare_op=ALU.not_equal, fill=1.0, base=0, channel_multiplier=1),
                  ['identf'], ['identf'])
        self.cp('dve', P['identb'][:], P['identf'][:], ['identf'], ['identb'])
        self.memset('pool', P['onesf'][:], 1.0, ['onesf'])
        self.memset('pool', P['hT'][:], 0.0, [('hT', i) for i in range(L // 128)])
        self.memset('pool', P['hTc'][:], 0.0, [('hTc', i) for i in range(LC // 128)])
        self.memset('pool', P['Vc1'][:], 1.0, ['Vc1'])
        self.dma(P['fgbc'][:], I['final_g'][0:1, :].partition_broadcast(128), [], ['fgbc'], 'fgbc')
        self.S.phase = 'params'
        with ExitStack() as es:
            stage = self.sb(es, 'stage', [8, 6 * D], F32)
            s5 = self.sb(es, 's5', [8, D], F32)
            s5T = self.sb(es, 's5T', [128, 8, NB + 1], F32)
            adw = [self.sb(es, 'adw', [128, 8, 512], F32) for _ in range(2)]
            adb = self.sb(es, 'adb', [1, 6 * D], F32)
            sgw = self.sb(es, 'sgw', [128, 4, 128], F32)
            gst = self.sb(es, 'gst', [128, 8 * 14 * 64 // 2], F32)
            tblb = self.sb(es, 'tblb', [128, NH, 27, 64], BF16)
            m8 = self.sb(es, 'm8', [128, 64], F32)
            mn8 = self.sb(es, 'mn8', [128, 64], F32)
            nb1 = NB + 1
            self.dma(m8[:], I['m8'], [], ['m8'], 'm8')
            self.dma(mn8[:], I['mn8'], [], ['mn8'], 'mn8')

            def rows_to_cols(src_rows, Rr, N, dst, tag):
                nch = N // 128
                self.dma(stage[0:Rr, 0:N], src_rows, [], ['stage'], 'stage')
                pb = self.ps[0]
                for c in range(nch):
                    self.tr(pb[:, c * Rr:(c + 1) * Rr], stage[0:Rr, c * 128:(c + 1) * 128], P['identf'][0:Rr, 0:Rr],
                            ['stage', 'identf'], ['ps0'])
                self.cp('dve', dst, pb[:, 0:nch * Rr].rearrange("p (c r) -> p c r", r=Rr), ['ps0'], [tag])

            self.dma(s5[0:nb1, :], I['c5'], [], ['s5'], 's5')
            self.act(s5[0:nb1, :], s5[0:nb1, :], AF.Silu, ['s5'], ['s5'])
            pb = self.ps[1]
            for k in range(8):
                self.tr(pb[:, k * nb1:(k + 1) * nb1], s5[0:nb1, k * 128:(k + 1) * 128], P['identf'][0:nb1, 0:nb1],
                        ['s5', 'identf'], ['ps1'])
            self.cp('dve', s5T[:], pb[:, 0:8 * nb1].rearrange("p (c r) -> p c r", r=nb1), ['ps1'], ['s5T'])
            self.chk('p_s5')
            for l in range(self.nlayers):
                rows_to_cols(I['norm1_g'][l:l + 1, :], 1, D, P['g1T%d' % l][:].unsqueeze(2), 'g1T%d' % l)
                rows_to_cols(I['norm2_g'][l:l + 1, :], 1, D, P['g2T%d' % l][:].unsqueeze(2), 'g2T%d' % l)
                rows_to_cols(I['sgu_norm_g'][l:l + 1, :], 1, 512, P['gsnT%d' % l][:].unsqueeze(2), 'gsnT%d' % l)
                rows_to_cols(I['hy_conv'][l], 4, 1536, P['hcw%d' % l][:], 'hcw%d' % l)
                rows_to_cols(I['ffn_conv'][l], 4, DFF, P['fcw%d' % l][:], 'fcw%d' % l)
                self.chk('p_rows')
                self.dma(P['bsb%d' % l][:].rearrange("p g q -> p (g q)"), I['sgu_b'][l:l + 1, :].partition_broadcast(128), [],
                         ['bsb%d' % l], 'bsb%d' % l)
                self.dma(sgw[:], I['sgu_w'][l].rearrange("g q p -> q g p"), [], ['sgw'], 'sgw')
                pb2 = self.ps[2]
                for g in range(4):
                    self.tr(pb2[:, g * 128:(g + 1) * 128], sgw[:, g, :], P['identf'][:], ['sgw', 'identf'], ['ps2'])
                self.cp('dve', P['wsT%d' % l][:].rearrange("p g q -> p (g q)"), pb2[:, :], ['ps2'], ['wsT%d' % l])
                self.chk('p_sgw')
                self.dma(adb[:], I['ada_b'][l:l + 1, :], [], ['adb'], 'adb')
                for n in range(12):
                    a = adw[n % 2]
                    self.dma(a[:], I['ada_w'][l][:, n * 512:(n + 1) * 512].rearrange("(kc p) c -> p kc c", p=128), [],
                             [('adw', n % 2)], 'adw%d' % (n % 2))
                    pm = self.ps[3 + (n % 2)]
                    pk = 'ps%d' % (3 + (n % 2))
                    for k in range(8):
                        self.mm(pm[0:nb1, :], s5T[:, k, :], a[:, k, :], k == 0, False, ['s5T', ('adw', n % 2)], [pk])
                    self.mm(pm[0:nb1, :], P['onesf'][0:1, 0:nb1], adb[0:1, n * 512:(n + 1) * 512], False, True,
                            ['onesf', 'adb'], [pk])
                    self.cp('dve', stage[0:nb1, n * 512:(n + 1) * 512], pm[0:nb1, :], [pk], ['stage'])
                self.dma(R['modrow'][l], stage[0:nb1, :], ['stage'], [('modrow', l)], 'modrow_st')
                self.chk('p_mod')
                pb = self.ps[5]
                for c in range(48):
                    self.tr(pb[:, c * nb1:(c + 1) * nb1], stage[0:nb1, c * 128:(c + 1) * 128], P['identf'][0:nb1, 0:nb1],
                            ['stage', 'identf'], ['ps5'])
                self.cp('dve', P['modT%d' % l][:], pb[:, 0:48 * nb1].rearrange("p (c r) -> p c r", r=nb1), ['ps5'],
                        ['modT%d' % l])
                self.chk('p_modT')
                for hh in range(2):
                    self.dma(gst[:], I['rpbg'][l][:, hh * 3584:(hh + 1) * 3584], [], ['gst'], 'gst')
                    g3 = gst[:].rearrange("p (a q) -> p a q", q=64)
                    self.tt('dve', g3, g3, m8[:].unsqueeze(1).to_broadcast([128, 56, 64]), ALU.mult, ['gst', 'm8'], ['gst'])
                    for h4 in range(4):
                        self.tt('dve', tblb[:, hh * 4 + h4, 0:14, :], g3[:, h4 * 14:(h4 + 1) * 14, :],
                                mn8[:].unsqueeze(1).to_broadcast([128, 14, 64]), ALU.add, ['gst', 'mn8'], ['tblb'])
                self.cp('dve', tblb[:, :, 14, :], tblb[:, :, 2, :], ['tblb'], ['tblb'])
                self.cp('dve', tblb[:, :, 15, :], tblb[:, :, 10, :], ['tblb'], ['tblb'])
                self.memset('dve', tblb[0:64, :, 14, :], NEG8, ['tblb'], ['tblb'])
                self.memset('dve', tblb[64:128, :, 15, :], NEG8, ['tblb'], ['tblb'])
                self.memset('dve', tblb[:, :, 16, :], NEG8, ['tblb'], ['tblb'])
                for ci, src_slot in enumerate((3, 14, 5, 4, 7, 6, 9, 8, 16, 15)):
                    self.cp('dve', tblb[:, :, 17 + ci, :], tblb[:, :, src_slot, :], ['tblb'], ['tblb'])
                self.dma(R['tbl'][l], tblb[:].rearrange("p h o q -> p (h o q)"), ['tblb'], [('tbl', l)], 'tbl_st')
        self.S.barrier()
        self.chk('params')
        self.S.phase = 'filter'
        with ExitStack() as es:
            st = [self.sb(es, 'cst', [128, 2048], F32) for _ in range(2)]
            sbf = [self.sb(es, 'csb', [128, 2048], BF16) for _ in range(2)]
            cnt = [0]
            engs = ['pool', 'act', 'pool']

            def cast_units(src, dst, KC, N):
                for kc in range(KC):
                    for n0 in range(0, N, 2048):
                        yield (src, dst, kc, n0, min(2048, N - n0))

            def all_units():
                for l in range(self.nlayers if not getattr(self, 'skip_casts', False) else 0):
                    yield from cast_units(I['w_in'][l], R['wA'][l], 8, 8192)
                    for i in range(3):
                        yield from cast_units(I['w_branch'][l, i], R['wB'][l, i], 4, D)
                    yield from cast_units(I['w_out'][l], R['wO'][l], 8, D)
                    yield from cast_units(I['ffn_up'][l], R['wU'][l], 8, 2 * DFF)
                    yield from cast_units(I['ffn_down'][l], R['wD'][l], 22, D)
            gen = all_units()

            def pump(k):
                ph = self.S.phase
                self.S.phase = 'casts'
                for _ in range(k):
                    u = next(gen, None)
                    if u is None:
                        break
                    src, dst, kc, n0, w = u
                    i = cnt[0] % 2
                    eng = engs[cnt[0] % 3]
                    cnt[0] += 1
                    self.dma(st[i][:, 0:w], src[kc * 128:(kc + 1) * 128, n0:n0 + w], [], [('cst', i)], 'cst%d' % i)
                    self.cp(eng, sbf[i][:, 0:w], st[i][:, 0:w], [('cst', i)], [('csb', i)])
                    g0 = n0 // 512
                    ng = w // 512
                    self.dma(dst[g0:g0 + ng, :, kc, :].rearrange("g p c -> p g c"),
                             sbf[i][:, 0:w].rearrange("p (g c) -> p g c", g=ng), [('csb', i)], [('wscr',)], 'csb%d' % i)
                self.S.phase = ph
            self.pump = pump
            for l in range(self.nlayers):
                self.hy_filter(l, 'L')
                if l == 0:
                    self.hy_filter(l, 'C')
            pump(10000)
            self.pump = lambda k: None
        self.S.barrier()
        self.chk('filter')

    def range_reduce_sin(self, u, tmp, keys_u, keys_tmp):
        for _ in range(2):
            self.ts('dve', tmp, u, math.pi, 2 * math.pi, ALU.is_gt, ALU.mult, keys_u, keys_tmp)
            self.tt('dve', u, u, tmp, ALU.subtract, keys_u + keys_tmp, keys_u)
            self.ts('dve', tmp, u, -math.pi, 2 * math.pi, ALU.is_lt, ALU.mult, keys_u, keys_tmp)
            self.tt('dve', u, u, tmp, ALU.add, keys_u + keys_tmp, keys_u)
        self.act(u, u, AF.Sin, keys_u, keys_u)

    def hy_filter(self, l, kind):
        P = self.P
        I = self.I
        R = self.R
        Ls = L if kind == 'L' else LC
        nt = Ls // 128
        TW = min(512, Ls)
        spec = R['spec' + kind][l]
        with ExitStack() as es:
            zT = self.sb(es, 'zT', [33, Ls], F32)
            w1 = self.sb(es, 'w1', [33, 64], F32)
            w2 = self.sb(es, 'w2', [64, 64], F32)
            w3 = self.sb(es, 'w3', [64, 2048], F32)
            p3 = self.sb(es, 'p3', [64, 4], F32)
            fb = self.sb(es, 'fb', [64, 2], F32)
            st3 = self.sb(es, 'st3', [4, 64], F32)
            h1T = self.sb(es, 'h1T', [64, Ls], F32)
            h2T = self.sb(es, 'h2T', [64, Ls], F32)
            tmp = self.sb(es, 'tmp', [64, TW], F32)
            he = self.sb(es, 'he', [128, nt, 512], BF16)
            ho = self.sb(es, 'ho', [128, nt, 512], BF16)
            dec = [self.sb(es, 'dec', [128, 512], F32) for _ in range(2)]
            hd = [self.sb(es, 'hd', [128, 512], F32) for _ in range(2)]
            ab = [self.sb(es, 'ab', [128, 512], F32) for _ in range(2)]
            rn = self.sb(es, 'rn', [128, 512], F32)
            skb = self.sb(es, 'skb', [128, 512], F32)
            cfb = [self.sb(es, 'cfb', [128, nt, 128], BF16) for _ in range(2)]
            sfb = [self.sb(es, 'sfb', [128, nt, 128], BF16) for _ in range(2)]
            ko = [self.sb(es, 'ko', [128, 2, 512], F32) for _ in range(2)]
            self.dma(zT[:], I['zT' + kind], [], [('hsrc', 0)], 'zT')
            self.dma(w1[:], I['hy_w1'][l], [], ['w1'], 'w1')
            self.dma(w2[:], I['hy_w2'][l], [], ['w2'], 'w2')
            self.dma(w3[:], I['hy_w3'][l], [], ['w3'], 'w3')
            self.dma(st3[0:3, :], I['hy_p3'][l], [], ['st3'], 'st3')
            pb = self.ps[0]
            self.tr(pb[0:64, 0:3], st3[0:3, :], P['identf'][0:3, 0:3], ['st3', 'identf'], ['ps0'])
            self.cp('dve', p3[:, 0:3], pb[0:64, 0:3], ['ps0'], ['p3'])
            self.tt('dve', fb[:, 0:1], p3[:, 0:1], p3[:, 2:3], ALU.mult, ['p3'], ['fb'])
            self.tt('dve', fb[:, 1:2], p3[:, 1:2], p3[:, 2:3], ALU.mult, ['p3', 'fb'], ['fb'])
            for li, (wsrc, src, dst, kk) in enumerate(((w1, zT, h1T, 33), (w2, h1T, h2T, 64))):
                for t0 in range(0, Ls, TW):
                    pi = 1 + ((t0 // TW) % 2)
                    pk = 'ps%d' % pi
                    self.mm(self.ps[pi][0:64, 0:TW], wsrc[0:kk, :], src[0:kk, t0:t0 + TW], True, True,
                            ['w%d' % (li + 1), ('hsrc', li)], [pk])
                    self.ts('dve', dst[:, t0:t0 + TW], self.ps[pi][0:64, 0:TW], p3[:, 2:3], fb[:, li:li + 1], ALU.mult, ALU.add,
                            [pk, 'p3', 'fb'], [('hsrc', li + 1)])
                    self.range_reduce_sin(dst[:, t0:t0 + TW], tmp[:, 0:TW], [('hsrc', li + 1)], ['rrtmp'])
            for o in range(2):
                pn = self.ps[3]
                first = True
                for i in range(nt):
                    self.pump(2)
                    self.dma(dec[i % 2][:], I['decay' + kind][:, i, :], [], [('dec', i % 2)], 'dec%d' % (i % 2))
                    for dr in range(2):
                        n = o * 2 + dr
                        pi = 4 + dr
                        pk = 'ps%d' % pi
                        self.mm(self.ps[pi][:, :], h2T[:, i * 128:(i + 1) * 128], w3[:, n * 512:(n + 1) * 512], True, True,
                                [('hsrc', 2), 'w3'], [pk])
                        self.tt('dve', hd[dr][:], self.ps[pi][:, :], dec[i % 2][:], ALU.mult, [pk, ('dec', i % 2)], [('hd', dr)])
                        if dr == 1 and i == 0:
                            self.memset('dve', hd[dr][0:1, :], 0.0, [('hd', dr)], [('hd', dr)])
                        self.act(ab[dr][:], hd[dr][:], AF.Abs, [('hd', dr)], [('ab', dr)])
                        last = (i == nt - 1 and dr == 1)
                        self.mm(pn[:, :], P['onesf'][:], ab[dr][:], first, last, ['onesf', ('ab', dr)], ['ps3'])
                        first = False
                    self.tt('dve', he[:, i, :], hd[0][:], hd[1][:], ALU.add, [('hd', 0), ('hd', 1)], [('he', i)])
                    self.tt('pool', ho[:, i, :], hd[0][:], hd[1][:], ALU.subtract, [('hd', 0), ('hd', 1)], [('ho', i)])
                self.recip(rn[:], pn[:, :], ['ps3'], ['rn'])
                self.ts('dve', rn[:], rn[:], 1.0 / Ls, None, ALU.mult, None, ['rn'], ['rn'])
                self.dma(skb[:], I['hy_skip'][l, o:o + 1, :].partition_broadcast(128), [], ['skb'], 'skb')
                self.ts('dve', skb[:], skb[:], 1.0 / Ls, None, ALU.mult, None, ['skb'], ['skb'])
                hek = [('he', i) for i in range(nt)]
                hok = [('ho', i) for i in range(nt)]
                for m in range(nt):
                    self.pump(2)
                    s = m % 2
                    self.dma(cfb[s][:], I['cf' + kind][m], [], [('cfb', s)], 'cfb%d' % s)
                    self.dma(sfb[s][:], I['sf' + kind][m], [], [('sfb', s)], 'sfb%d' % s)
                    pr, ps_ = self.ps[6], self.ps[7]
                    for tt_ in range(nt):
                        self.mm(pr[:, :], cfb[s][:, tt_, :], he[:, tt_, :], tt_ == 0, tt_ == nt - 1, [('cfb', s)] + hek, ['ps6'])
                    for tt_ in range(nt):
                        self.mm(ps_[:, :], sfb[s][:, tt_, :], ho[:, tt_, :], tt_ == 0, tt_ == nt - 1, [('sfb', s)] + hok, ['ps7'])
                    self.tt('dve', ko[s][:, 0, :], pr[:, :], rn[:], ALU.mult, ['ps6', 'rn'], [('ko', s)])
                    self.tt('dve', ko[s][:, 0, :], ko[s][:, 0, :], skb[:], ALU.add, [('ko', s), 'skb'], [('ko', s)])
                    self.tt('dve', ko[s][:, 1, :], ps_[:, :], rn[:], ALU.mult, ['ps7', 'rn', ('ko', s)], [('ko', s)])
                    self.dma(spec[o, m], ko[s][:].rearrange("p a c -> p (a c)"), [('ko', s)], [('spec', kind, l)], 'ko%d' % s)
        self.S.barrier()

    def norm_tile(self, es_bufs, x_sb, xk, A, B, Akeys, hT, hkey, i):
        P = self.P
        junks, xns, _unused, pbi = es_bufs
        par = i % 2
        junk, xn = junks[par], xns[par]
        ss = P['ss'][:, 4 * par:4 * par + 4]
        sk = ('ss', par)
        pb = self.ps[pbi]
        pk = 'ps%d' % pbi
        self.memset('pool', ss[:, 0:1], 0.0, [sk])
        self.act(junk[:], x_sb, AF.Square, [xk, sk], [('njunk', par), sk], accum=ss[:, 0:1])
        self.ts('dve', ss[:, 1:2], ss[:, 0:1], 1.0 / D, EPS, ALU.mult, ALU.add, [sk], [sk])
        self.act(ss[:, 2:3], ss[:, 1:2], AF.Sqrt, [sk], [sk])
        self.recip(ss[:, 3:4], ss[:, 2:3], [sk], [sk])
        self.act(xn[:], x_sb, AF.Copy, [xk, sk], [('nxn', par)], scale=ss[:, 3:4])
        pbb = pb[:].bitcast(BF16)
        for c in range(8):
            self.tr(pbb[:, c * 128:(c + 1) * 128], xn[:, c * 128:(c + 1) * 128], P['identb'][:], [('nxn', par), 'identb'], [pk])
        for c in range(8):
            self.ts('dve', hT[:, c, 1 + 128 * i:1 + 128 * (i + 1)], pbb[:, c * 128:(c + 1) * 128], A[:, c:c + 1], B[:, c:c + 1],
                    ALU.mult, ALU.add, [pk] + Akeys, [(hkey, i, c)])

    def seq_pass(self, b, l, kind):
        P = self.P
        I = self.I
        R = self.R
        NB = self.NB
        isL = (kind == 'L')
        Ls = L if isL else LC
        nt = Ls // 128
        TW = min(512, Ls)
        nT = Ls // TW
        spt = TW // 128
        last_layer = (l == self.nlayers - 1)
        full = isL or (not last_layer)
        bcol = b if isL else NB
        hT = P['hT'] if isL else P['hTc']
        hkey = 'hT' if isL else 'hTc'
        modT = P['modT%d' % l]
        wA = R['wA'][l]
        if isL:
            xsrc = I['x'][b] if l == 0 else R['xs'][b]
            xsk = ('xin',) if l == 0 else ('xs', b)
            xdst = R['xs'][b]
            xdk = ('xs', b)
        else:
            xsrc = I['ctx'][b] if l == 0 else R['xcs'][b]
            xsk = ('xin',) if l == 0 else ('xcs', b)
            xdst = R['xcs'][b]
            xdk = ('xcs', b)

        def hkeys(T, halo=False, k=0):
            ks = [(hkey, T * spt + s, k) for s in range(spt)]
            if halo:
                if T * spt - 1 >= 0:
                    ks.append((hkey, T * spt - 1, k))
                if (T + 1) * spt < nt:
                    ks.append((hkey, (T + 1) * spt, k))
            return ks

        def hcols(T):
            return slice(1 + T * TW, 1 + (T + 1) * TW)

        def halo_ap(kc, T):
            base = hT[:, kc, T * TW:T * TW + 1]
            return bass.AP(base.tensor, base.offset, [list(base.ap[0]), [TW + 1, 2]])

        with ExitStack() as esP:
            AB = self.sb(esP, 'AB', [128, 4, 8], F32)
            PF = [self.sb(esP, 'PF', [128, 8, 512], BF16) for _ in range(2)]
            self.PF = PF

            def pf_fill(i, src):
                self.dma(PF[i][:], src, [('wscr',)], [('PF', i)], 'PF%d' % i)
            self.pf_fill = pf_fill
            A1 = AB[:, 0, :]
            A2 = AB[:, 1, :]
            B1 = modT[:, 0:8, bcol]
            B2 = modT[:, 24:32, bcol]
            self.stt('dve', A1, modT[:, 8:16, bcol], 1.0, P['g1T%d' % l][:], ALU.add, ALU.mult, ['modT%d' % l, 'g1T%d' % l], ['AB'])
            self.stt('dve', A2, modT[:, 32:40, bcol], 1.0, P['g2T%d' % l][:], ALU.add, ALU.mult, ['modT%d' % l, 'g2T%d' % l, 'AB'],
                     ['AB'])
            ABk = ['AB', 'modT%d' % l]
            self.S.phase = kind + '_N1'
            with ExitStack() as es:
                xt = [self.sb(es, 'xt', [128, D], F32) for _ in range(3)]
                junk = [self.sb(es, 'junk', [128, D], BF16) for _ in range(2)]
                xn = [self.sb(es, 'xn', [128, D], BF16) for _ in range(2)]
                for i in range(nt):
                    s = i % 3
                    self.dma(xt[s][:], xsrc[i * 128:(i + 1) * 128, :], [xsk + (i,)], [('xt', s)], 'xt%d' % s)
                    self.norm_tile((junk, xn, None, i % 2), xt[s][:], ('xt', s), A1, B1, ABk, hT, hkey, i)
                if isL and b == 0 and l == 0:
                    self.dump('hT', hT[:], [(hkey, i, c) for i in range(nt) for c in range(8)], [128, 8, L + 2], BF16)
                if isL:
                    pf_fill(0, wA[2])
                    pf_fill(1, wA[14])
                elif full:
                    pf_fill(0, wA[2])
                    pf_fill(1, wA[3])
                else:
                    pf_fill(0, wA[3])
                    pf_fill(1, wA[4])
                if full:
                    self.S.phase = kind + '_sgu'
                    self.phase_sgu(b, l, kind, hT, hkeys, hcols)
            self.S.barrier()
            if isL:
                self.chk('sgu')
            self.S.phase = kind + '_attn'
            self.phase_attn(b, l, kind, hT, hkeys, hcols, full)
            self.S.barrier()
            if isL:
                self.chk('attn')
            if not full:
                return
            with ExitStack() as esH:
                hyT = self.sb(esH, 'hyT', [128, 4, Ls], BF16)
                self.S.phase = kind + '_hyena'
                self.phase_hyena(b, l, kind, hT, hkeys, hcols, halo_ap, hyT)
                self.S.barrier()
                if isL:
                    self.chk('hyena')
                if isL and b == 0 and l == 0:
                    self.dump('hyT', hyT[:], [], [128, 4, L], BF16)
                mT = self.sb(esH, 'mT', [128, 8, Ls], BF16)
                self.S.phase = kind + '_merge1'
                self.phase_merge1(b, l, kind, hT, hkeys, hcols, hyT, mT)
                self.S.barrier()
                if isL:
                    self.chk('merge1')
                if isL and b == 0 and l == 0:
                    self.dump('mT', mT[:], [], [128, 8, L], BF16)
                self.S.phase = kind + '_merge2'
                self.phase_merge2(b, l, kind, hT, hkey, mT, xsrc, xsk, xdst, xdk, A2, B2, ABk, bcol)
                self.S.barrier()
                if isL:
                    self.chk('merge2')
            if isL and b == 0 and l == 0:
                self.dump('h2T', hT[:], [], [128, 8, L + 2], BF16)
            self.S.phase = kind + '_ffn'
            self.phase_ffn(b, l, kind, hT, hkeys, hcols, halo_ap, xdst, xdk, bcol, last_layer and isL)
            self.S.barrier()

    def phase_sgu(self, b, l, kind, hT, hkeys, hcols):
        P = self.P
        R = self.R
        isL = kind == 'L'
        Ls = L if isL else LC
        nt = Ls // 128
        TW = min(512, Ls)
        nT = Ls // TW
        spt = TW // 128
        wA = R['wA'][l]
        with ExitStack() as es:
            aT = self.sb(es, 'aT', [128, 4, Ls], BF16)
            wu = self.sb(es, 'wu', [128, 8, 512], BF16)
            wv = self.sb(es, 'wv', [128, 8, 512], BF16)
            vg = [self.sb(es, 'vg', [128, 512], F32) for _ in range(2)]
            vjs = [self.sb(es, 'vj', [128, 512], BF16) for _ in range(2)]
            vs = [self.sb(es, 'vs', [128, 512], BF16) for _ in range(2)]
            t1 = [self.sb(es, 't1', [128, 4, 128], F32) for _ in range(2)]
            svs = self.sb(es, 'sv', [128, 8], F32)
            self.dma(wu[:], wA[0], [('wscr',)], ['wu'], 'wu')
            self.dma(wv[:], wA[1], [('wscr',)], ['wv'], 'wv')
            cnt = 0
            for T in range(nT):
                for fc in range(4):
                    pi = cnt % 3
                    cnt += 1
                    pk = 'ps%d' % pi
                    for k in range(8):
                        self.mm(self.ps[pi][:, 0:TW], wu[:, k, fc * 128:(fc + 1) * 128], hT[:, k, hcols(T)], k == 0, k == 7,
                                ['wu'] + hkeys(T, k=k), [pk])
                    self.act(aT[:, fc, T * TW:(T + 1) * TW], self.ps[pi][:, 0:TW], AF.Gelu, [pk], [('aT', T, fc)])
                for s in range(spt):
                    i = T * spt + s
                    pi = 3 + (i % 2)
                    pk = 'ps%d' % pi
                    q = i % 2
                    for k in range(8):
                        self.mm(self.ps[pi][:, :], hT[:, k, 1 + i * 128:1 + (i + 1) * 128], wv[:, k, :], k == 0, k == 7,
                                ['wv', (hkeys(T, k=k)[s])], [pk])
                    self.act(vg[q][:], self.ps[pi][:, :], AF.Gelu, [pk], [('vg', q)])
                    sv = svs[:, 4 * q:4 * q + 4]
                    svk = ('sv', q)
                    vj = vjs[q]
                    self.memset('pool', sv[:, 0:1], 0.0, [svk])
                    self.act(vj[:], vg[q][:], AF.Square, [('vg', q), svk], [('vj', q), svk], accum=sv[:, 0:1])
                    self.ts('dve', sv[:, 1:2], sv[:, 0:1], 1.0 / 512, EPS, ALU.mult, ALU.add, [svk], [svk])
                    self.act(sv[:, 2:3], sv[:, 1:2], AF.Sqrt, [svk], [svk])
                    self.recip(sv[:, 3:4], sv[:, 2:3], [svk], [svk])
                    self.ts('dve', vs[q][:], vg[q][:], sv[:, 3:4], None, ALU.mult, None, [('vg', q), svk], [('vs', q)])
                    pj = 5 + (i % 2)
                    pjk = 'ps%d' % pj
                    for g in range(4):
                        self.mm(self.ps[pj][:, g * 128:(g + 1) * 128], vs[q][:, g * 128:(g + 1) * 128], P['wsT%d' % l][:, g, :],
                                True, True, [('vs', q), 'wsT%d' % l], [pjk])
                    self.tt('dve', t1[q][:], self.ps[pj][:, :].rearrange("p (g q) -> p g q", g=4),
                            P['gsnT%d' % l][:].unsqueeze(2).to_broadcast([128, 4, 128]), ALU.mult, [pjk, 'gsnT%d' % l], [('t1', q)])
                    self.tt('pool', t1[q][:], t1[q][:], P['bsb%d' % l][:], ALU.add, [('t1', q), 'bsb%d' % l], [('t1', q)])
                    akeys = [('aT', T, fc) for fc in range(4)]
                    self.tt('dve', aT[:, :, i * 128:(i + 1) * 128], aT[:, :, i * 128:(i + 1) * 128], t1[q][:], ALU.mult,
                            [('t1', q)] + akeys, [('aT2', i)])
            allk = [('aT', T, fc) for T in range(nT) for fc in range(4)] + [('aT2', i) for i in range(nt)]
            self.dma(R['sa' + kind], aT[:], allk, [('sa', kind)], 'sa_st')
            if isL and b == 0 and l == 0:
                self.dump('aT', aT[:], allk, [128, 4, L], BF16)

    def phase_attn(self, b, l, kind, hT, hkeys, hcols, full):
        P = self.P
        I = self.I
        R = self.R
        isL = kind == 'L'
        Ls = L if isL else LC
        nt = Ls // 128
        TW = min(512, Ls)
        nT = Ls // TW
        spt = TW // 128
        wA = R['wA'][l]
        with ExitStack() as es:
            es1 = es.enter_context(ExitStack())
            if isL:
                QT = self.sb(es, 'QTz', [128, NH, Ls], BF16)
                KT = self.sb(es, 'KT', [128, 4, Ls], BF16)
                V1 = self.sb(es, 'V1', [128, nt, NH, 65], BF16)
                tbl = self.sb(es, 'tbl', [128, NH, 27, 64], BF16)
                cosT = self.sb(es1, 'cosT', [128, L], F32)
                sinT = self.sb(es1, 'sinT', [128, L], F32)
                self.dma(cosT[:], I['cosT'], [], ['cosT'], 'cosT')
                self.dma(sinT[:], I['sinT'], [], ['sinT'], 'sinT')
                self.dma(tbl[:].rearrange("p h o q -> p (h o q)"), R['tbl'][l], [('tbl', l)], ['tbl'], 'tbl')
                self.memset('pool', V1[:], 1.0, ['V1'])
                self.memset('dve', QT[:], 0.0, ['QTz0'])
            else:
                KT = P['KcT']
                V1 = P['Vc1']
                if full:
                    QT = self.sb(es, 'QTc', [128, 4, Ls], BF16)
            wq = [self.sb(es1, 'wq', [128, 8, 512], BF16) for _ in range(2)]
            r1 = [self.sb(es1, 'r1', [128, 512], F32) for _ in range(2)]
            r2 = [self.sb(es1, 'r2', [128, 512], F32) for _ in range(2)]
            wcnt = [0]

            pfn = [0]

            def wload(g):
                if pfn[0] < 2:
                    i_ = pfn[0]
                    pfn[0] += 1
                    return self.PF[i_], ('PF', i_)
                s = wcnt[0] % 2
                wcnt[0] += 1
                self.dma(wq[s][:], wA[g], [('wscr',)], [('wq', s)], 'wq%d' % s)
                return wq[s], ('wq', s)

            pcnt = [0]

            def proj_fm(wb, wk, T, fc, pi):
                pk = 'ps%d' % pi
                for k in range(8):
                    self.mm(self.ps[pi][:, 0:TW], wb[:, k, fc * 128:(fc + 1) * 128], hT[:, k, hcols(T)], k == 0, k == 7,
                            [wk] + hkeys(T, k=k), [pk])
                return self.ps[pi][:, 0:TW], pk

            kname = 'KT' if isL else 'KcT'
            vname = 'V1' if isL else 'Vc1'
            for name, g, gsw, dstT in (('q', 2, 14, 'QT'), ('k', 3, 15, kname)):
                if name == 'q' and not full:
                    continue
                dst = QT if name == 'q' else KT
                wb, wk = wload(g)
                if isL:
                    wb2, wk2 = wload(gsw)
                for T in range(nT):
                    for fc in range(4):
                        pa, pak = proj_fm(wb, wk, T, fc, (pcnt[0] % 2) * 2)
                        if isL:
                            pb_, pbk = proj_fm(wb2, wk2, T, fc, (pcnt[0] % 2) * 2 + 1)
                            q = pcnt[0] % 2
                            self.tt('dve', r1[q][:], pa, cosT[:, T * TW:(T + 1) * TW], ALU.mult, [pak, 'cosT'], [('r1', q)])
                            self.tt('dve', r2[q][:], pb_, sinT[:, T * TW:(T + 1) * TW], ALU.mult, [pbk, 'sinT'], [('r2', q)])
                            if name == 'q':
                                for hp in range(2):
                                    self.tt('pool', dst[hp * 64:(hp + 1) * 64, 2 * fc + hp, T * TW:(T + 1) * TW],
                                            r1[q][hp * 64:(hp + 1) * 64, :], r2[q][hp * 64:(hp + 1) * 64, :], ALU.add,
                                            [('r1', q), ('r2', q), 'QTz0'], [(dstT, T, fc, hp)])
                            else:
                                self.tt('pool', dst[:, fc, T * TW:(T + 1) * TW], r1[q][:], r2[q][:], ALU.add, [('r1', q), ('r2', q)],
                                        [(dstT, T, fc)])
                        else:
                            self.cp('act', dst[:, fc, T * TW:(T + 1) * TW], pa, [pak], [(dstT, T, fc)])
                        pcnt[0] += 1
            wb, wk = wload(4)
            for i in range(nt):
                pi = 4 + (i % 2)
                pk = 'ps%d' % pi
                for k in range(8):
                    self.mm(self.ps[pi][:, :], hT[:, k, 1 + i * 128:1 + (i + 1) * 128], wb[:, k, :], k == 0, k == 7,
                            [wk, hkeys(i // spt, k=k)[i % spt]], [pk])
                self.cp('act', V1[:, i, :, 0:64], self.ps[pi][:, :].rearrange("p (h d) -> p h d", h=NH), [pk, vname], [(vname, i)])
            if not full:
                return
            if isL:
                QTk = [('QT', T, fc, hp) for T in range(nT) for fc in range(4) for hp in range(2)]
            else:
                QTk = [('QT', T, fc) for T in range(nT) for fc in range(4)]
            KTk = [(kname, T, fc) for T in range(nT) for fc in range(4)]
            if isL and b == 0 and l == 0:
                self.dump('QT', QT[:], QTk, [128, NH, L], BF16)
                self.dump('KT', KT[:], KTk, [128, 4, L], BF16)
            self.S.barrier()
            es1.close()
            self.pf_fill(0, wA[5])
            self.pf_fill(1, wA[6])
            self.S.phase = kind + '_attn2'
            attnT = self.sb(es, 'attnT', [128, 4, Ls], BF16)
            Pb = [self.sb(es, 'Pb', [128, 896], BF16) for _ in range(3)]
            atok = [self.sb(es, 'atok', [128, 512], BF16) for _ in range(2)]
            rc = [self.sb(es, 'rc', [128, 8], F32) for _ in range(2)]
            KcT = P['KcT']
            Vc1 = P['Vc1']
            nqb = Ls // 128
            sc = 0
            pc_ = 0
            for pr in range(nqb):
                r = 2 * pr
                tiles = []
                interior = False
                if isL:
                    if 4 <= r <= 26:
                        interior = True
                        for t in range(5):
                            tiles.append(('loc', (r - 4) // 2 + t, None))
                    else:
                        j0 = 0 if r < 4 else 12
                        for t in range(4):
                            j = j0 + t
                            tiles.append(('loc', j, (2 * j - r + 7, 2 * j - (r + 1) + 7)))
                tiles.append(('ctx', 0, None))
                tiles.append(('ctx', 1, None))
                ntl = len(tiles)
                rp = pr % 2
                OA, OB = self.ps[4 + 2 * rp], self.ps[5 + 2 * rp]
                OAk, OBk = 'ps%d' % (4 + 2 * rp), 'ps%d' % (5 + 2 * rp)
                for h in range(NH):
                    c = h // 2
                    p0 = (h % 2) * 64
                    st_ = sc % 2
                    sc += 1
                    SB_ = (self.ps[2 * st_], self.ps[2 * st_ + 1])
                    SK_ = ('ps%d' % (2 * st_), 'ps%d' % (2 * st_ + 1))
                    Tq = (r * 64) // TW
                    if isL:
                        qap = QT[:, h, r * 64:r * 64 + 128]
                        qk = [('QT', Tq, c, h % 2)]
                        ksl = slice(0, 128)
                    else:
                        qap = QT[p0:p0 + 64, c, r * 64:r * 64 + 128]
                        qk = [('QT', Tq, c)]
                        ksl = slice(p0, p0 + 64)
                    if interior:
                        self.mm(SB_[0][:, 0:512], P['identb'][:], tbl[:, h, 17:25, :].rearrange("p a q -> p (a q)"), True, False,
                                ['identb', 'tbl'], [SK_[0]])
                        self.mm(SB_[1][:, 0:128], P['identb'][:], tbl[:, h, 25:27, :].rearrange("p a q -> p (a q)"), True, False,
                                ['identb', 'tbl'], [SK_[1]])
                    for ti, (ty, j, bspec) in enumerate(tiles):
                        Sb, Sk = SB_[ti // 4], SK_[ti // 4]
                        co = (ti % 4) * 128
                        if ty == 'loc':
                            kap = KT[ksl, c, j * 128:(j + 1) * 128]
                            kk = [('KT', (j * 128) // TW, c)]
                            if interior:
                                self.mm(Sb[:, co:co + 128], kap, qap, False, (ti == 3 or ti == 4), kk + qk, [Sk])
                            else:
                                self.mm(Sb[:, co:co + 128], kap, qap, True, False, kk + qk, [Sk])
                                for a_ in range(2):
                                    self.mm(Sb[:, co + a_ * 64:co + (a_ + 1) * 64], P['identb'][:], tbl[:, h, bspec[a_], :], False, a_ == 1,
                                            ['identb', 'tbl'], [Sk])
                        else:
                            kap = KcT[ksl, c, j * 128:(j + 1) * 128]
                            kk = [('KcT', 0, c)]
                            self.mm(Sb[:, co:co + 128], kap, qap, True, True, kk + qk, [Sk])
                    pq = pc_ % 3
                    pc_ += 1
                    na = min(ntl, 4) * 128
                    self.act(Pb[pq][:, 0:na], SB_[0][:, 0:na], AF.Exp, [SK_[0]], [('Pb', pq, 0)], scale=0.125)
                    pkeys = [('Pb', pq, 0)]
                    if ntl > 4:
                        nb_ = (ntl - 4) * 128
                        self.act(Pb[pq][:, 512:512 + nb_], SB_[1][:, 0:nb_], AF.Exp, [SK_[1]], [('Pb', pq, 1)], scale=0.125)
                        pkeys.append(('Pb', pq, 1))
                    O = OA if h < 4 else OB
                    Ok = OAk if h < 4 else OBk
                    hh = h % 4
                    for ti, (ty, j, bspec) in enumerate(tiles):
                        if ty == 'loc':
                            vap = V1[:, j, h, :]
                            vk = [('V1', j)]
                        else:
                            vap = Vc1[:, j, h, :]
                            vk = [('Vc1', j)]
                        self.mm(O[:, hh * 65:(hh + 1) * 65], Pb[pq][:, ti * 128:(ti + 1) * 128], vap, ti == 0, ti == ntl - 1,
                                [pkeys[ti // 4]] + vk, [Ok])
                q = pr % 2
                for half, (O, Ok) in enumerate(((OA, OAk), (OB, OBk))):
                    o3 = O[:, 0:260].rearrange("p (h e) -> p h e", h=4)
                    self.recip(rc[q][:, half * 4:(half + 1) * 4].unsqueeze(2), o3[:, :, 64:65], [Ok], [('rc', q, half)])
                    self.tt('dve', atok[q][:, half * 256:(half + 1) * 256].rearrange("p (h d) -> p h d", h=4), o3[:, :, 0:64],
                            rc[q][:, half * 4:(half + 1) * 4].unsqueeze(2).to_broadcast([128, 4, 64]), ALU.mult,
                            [Ok, ('rc', q, half)], [('atok', q, half)])
                st_ = sc % 2
                sc += 1
                Tb = self.ps[2 * st_][:].bitcast(BF16)
                Tk = 'ps%d' % (2 * st_)
                for c in range(4):
                    self.tr(Tb[:, c * 128:(c + 1) * 128], atok[q][:, c * 128:(c + 1) * 128], P['identb'][:],
                            [('atok', q, 0), ('atok', q, 1), 'identb'], [Tk])
                self.cp('act', attnT[:, :, pr * 128:(pr + 1) * 128], Tb[:, 0:512].rearrange("p (c t) -> p c t", c=4), [Tk],
                        [('attnT', pr)])
            allk = [('attnT', r) for r in range(nqb)]
            self.dma(R['satt' + kind], attnT[:], allk, [('satt', kind)], 'satt_st')
            if isL and b == 0 and l == 0:
                self.dump('attnT', attnT[:], allk, [128, 4, L], BF16)

    def phase_hyena(self, b, l, kind, hT, hkeys, hcols, halo_ap, hyT):
        P = self.P
        I = self.I
        R = self.R
        isL = kind == 'L'
        Ls = L if isL else LC
        nt = Ls // 128
        TW = min(512, Ls)
        nT = Ls // TW
        spt = TW // 128
        wA = R['wA'][l]
        hcw = P['hcw%d' % l]
        spec = R['spec' + kind][l]
        with ExitStack() as es:
            ztok = self.sb(es, 'ztok', [128, nt, 512], BF16)
            x1tok = self.sb(es, 'x1tok', [128, nt, 512], BF16)
            Yr = self.sb(es, 'Yr', [128, nt, 512], BF16)
            Ys = self.sb(es, 'Ys', [128, nt, 512], BF16)
            cb = [self.sb(es, 'cb', [128, nt, 128], BF16) for _ in range(2)]
            sbf_ = [self.sb(es, 'sbf', [128, nt, 128], BF16) for _ in range(2)]
            kk_ = [self.sb(es, 'kk', [128, 2, 512], F32) for _ in range(2)]
            with ExitStack() as es2:
                wh = [self.sb(es2, 'wh', [128, 8, 512], BF16) for _ in range(2)]
                pbuf = [self.sb(es2, 'pbuf', [128, 516], F32) for _ in range(2)]
                cacc = [self.sb(es2, 'cacc', [128, 512], F32) for _ in range(2)]
                cfm = [self.sb(es2, 'cfm', [128, 512], BF16) for _ in range(2)]
                cnt = 0
                for gi, g in enumerate((5, 6, 7)):
                    s = gi % 2
                    if gi < 2:
                        whb, whk = self.PF[gi], ('PF', gi)
                    else:
                        self.dma(wh[0][:], wA[g], [('wscr',)], [('wh', 0)], 'wh0')
                        whb, whk = wh[0], ('wh', 0)
                    for T in range(nT):
                        for fc in range(4):
                            ch = gi * 4 + fc
                            q = cnt % 2
                            cnt += 1
                            pi = q * 2
                            pk, pk2 = 'ps%d' % pi, 'ps%d' % (pi + 1)
                            for k in range(8):
                                self.mm(self.ps[pi][:, 0:TW], whb[:, k, fc * 128:(fc + 1) * 128], hT[:, k, hcols(T)], k == 0, k == 7,
                                        [whk] + hkeys(T, k=k), [pk])
                            for k in range(8):
                                self.mm(self.ps[pi + 1][:, 0:2], whb[:, k, fc * 128:(fc + 1) * 128], halo_ap(k, T), k == 0, k == 7,
                                        [whk] + hkeys(T, True, k), [pk2])
                            self.cp('act', pbuf[q][:, 1:1 + TW], self.ps[pi][:, 0:TW], [pk], [('pbuf', q)])
                            hdst = bass.AP(pbuf[q][:, 0:1].tensor, pbuf[q][:, 0:1].offset,
                                           [list(pbuf[q][:, 0:1].ap[0]), [TW + 1, 2]])
                            self.cp('act', hdst, self.ps[pi + 1][:, 0:2], [pk2, ('pbuf', q)], [('pbuf', q)])
                            self.ts('dve', cacc[q][:, 0:TW], pbuf[q][:, 1:1 + TW], hcw[:, ch, 1:2], hcw[:, ch, 3:4], ALU.mult, ALU.add,
                                    [('pbuf', q), 'hcw%d' % l], [('cacc', q)])
                            self.stt('dve', cacc[q][:, 0:TW], pbuf[q][:, 0:TW], hcw[:, ch, 0:1], cacc[q][:, 0:TW], ALU.mult, ALU.add,
                                     [('pbuf', q), 'hcw%d' % l, ('cacc', q)], [('cacc', q)])
                            if gi == 2:
                                self.stt('dve', hyT[:, fc, T * TW:(T + 1) * TW], pbuf[q][:, 2:2 + TW], hcw[:, ch, 2:3], cacc[q][:, 0:TW],
                                         ALU.mult, ALU.add, [('pbuf', q), 'hcw%d' % l, ('cacc', q)], [('x2T', T, fc)])
                            else:
                                self.stt('dve', cfm[q][:, 0:TW], pbuf[q][:, 2:2 + TW], hcw[:, ch, 2:3], cacc[q][:, 0:TW],
                                         ALU.mult, ALU.add, [('pbuf', q), 'hcw%d' % l, ('cacc', q)], [('cfm', q)])
                                pt = 4 + q
                                ptk = 'ps%d' % pt
                                Tb = self.ps[pt][:].bitcast(BF16)
                                for s4 in range(spt):
                                    self.tr(Tb[:, s4 * 128:(s4 + 1) * 128], cfm[q][:, s4 * 128:(s4 + 1) * 128], P['identb'][:],
                                            [('cfm', q), 'identb'], [ptk])
                                dstt = ztok if gi == 0 else x1tok
                                dk = 'ztok' if gi == 0 else 'x1tok'
                                self.cp('act', dstt[:, T * spt:(T + 1) * spt, fc * 128:(fc + 1) * 128],
                                        Tb[:, 0:spt * 128].rearrange("p (s c) -> p s c", s=spt), [ptk],
                                        [(dk, T * spt + s4, fc) for s4 in range(spt)])
            self.dma(cb[0][:], I['cf' + kind][0], [], [('cb', 0)], 'cb0')
            self.dma(sbf_[0][:], I['sf' + kind][0], [], [('sbf', 0)], 'sbf0')
            self.dma(kk_[0][:].rearrange("p a c -> p (a c)"), spec[0, 0], [('spec', kind, l)], [('kk', 0)], 'kk0')
            self.S.barrier()
            self.pf_fill(0, wA[8])
            self.pf_fill(1, wA[10])
            self.S.phase = kind + '_hyena2'
            zk = [('ztok', i, fc) for i in range(nt) for fc in range(4)]
            x1k = [('x1tok', i, fc) for i in range(nt) for fc in range(4)]
            if isL and b == 0 and l == 0:
                self.dump('ztok', ztok[:], zk, [128, nt, 512], BF16)
                self.dump('x1tok', x1tok[:], x1k, [128, nt, 512], BF16)
                self.dump('x2T', hyT[:], [('x2T', T, fc) for T in range(nT) for fc in range(4)], [128, 4, L], BF16)
            with ExitStack() as es3:
                zs = [self.sb(es3, 'zs', [128, 512], F32) for _ in range(2)]
                ta = [self.sb(es3, 'ta', [128, 512], F32) for _ in range(2)]
                tb_ = [self.sb(es3, 'tb', [128, 512], F32) for _ in range(2)]
                tc_ = [self.sb(es3, 'tc', [128, 512], F32) for _ in range(2)]
                td = [self.sb(es3, 'td', [128, 512], F32) for _ in range(2)]
                ytok = [self.sb(es3, 'ytok', [128, 512], BF16) for _ in range(2)]
                cntl = 0
                for o in range(2):
                    zin_keys = zk if o == 0 else [('z1', i) for i in range(nt)]
                    Yk = []
                    for m in range(nt):
                        s = cntl % 2
                        cntl += 1
                        if not (o == 0 and m == 0):
                            self.dma(cb[s][:], I['cf' + kind][m], [], [('cb', s)], 'cb%d' % s)
                            self.dma(sbf_[s][:], I['sf' + kind][m], [], [('sbf', s)], 'sbf%d' % s)
                            self.dma(kk_[s][:].rearrange("p a c -> p (a c)"), spec[o, m], [('spec', kind, l)], [('kk', s)], 'kk%d' % s)
                        pr, psn = self.ps[s * 2], self.ps[s * 2 + 1]
                        prk, psk = 'ps%d' % (s * 2), 'ps%d' % (s * 2 + 1)
                        for t_ in range(nt):
                            self.mm(pr[:, :], cb[s][:, t_, :], ztok[:, t_, :], t_ == 0, t_ == nt - 1, [('cb', s)] + zin_keys, [prk])
                        for t_ in range(nt):
                            self.mm(psn[:, :], sbf_[s][:, t_, :], ztok[:, t_, :], t_ == 0, t_ == nt - 1, [('sbf', s)] + zin_keys, [psk])
                        self.cp('act', zs[s][:], psn[:, :], [psk], [('zs', s)])
                        self.tt('dve', ta[s][:], pr[:, :], kk_[s][:, 0, :], ALU.mult, [prk, ('kk', s)], [('ta', s)])
                        self.tt('pool', tb_[s][:], zs[s][:], kk_[s][:, 1, :], ALU.mult, [('zs', s), ('kk', s)], [('tb', s)])
                        self.tt('dve', Yr[:, m, :], ta[s][:], tb_[s][:], ALU.subtract, [('ta', s), ('tb', s)], [('Yr', o, m)])
                        self.tt('dve', tc_[s][:], pr[:, :], kk_[s][:, 1, :], ALU.mult, [prk, ('kk', s)], [('tc', s)])
                        self.tt('pool', td[s][:], zs[s][:], kk_[s][:, 0, :], ALU.mult, [('zs', s), ('kk', s)], [('td', s)])
                        self.tt('pool', Ys[:, m, :], tc_[s][:], td[s][:], ALU.add, [('tc', s), ('td', s)], [('Ys', o, m)])
                        Yk.append(('Yr', o, m))
                        Yk.append(('Ys', o, m))
                    for m in range(nt):
                        s = cntl % 2
                        cntl += 1
                        self.dma(cb[s][:], I['ci' + kind][m], [], [('cb', s)], 'cb%d' % s)
                        self.dma(sbf_[s][:], I['si' + kind][m], [], [('sbf', s)], 'sbf%d' % s)
                        py = self.ps[4 + s]
                        pyk = 'ps%d' % (4 + s)
                        for t_ in range(nt):
                            self.mm(py[:, :], cb[s][:, t_, :], Yr[:, t_, :], t_ == 0, False, [('cb', s)] + Yk, [pyk])
                        for t_ in range(nt):
                            self.mm(py[:, :], sbf_[s][:, t_, :], Ys[:, t_, :], False, t_ == nt - 1, [('sbf', s)] + Yk, [pyk])
                        if o == 0:
                            self.tt('dve', ztok[:, m, :], py[:, :], x1tok[:, m, :], ALU.mult, [pyk] + x1k + zk, [('z1', m)] + [('ztok', m, fc) for fc in range(4)])
                        else:
                            self.cp('act', ytok[s][:], py[:, :], [pyk], [('ytok', s)])
                            pt = 6 + s
                            ptk = 'ps%d' % pt
                            Tb = self.ps[pt][:].bitcast(BF16)
                            for cc in range(4):
                                self.tr(Tb[:, cc * 128:(cc + 1) * 128], ytok[s][:, cc * 128:(cc + 1) * 128], P['identb'][:],
                                        [('ytok', s), 'identb'], [ptk])
                            T_ = (m * 128) // TW
                            self.tt('dve', hyT[:, :, m * 128:(m + 1) * 128], Tb[:, 0:512].rearrange("p (c t) -> p c t", c=4),
                                    hyT[:, :, m * 128:(m + 1) * 128], ALU.mult, [ptk] + [('x2T', T_, fc) for fc in range(4)],
                                    [('hyT', m)] + [('x2T', T_, fc) for fc in range(4)])
                    if o == 0 and isL and b == 0 and l == 0:
                        self.dump('z1tok', ztok[:], [('z1', i) for i in range(nt)], [128, nt, 512], BF16)

    def phase_merge1(self, b, l, kind, hT, hkeys, hcols, hyT, mT):
        P = self.P
        R = self.R
        isL = kind == 'L'
        Ls = L if isL else LC
        TW = min(512, Ls)
        nT = Ls // TW
        wA = R['wA'][l]
        wB = R['wB'][l]
        with ExitStack() as es:
            aT = self.sb(es, 'aTm', [128, 4, Ls], BF16)
            attnT = self.sb(es, 'attnTm', [128, 4, Ls], BF16)
            wg = [self.sb(es, 'wg', [128, 8, 512], BF16) for _ in range(3)]
            wb = [self.sb(es, 'wbr', [128, 4, 512], BF16) for _ in range(3)]
            sg = [self.sb(es, 'sg', [128, 512], BF16) for _ in range(3)]
            m1 = [self.sb(es, 'm1', [128, 512], F32) for _ in range(2)]
            m2 = [self.sb(es, 'm2', [128, 512], F32) for _ in range(2)]
            self.dma(aT[:], R['sa' + kind], [('sa', kind)], ['aTm'], 'aTm')
            self.dma(attnT[:], R['satt' + kind], [('satt', kind)], ['attnTm'], 'attnTm')
            srcs = ((aT, 'aTm'), (attnT, 'attnTm'), (hyT, None))
            cnt = 0
            for fq in range(2):
                wgb = []
                for gi in range(3):
                    if fq == 0 and gi < 2:
                        wgb.append((self.PF[gi], ('PF', gi)))
                    else:
                        self.dma(wg[gi][:], wA[8 + 2 * gi + fq], [('wscr',)], [('wg', gi)], 'wg%d' % gi)
                        wgb.append((wg[gi], ('wg', gi)))
                    self.dma(wb[gi][:], wB[gi, fq], [('wscr',)], [('wbr', gi)], 'wbr%d' % gi)
                if fq == 1:
                    self.pf_fill(0, R['wO'][l][0])
                    self.pf_fill(1, R['wO'][l][1])
                for f4 in range(4):
                    fc = fq * 4 + f4
                    for T in range(nT):
                        q = cnt % 2
                        cnt += 1
                        for gi in range(3):
                            pg = gi
                            pgk = 'ps%d' % pg
                            for k in range(8):
                                self.mm(self.ps[pg][:, 0:TW], wgb[gi][0][:, k, f4 * 128:(f4 + 1) * 128], hT[:, k, hcols(T)], k == 0, k == 7,
                                        [wgb[gi][1]] + hkeys(T, k=k), [pgk])
                            self.act(sg[gi][:, 0:TW], self.ps[pg][:, 0:TW], AF.Sigmoid, [pgk], [('sg', gi)])
                        for gi in range(3):
                            pbn = 3 + gi
                            pbk = 'ps%d' % pbn
                            src, sk = srcs[gi]
                            if sk is None:
                                skeys = [('hyT', i) for i in range(Ls // 128)]
                            else:
                                skeys = [sk]
                            for k in range(4):
                                self.mm(self.ps[pbn][:, 0:TW], wb[gi][:, k, f4 * 128:(f4 + 1) * 128], src[:, k, T * TW:(T + 1) * TW],
                                        k == 0, k == 3, [('wbr', gi)] + skeys, [pbk])
                        self.tt('dve', m1[q][:, 0:TW], self.ps[3][:, 0:TW], sg[0][:, 0:TW], ALU.mult, ['ps3', ('sg', 0)], [('m1', q)])
                        self.tt('dve', m2[q][:, 0:TW], self.ps[4][:, 0:TW], sg[1][:, 0:TW], ALU.mult, ['ps4', ('sg', 1)], [('m2', q)])
                        self.tt('pool', m1[q][:, 0:TW], m1[q][:, 0:TW], m2[q][:, 0:TW], ALU.add, [('m1', q), ('m2', q)], [('m1', q)])
                        self.tt('dve', m2[q][:, 0:TW], self.ps[5][:, 0:TW], sg[2][:, 0:TW], ALU.mult, ['ps5', ('sg', 2), ('m2', q)],
                                [('m2', q)])
                        self.tt('pool', mT[:, fc, T * TW:(T + 1) * TW], m1[q][:, 0:TW], m2[q][:, 0:TW], ALU.add, [('m1', q), ('m2', q)],
                                [('mT', T, fc)])

    def phase_merge2(self, b, l, kind, hT, hkey, mT, xsrc, xsk, xdst, xdk, A2, B2, ABk, bcol):
        P = self.P
        R = self.R
        isL = kind == 'L'
        Ls = L if isL else LC
        nt = Ls // 128
        TW = min(512, Ls)
        wO = R['wO'][l]
        with ExitStack() as es:
            g1bc = self.sb(es, 'g1bc', [128, D], F32)
            xt = [self.sb(es, 'xtm', [128, D], F32) for _ in range(3)]
            tmp = [self.sb(es, 'tmpm', [128, 512], F32) for _ in range(2)]
            junk = [self.sb(es, 'junk2', [128, D], BF16) for _ in range(2)]
            xn = [self.sb(es, 'xn2', [128, D], BF16) for _ in range(2)]
            tmpf = None
            self.dma(g1bc[:], R['modrow'][l][bcol:bcol + 1, 2 * D:3 * D].partition_broadcast(128), [('modrow', l)], ['g1bc'], 'g1bc')
            for i in range(nt):
                s = i % 3
                T = (i * 128) // TW
                self.dma(xt[s][:], xsrc[i * 128:(i + 1) * 128, :], [xsk + (i,)], [('xtm', s)], 'xtm%d' % s)
                for half in range(2):
                    pi = 2 + half + 2 * (i % 2)
                    pk = 'ps%d' % pi
                    for k in range(8):
                        self.mm(self.ps[pi][:, :], mT[:, k, i * 128:(i + 1) * 128], self.PF[half][:, k, :], k == 0, k == 7,
                                [('PF', half)] + [('mT', T, k)], [pk])
                    self.tt('dve', tmp[half][:], self.ps[pi][:, :], g1bc[:, half * 512:(half + 1) * 512], ALU.mult, [pk, 'g1bc'],
                            [('tmpm', half)])
                    self.tt('pool', xt[s][:, half * 512:(half + 1) * 512], xt[s][:, half * 512:(half + 1) * 512], tmp[half][:], ALU.add,
                            [('tmpm', half), ('xtm', s)], [('xtm', s)])
                self.dma(xdst[i * 128:(i + 1) * 128, :], xt[s][:], [('xtm', s)], [xdk + (i,)], 'xtm_st%d' % s)
                self.norm_tile((junk, xn, tmpf, i % 2), xt[s][:], ('xtm', s), A2, B2, ABk, hT, hkey, i)
            self.pf_fill(0, R['wU'][l][0])
            self.pf_fill(1, R['wU'][l][5])

    def phase_ffn(self, b, l, kind, hT, hkeys, hcols, halo_ap, xdst, xdk, bcol, final):
        P = self.P
        R = self.R
        isL = kind == 'L'
        Ls = L if isL else LC
        nt = Ls // 128
        TW = min(512, Ls)
        nT = Ls // TW
        spt = TW // 128
        wU = R['wU'][l]
        wD = R['wD'][l]
        fcw = P['fcw%d' % l]
        with ExitStack() as es:
            wd = self.sb(es, 'wd', [128, 2, 22, 512], BF16)
            g2bc = self.sb(es, 'g2bc', [128, D], F32)
            G = self.sb(es, 'G', [128, 22, TW], BF16)
            wa = [self.sb(es, 'wa', [128, 8, 512], BF16) for _ in range(2)]
            wbb = [self.sb(es, 'wbb', [128, 8, 512], BF16) for _ in range(2)]
            pbuf = [self.sb(es, 'pbuf2', [128, 516], F32) for _ in range(2)]
            cacc = [self.sb(es, 'cacc2', [128, 512], F32) for _ in range(2)]
            ga = [self.sb(es, 'ga', [128, 512], F32) for _ in range(2)]
            xt = [self.sb(es, 'xtf', [128, D], F32) for _ in range(2)]
            tmp = [self.sb(es, 'tmpf3', [128, 512], F32) for _ in range(2)]
            sf_ = self.sb(es, 'sfin', [128, 8], F32)
            junk = self.sb(es, 'junk3', [128, D], BF16)
            for half in range(2):
                self.dma(wd[:, half], wD[half], [('wscr',)], [('wd', half)], 'wd%d' % half)
            self.dma(g2bc[:], R['modrow'][l][bcol:bcol + 1, 5 * D:6 * D].partition_broadcast(128), [('modrow', l)], ['g2bc'], 'g2bc')
            cur = {'a': (None, None), 'b': (None, None)}
            lc = {'a': 0, 'b': 0}

            pfused = {'a': False, 'b': False}

            def getw(which, g):
                bufs = wa if which == 'a' else wbb
                if not pfused[which]:
                    pi_ = 0 if which == 'a' else 1
                    if cur[which][0] is None:
                        cur[which] = (g, 'pf')
                    if cur[which] == (g, 'pf'):
                        return self.PF[pi_], ('PF', pi_)
                    pfused[which] = True
                if cur[which][0] != g or cur[which][1] == 'pf':
                    s = lc[which] % 2
                    lc[which] += 1
                    self.dma(bufs[s][:], wU[g], [('wscr',)], [('w' + which, s)], 'wu%s%d' % (which, s))
                    cur[which] = (g, s)
                s = cur[which][1]
                return bufs[s], ('w' + which, s)
            cnt = 0
            xcnt = 0
            for T in range(nT):
                for fc in range(22):
                    q = cnt % 2
                    cnt += 1
                    ba, bak = getw('a', fc // 4)
                    bb_, bbk = getw('b', (22 + fc) // 4)
                    ca = (fc % 4) * 128
                    cbo = ((22 + fc) % 4) * 128
                    pa, ph, pb_ = self.ps[q * 3], self.ps[q * 3 + 1], self.ps[q * 3 + 2]
                    pak, phk, pbk = 'ps%d' % (q * 3), 'ps%d' % (q * 3 + 1), 'ps%d' % (q * 3 + 2)
                    for k in range(8):
                        self.mm(pa[:, 0:TW], ba[:, k, ca:ca + 128], hT[:, k, hcols(T)], k == 0, k == 7, [bak] + hkeys(T, k=k), [pak])
                    for k in range(8):
                        self.mm(ph[:, 0:2], ba[:, k, ca:ca + 128], halo_ap(k, T), k == 0, k == 7, [bak] + hkeys(T, True, k), [phk])
                    for k in range(8):
                        self.mm(pb_[:, 0:TW], bb_[:, k, cbo:cbo + 128], hT[:, k, hcols(T)], k == 0, k == 7, [bbk] + hkeys(T, k=k), [pbk])
                    self.cp('act', pbuf[q][:, 1:1 + TW], pa[:, 0:TW], [pak], [('pbuf2', q)])
                    hdst = bass.AP(pbuf[q][:, 0:1].tensor, pbuf[q][:, 0:1].offset, [list(pbuf[q][:, 0:1].ap[0]), [TW + 1, 2]])
                    self.cp('act', hdst, ph[:, 0:2], [phk, ('pbuf2', q)], [('pbuf2', q)])
                    self.ts('dve', cacc[q][:, 0:TW], pbuf[q][:, 1:1 + TW], fcw[:, fc, 1:2], fcw[:, fc, 3:4], ALU.mult, ALU.add,
                            [('pbuf2', q), 'fcw%d' % l], [('cacc2', q)])
                    self.stt('dve', cacc[q][:, 0:TW], pbuf[q][:, 0:TW], fcw[:, fc, 0:1], cacc[q][:, 0:TW], ALU.mult, ALU.add,
                             [('pbuf2', q), 'fcw%d' % l, ('cacc2', q)], [('cacc2', q)])
                    self.stt('dve', cacc[q][:, 0:TW], pbuf[q][:, 2:2 + TW], fcw[:, fc, 2:3], cacc[q][:, 0:TW], ALU.mult, ALU.add,
                             [('pbuf2', q), 'fcw%d' % l, ('cacc2', q)], [('cacc2', q)])
                    self.act(ga[q][:, 0:TW], cacc[q][:, 0:TW], AF.Gelu, [('cacc2', q)], [('ga', q)])
                    self.tt('dve', G[:, fc, :], ga[q][:, 0:TW], pb_[:, 0:TW], ALU.mult, [('ga', q), pbk], [('G', fc)])
                Gk = [('G', fc) for fc in range(22)]
                for s4 in range(spt):
                    i = T * spt + s4
                    s = xcnt % 2
                    xcnt += 1
                    self.dma(xt[s][:], xdst[i * 128:(i + 1) * 128, :], [xdk + (i,)], [('xtf', s)], 'xtf%d' % s)
                    for half in range(2):
                        pi = 6 + half
                        pk = 'ps%d' % pi
                        for k in range(22):
                            self.mm(self.ps[pi][:, :], G[:, k, s4 * 128:(s4 + 1) * 128], wd[:, half, k, :], k == 0, k == 21,
                                    [('wd', half)] + Gk, [pk])
                        self.tt('dve', tmp[half][:], self.ps[pi][:, :], g2bc[:, half * 512:(half + 1) * 512], ALU.mult, [pk, 'g2bc'],
                                [('tmpf3', half)])
                        self.tt('pool', xt[s][:, half * 512:(half + 1) * 512], xt[s][:, half * 512:(half + 1) * 512], tmp[half][:],
                                ALU.add, [('tmpf3', half), ('xtf', s)], [('xtf', s)])
                    if final:
                        self.memset('pool', sf_[:, 0:1], 0.0, ['sfin'])
                        self.act(junk[:], xt[s][:], AF.Square, [('xtf', s), 'sfin'], ['junk3', 'sfin'], accum=sf_[:, 0:1])
                        self.ts('dve', sf_[:, 1:2], sf_[:, 0:1], 1.0 / D, EPS, ALU.mult, ALU.add, ['sfin'], ['sfin'])
                        self.act(sf_[:, 2:3], sf_[:, 1:2], AF.Sqrt, ['sfin'], ['sfin'])
                        self.recip(sf_[:, 3:4], sf_[:, 2:3], ['sfin'], ['sfin'])
                        self.stt('dve', xt[s][:], xt[s][:], sf_[:, 3:4], P['fgbc'][:], ALU.mult, ALU.mult, [('xtf', s), 'sfin', 'fgbc'],
                                 [('xtf', s)])
                        self.dma(self.out[b, i * 128:(i + 1) * 128, :], xt[s][:], [('xtf', s)], [('out', b, i)], 'xtf_st%d' % s)
                    else:
                        self.dma(xdst[i * 128:(i + 1) * 128, :], xt[s][:], [('xtf', s)], [xdk + (i,)], 'xtf_st%d' % s)


def _prep_inputs(inp, NB, core):
    C = _consts()
    f32 = np.float32
    b0 = core * NB
    m = {}
    m['x'] = np.ascontiguousarray(inp['x'][b0:b0 + NB])
    m['ctx'] = np.ascontiguousarray(inp['ctx'][b0:b0 + NB])
    m['c5'] = np.ascontiguousarray(np.concatenate([inp['c'][b0:b0 + NB], inp['c_ctx'][None, :]], axis=0))
    return m


def _shared_inputs(inp):
    C = _consts()
    perm = _swap_perm()
    w_in = inp['w_in']
    w_ext = np.concatenate([w_in, w_in[:, :, 1024 + perm], w_in[:, :, 1536 + perm]], axis=2)
    m = {}
    m['ada_w'] = inp['ada_w']
    m['ada_b'] = inp['ada_b']
    m['norm1_g'] = inp['norm1_g']
    m['norm2_g'] = inp['norm2_g']
    m['w_in'] = np.ascontiguousarray(w_ext)
    m['sgu_norm_g'] = inp['sgu_norm_g']
    m['sgu_w'] = inp['sgu_w']
    m['sgu_b'] = np.ascontiguousarray(inp['sgu_b'].reshape(2, 512))
    m['rpbg'] = np.ascontiguousarray(_rpb_gather(inp['na_rpb']).reshape(2, 128, NH * 14 * 64))
    m['hy_conv'] = np.ascontiguousarray(np.concatenate([inp['hy_conv_w'], inp['hy_conv_b'][:, None, :]], axis=1))
    m['hy_p3'] = np.ascontiguousarray(np.stack([inp['hy_filt_b1'], inp['hy_filt_b2'], inp['hy_sin_freq']], axis=1))
    m['hy_w1'] = inp['hy_filt_w1']
    m['hy_w2'] = inp['hy_filt_w2']
    m['hy_w3'] = inp['hy_filt_w3']
    m['hy_skip'] = inp['hy_skip']
    m['w_branch'] = inp['w_branch']
    m['w_out'] = inp['w_out']
    m['ffn_up'] = inp['ffn_w_up']
    m['ffn_conv'] = np.ascontiguousarray(np.concatenate([inp['ffn_conv_w'], inp['ffn_conv_b'][:, None, :]], axis=1))
    m['ffn_down'] = inp['ffn_w_down']
    m['final_g'] = np.ascontiguousarray(inp['final_norm_g'][None, :])
    m['cosT'] = C['cosT']
    m['sinT'] = C['sinT']
    m['m8'] = C['m8']
    m['mn8'] = C['mn8']
    for kind, key in (('L', 'hyL'), ('C', 'hyC')):
        h = C[key]
        m['zT' + kind] = h['zT']
        m['decay' + kind] = h['decay']
        for nm in ('cf', 'sf', 'ci', 'si'):
            m[nm + kind] = h[nm]
    return {k: np.ascontiguousarray(v) for k, v in m.items()}


def run(inputs, NB=4, nlayers=2, debug=(), ncores=NCORES, stop=None):
    inp = {k: np.asarray(v) for k, v in inputs.items()}
    prog = Prog(NB=NB, nlayers=nlayers, debug=debug, stop=stop)
    nc = prog.build()
    shared = _shared_inputs(inp)
    in_maps = []
    for c in range(ncores):
        m = dict(shared)
        m.update(_prep_inputs(inp, NB, c))
        in_maps.append(m)
    res = run_bass_kernel_spmd(nc, in_maps, core_ids=list(range(ncores)))
    return prog, res


def kernel(**inputs):
    prog, res = run(inputs, NB=4, nlayers=2)
    out = np.concatenate([np.asarray(r['out']) for r in res.results], axis=0)
    return out.astype(np.float32)
```

```python
import math
from contextlib import ExitStack

import numpy as np
import ml_dtypes

import concourse.bass as bass
import concourse.mybir as mybir
from concourse.bass_utils import run_bass_kernel_spmd

F32 = mybir.dt.float32
BF16 = mybir.dt.bfloat16
AF = mybir.ActivationFunctionType
ALU = mybir.AluOpType

D = 1024
L = 2048
LC = 256
DFF = 2816
NH = 8
HD = 64
GRID_W = 64
NROWS = 32
EPS = 1e-6
NCORES = 8
NEG8 = -240000.0


class Sched:
    ENG = ('pe', 'act', 'dve', 'pool', 'sp')
    DMA_BW = 140.0
    DMA_LAT = 2000.0
    frozen = False
    reorder = True

    def __init__(self, nc, nsem=4, epoch=4096):
        self.nc = nc
        self.e = {'pe': nc.tensor, 'act': nc.scalar, 'dve': nc.vector, 'pool': nc.gpsimd, 'sp': nc.sync}
        self.ops = []
        self.labels = []
        self.costs = []
        self.phase = 'init'
        self.nsem = nsem
        self.epoch = epoch

    def op(self, eng, fn, reads=(), writes=(), dma=None, cost=200.0, nbytes=0):
        if self.frozen:
            return
        self.ops.append((eng, fn, tuple(reads), tuple(writes), dma))
        self.labels.append(self.phase)
        self.costs.append((cost, nbytes))

    def barrier(self):
        if self.frozen:
            return
        self.ops.append(('barrier', None, (), (), None))
        self.labels.append(self.phase)
        self.costs.append((0.0, 0))

    def _schedule(self, seg, deps_all):
        import heapq
        ops = self.ops
        succ = {i: [] for i in seg}
        indeg = {i: 0 for i in seg}
        for i in seg:
            for j in deps_all[i]:
                succ[j].append(i)
                indeg[i] += 1
        eng_free = {e: 0.0 for e in self.ENG}
        dma_free = 0.0
        dep_ready = {i: 0.0 for i in seg}
        ready = []
        for i in seg:
            if indeg[i] == 0:
                heapq.heappush(ready, (0.0, i))
        out = []
        while ready:
            est, i = heapq.heappop(ready)
            eng = ops[i][0]
            t0 = max(eng_free[eng], dep_ready[i])
            if t0 > est + 1e-6:
                heapq.heappush(ready, (t0, i))
                continue
            cost, nbytes = self.costs[i]
            if ops[i][4] is not None:
                eng_free[eng] = t0 + 60.0
                ts = max(t0 + 60.0, dma_free)
                dma_free = ts + nbytes / self.DMA_BW
                fin = dma_free + self.DMA_LAT
            else:
                eng_free[eng] = t0 + cost
                fin = t0 + cost
            out.append(i)
            for s_ in succ[i]:
                lat = 40.0 if (ops[s_][0] == eng and ops[i][4] is None) else 160.0
                if eng == 'pe' and ops[s_][0] == 'pe' and ops[i][4] is None:
                    lat = 0.0
                r_ = fin + lat
                if r_ > dep_ready[s_]:
                    dep_ready[s_] = r_
                indeg[s_] -= 1
                if indeg[s_] == 0:
                    heapq.heappush(ready, (max(dep_ready[s_], eng_free[ops[s_][0]]), s_))
        assert len(out) == len(seg)
        return out, max(max(eng_free.values()), dma_free)

    def emit(self):
        nc = self.nc
        ops = self.ops
        n = len(ops)
        last_w = {}
        readers = {}
        deps_all = [None] * n
        segments = []
        cur = []
        for i, (eng, fn, reads, writes, dma) in enumerate(ops):
            if eng == 'barrier':
                segments.append((cur, i))
                cur = []
                last_w = {}
                readers = {}
                continue
            d = set()
            for k in reads:
                w = last_w.get(k)
                if w is not None:
                    d.add(w)
            for k in writes:
                w = last_w.get(k)
                if w is not None:
                    d.add(w)
                rr = readers.get(k)
                if rr:
                    d.update(rr)
            d.discard(i)
            deps_all[i] = d
            for k in writes:
                last_w[k] = i
                readers[k] = []
            for k in reads:
                readers.setdefault(k, []).append(i)
            cur.append(i)
        assert not cur, "program must end with a barrier"
        order = []
        model_ns = 0.0
        for seg, bi in segments:
            if self.reorder and seg:
                seg2, t = self._schedule(seg, deps_all)
                model_ns += t
            else:
                seg2 = seg
            order.extend(seg2)
            order.append(bi)
        need_sig = [False] * n
        fdeps = [None] * n
        has_dep = [False] * n
        segops = []
        for i in order:
            eng, fn, reads, writes, dma = ops[i]
            if eng == 'barrier':
                lastc = {}
                out = []
                for j in segops:
                    if ops[j][4] is None:
                        lastc[ops[j][0]] = j
                    elif not has_dep[j]:
                        out.append(j)
                out.extend(lastc.values())
                for j in out:
                    need_sig[j] = True
                fdeps[i] = sorted(out)
                segops = []
                continue
            out = []
            for j in deps_all[i]:
                has_dep[j] = True
                eng_j, _, _, _, dma_j = ops[j]
                if dma_j is None and dma is None and eng_j == eng and eng == 'pe':
                    continue
                out.append(j)
                need_sig[j] = True
            fdeps[i] = out
            segops.append(i)
        ticket = [None] * n
        cnt = {e: 0 for e in self.ENG}
        semc = {}

        def getsem(name):
            if name not in semc:
                semc[name] = nc.alloc_semaphore(name)
            return semc[name]
        semval = {}
        for i in order:
            eng, fn, reads, writes, dma = ops[i]
            if eng == 'barrier':
                continue
            if dma is not None:
                nm = 'd_' + str(dma)
                semval[nm] = semval.get(nm, 0) + 16
                ticket[i] = (nm, semval[nm])
            elif need_sig[i]:
                ep = (cnt[eng] // self.epoch) % self.nsem
                cnt[eng] += 1
                nm = 'c_%s%d' % (eng, ep)
                semval[nm] = semval.get(nm, 0) + 1
                ticket[i] = (nm, semval[nm])
        waited = {e: {} for e in self.ENG}
        nwait = 0
        for i in order:
            eng, fn, reads, writes, dma = ops[i]
            need = {}
            for j in fdeps[i]:
                nm, v = ticket[j]
                if need.get(nm, 0) < v:
                    need[nm] = v
            if eng == 'barrier':
                for en in self.ENG:
                    E = self.e[en]
                    for nm, v in need.items():
                        if waited[en].get(nm, 0) < v:
                            E.wait_ge(getsem(nm), v)
                            waited[en][nm] = v
                            nwait += 1
                continue
            E = self.e[eng]
            for nm, v in need.items():
                if waited[eng].get(nm, 0) < v:
                    E.wait_ge(getsem(nm), v)
                    waited[eng][nm] = v
                    nwait += 1
            ins = fn(E)
            if ticket[i] is not None:
                nm, v = ticket[i]
                ins.then_inc(getsem(nm), 16 if dma is not None else 1)
        self.order = order
        self.stats = dict(ops=n, waits=nwait, sems=len(semc), maxval=max(semval.values()) if semval else 0,
                          model_us=model_ns / 1e3)


def _rope_tables():
    t = np.arange(L)
    rows = t // GRID_W
    cols = t % GRID_W
    inv = 10000.0 ** (-np.arange(16, dtype=np.float64) * 2.0 / 32.0)
    cosT = np.zeros((128, L), np.float64)
    sinT = np.zeros((128, L), np.float64)
    for p in range(128):
        d = p % 64
        i = d % 16
        isb = (d % 32) >= 16
        pos = rows if d < 32 else cols
        ang = pos.astype(np.float32).astype(np.float64) * np.float32(inv[i]).astype(np.float64)
        cosT[p] = np.cos(ang)
        sinT[p] = np.sin(ang) * (1.0 if isb else -1.0)
    return cosT.astype(np.float32), sinT.astype(np.float32)


def _swap_perm():
    perm = np.zeros(512, np.int64)
    for h in range(NH):
        for d in range(HD):
            dd = d + 16 if (d % 32) < 16 else d - 16
            perm[h * 64 + d] = h * 64 + dd
    return perm


def _mask_tables():
    m8 = np.zeros((128, 64), np.float32)
    mn8 = np.zeros((128, 64), np.float32)
    for p in range(128):
        kc = p % 64
        for qc in range(64):
            cs = min(max(qc - 8, 0), 48)
            ok = (kc >= cs) and (kc < cs + 16)
            m8[p, qc] = 8.0 if ok else 0.0
            mn8[p, qc] = 0.0 if ok else NEG8
    return m8, mn8


def _rpb_gather(rpb):
    kc = np.arange(64)[:, None]
    qc = np.arange(64)[None, :]
    dc = np.clip(kc - qc + 15, 0, 30)
    out = np.zeros((2, 128, NH, 14, 64), np.float32)
    for half in range(2):
        for o in range(14):
            g = rpb[:, :, o + half, :][:, :, dc]
            out[:, half * 64:(half + 1) * 64, :, o, :] = g.transpose(0, 2, 1, 3)
    return out


def _hy_consts(Ls):
    nt = Ls // 128
    f32 = np.float32
    t = np.linspace(0.0, 1.0, Ls, dtype=f32)[:, None]
    bands = np.linspace(1e-4, 15, 16, dtype=f32)[None, :]
    ang = (f32(2.0 * math.pi) * np.arange(Ls, dtype=f32)[:, None] / f32(Ls)) * bands
    z = np.concatenate([t, np.cos(ang), -np.sin(ang)], axis=-1).astype(f32)
    zT = np.ascontiguousarray(z.T)
    max_decay = math.log(1e-2) / 0.3
    min_decay = math.log(1e-2) / 1.5
    deltas = np.abs(np.linspace(min_decay, max_decay, 512, dtype=f32))
    decay = np.exp(-t * deltas[None, :]).astype(f32)
    decay_t = np.ascontiguousarray(decay.reshape(nt, 128, 512).transpose(1, 0, 2))
    tt = np.arange(Ls, dtype=np.int64)[:, None]
    ff = np.arange(Ls, dtype=np.int64)[None, :]
    nph = (tt * (2 * ff + 1)) % (4 * Ls)
    ang2 = 2.0 * math.pi * nph.astype(np.float64) / (4 * Ls)
    C = np.cos(ang2)
    Sn = np.sin(ang2)

    def tile_fwd(M):
        return np.ascontiguousarray(M.reshape(nt, 128, nt, 128).transpose(2, 1, 0, 3)).astype(ml_dtypes.bfloat16)

    def tile_inv(M):
        return np.ascontiguousarray(M.reshape(nt, 128, nt, 128).transpose(0, 3, 2, 1)).astype(ml_dtypes.bfloat16)
    return dict(zT=zT, decay=decay_t, cf=tile_fwd(C), sf=tile_fwd(Sn), ci=tile_inv(C), si=tile_inv(Sn))


_CONST_CACHE = {}


def _consts():
    if not _CONST_CACHE:
        cosT, sinT = _rope_tables()
        m8, mn8 = _mask_tables()
        _CONST_CACHE.update(cosT=cosT, sinT=sinT, m8=m8, mn8=mn8, hyL=_hy_consts(L), hyC=_hy_consts(LC))
    return _CONST_CACHE


class StopBuild(Exception):
    pass


class Prog:
    def __init__(self, NB=4, nlayers=2, debug=(), stop=None):
        self.stop = stop
        self.NB = NB
        self.nlayers = nlayers
        self.debug = set(debug)
        self.nc = bass.Bass("TRN2", target_bir_lowering=False)
        self.S = Sched(self.nc)
        self.uid = 0
        self.dbg_out = {}

    def din(self, name, shape, dt=F32):
        return self.nc.dram_tensor(name, list(shape), dt, kind="ExternalInput").ap()

    def dscr(self, name, shape, dt):
        return self.nc.dram_tensor(name, list(shape), dt, kind="Internal").ap()

    def dout(self, name, shape, dt=F32):
        return self.nc.dram_tensor(name, list(shape), dt, kind="ExternalOutput").ap()

    def sb(self, es, name, shape, dt):
        self.uid += 1
        return es.enter_context(self.nc.sbuf_tensor("%s_%d" % (name, self.uid), list(shape), dt))

    @staticmethod
    def _fs(ap):
        n = 1
        for d in ap.shape[1:]:
            n *= int(d)
        return n

    def mm(self, out, lhsT, rhs, start, stop, r, w):
        nn = self._fs(rhs)
        c = max(nn, 48) / 2.2 + 16.0
        if lhsT.dtype == F32:
            c *= 4.0
        self.S.op('pe', lambda e: e.matmul(out, lhsT=lhsT, rhs=rhs, start=start, stop=stop), r, w, cost=c)

    def tr(self, out, in_, ident, r, w):
        c = 64.0 * (4.0 if in_.dtype == F32 else 1.0)
        self.S.op('pe', lambda e: e.transpose(out=out, in_=in_, identity=ident), r, w, cost=c)

    def act(self, out, in_, func, r, w, bias=None, scale=None, accum=None):
        kw = {}
        if bias is not None:
            kw['bias'] = bias
        if scale is not None:
            kw['scale'] = scale
        if accum is not None:
            kw['accum_out'] = accum
        self.S.op('act', lambda e: e.activation(out=out, in_=in_, func=func, **kw), r, w, cost=220.0 + self._fs(in_) / 1.1)

    def _vcost(self, eng, ap):
        nn = self._fs(ap)
        if eng == 'pool':
            return 250.0 + nn / 0.45
        if eng == 'act':
            return 220.0 + nn / 1.1
        return 130.0 + nn / 0.9

    def tt(self, eng, out, in0, in1, op, r, w):
        self.S.op(eng, lambda e: e.tensor_tensor(out=out, in0=in0, in1=in1, op=op), r, w, cost=self._vcost(eng, out))

    def ts(self, eng, out, in0, s1, s2, op0, op1, r, w):
        c = self._vcost(eng, out)
        if op1 is None:
            self.S.op(eng, lambda e: e.tensor_scalar(out=out, in0=in0, scalar1=s1, scalar2=None, op0=op0), r, w, cost=c)
        else:
            self.S.op(eng, lambda e: e.tensor_scalar(out=out, in0=in0, scalar1=s1, scalar2=s2, op0=op0, op1=op1), r, w, cost=c)

    def stt(self, eng, out, in0, scalar, in1, op0, op1, r, w):
        self.S.op(eng, lambda e: e.scalar_tensor_tensor(out=out, in0=in0, scalar=scalar, in1=in1, op0=op0, op1=op1), r, w,
                  cost=self._vcost(eng, out))

    def cp(self, eng, out, in_, r, w):
        c = self._vcost(eng, out)
        if eng == 'act':
            self.S.op('act', lambda e: e.copy(out=out, in_=in_), r, w, cost=c)
        else:
            self.S.op(eng, lambda e: e.tensor_copy(out=out, in_=in_), r, w, cost=c)

    def memset(self, eng, ap, val, w, r=()):
        self.S.op(eng, lambda e: e.memset(ap, val), r, w, cost=self._vcost(eng, ap))

    def recip(self, out, in_, r, w):
        self.S.op('dve', lambda e: e.reciprocal(out=out, in_=in_), r, w, cost=self._vcost('dve', out))

    def dma(self, out, in_, r, w, key):
        nb = int(out.shape[0]) * self._fs(out) * (2 if out.dtype == BF16 else 4)
        self.S.op('sp', lambda e: e.dma_start(out=out, in_=in_), r, w, dma=key, nbytes=nb)

    def chk(self, name):
        if self.stop == name:
            self.S.frozen = True

    def dump(self, tag, sb_ap, rkeys, shape, dt):
        if tag not in self.debug:
            return
        o = self.dout("dbg_" + tag, shape, dt)
        self.dbg_out[tag] = "dbg_" + tag
        self.dma(o, sb_ap, list(rkeys), [('dbg', tag)], 'dbg_' + tag)

    def build(self):
        nc = self.nc
        NB = self.NB
        NL = self.nlayers
        I = {}
        I['x'] = self.din('x', [NB, L, D])
        I['ctx'] = self.din('ctx', [NB, LC, D])
        I['c5'] = self.din('c5', [NB + 1, D])
        I['ada_w'] = self.din('ada_w', [2, D, 6 * D])
        I['ada_b'] = self.din('ada_b', [2, 6 * D])
        I['norm1_g'] = self.din('norm1_g', [2, D])
        I['norm2_g'] = self.din('norm2_g', [2, D])
        I['w_in'] = self.din('w_in', [2, D, 8192])
        I['sgu_norm_g'] = self.din('sgu_norm_g', [2, 512])
        I['sgu_w'] = self.din('sgu_w', [2, 4, 128, 128])
        I['sgu_b'] = self.din('sgu_b', [2, 512])
        I['rpbg'] = self.din('rpbg', [2, 128, NH * 14 * 64])
        I['hy_conv'] = self.din('hy_conv', [2, 4, 1536])
        I['hy_p3'] = self.din('hy_p3', [2, 3, 64])
        I['hy_w1'] = self.din('hy_w1', [2, 33, 64])
        I['hy_w2'] = self.din('hy_w2', [2, 64, 64])
        I['hy_w3'] = self.din('hy_w3', [2, 64, 2048])
        I['hy_skip'] = self.din('hy_skip', [2, 2, 512])
        I['w_branch'] = self.din('w_branch', [2, 3, 512, D])
        I['w_out'] = self.din('w_out', [2, D, D])
        I['ffn_up'] = self.din('ffn_up', [2, D, 2 * DFF])
        I['ffn_conv'] = self.din('ffn_conv', [2, 4, DFF])
        I['ffn_down'] = self.din('ffn_down', [2, DFF, D])
        I['final_g'] = self.din('final_g', [1, D])
        I['cosT'] = self.din('cosT', [128, L])
        I['sinT'] = self.din('sinT', [128, L])
        I['m8'] = self.din('m8', [128, 64])
        I['mn8'] = self.din('mn8', [128, 64])
        for kind, Ls in (('L', L), ('C', LC)):
            nt = Ls // 128
            I['zT' + kind] = self.din('zT' + kind, [33, Ls])
            I['decay' + kind] = self.din('decay' + kind, [128, nt, 512])
            for nm in ('cf', 'sf', 'ci', 'si'):
                I[nm + kind] = self.din(nm + kind, [nt, 128, nt, 128], BF16)
        self.I = I
        self.out = self.dout('out', [NB, L, D])
        R = {}
        R['wA'] = self.dscr('s_wA', [2, 16, 128, 8, 512], BF16)
        R['wB'] = self.dscr('s_wB', [2, 3, 2, 128, 4, 512], BF16)
        R['wO'] = self.dscr('s_wO', [2, 2, 128, 8, 512], BF16)
        R['wU'] = self.dscr('s_wU', [2, 11, 128, 8, 512], BF16)
        R['wD'] = self.dscr('s_wD', [2, 2, 128, 22, 512], BF16)
        R['modrow'] = self.dscr('s_modrow', [2, NB + 1, 6 * D], F32)
        R['tbl'] = self.dscr('s_tbl', [2, 128, NH * 27 * 64], BF16)
        R['specL'] = self.dscr('s_specL', [2, 2, L // 128, 128, 1024], F32)
        R['specC'] = self.dscr('s_specC', [2, 2, LC // 128, 128, 1024], F32)
        R['xs'] = self.dscr('s_xs', [NB, L, D], F32)
        R['xcs'] = self.dscr('s_xcs', [NB, LC, D], F32)
        R['saL'] = self.dscr('s_saL', [128, 4, L], BF16)
        R['sattL'] = self.dscr('s_sattL', [128, 4, L], BF16)
        R['saC'] = self.dscr('s_saC', [128, 4, LC], BF16)
        R['sattC'] = self.dscr('s_sattC', [128, 4, LC], BF16)
        self.R = R

        with ExitStack() as es:
            self.ps = [es.enter_context(nc.psum_tensor("ps%d" % i, [128, 512], F32)) for i in range(8)]
            P = {}
            P['identb'] = self.sb(es, 'identb', [128, 128], BF16)
            P['identf'] = self.sb(es, 'identf', [128, 128], F32)
            P['onesf'] = self.sb(es, 'onesf', [128, 128], F32)
            P['ss'] = self.sb(es, 'ss', [128, 8], F32)
            for l in range(2):
                P['g1T%d' % l] = self.sb(es, 'g1T', [128, 8], F32)
                P['g2T%d' % l] = self.sb(es, 'g2T', [128, 8], F32)
                P['gsnT%d' % l] = self.sb(es, 'gsnT', [128, 4], F32)
                P['wsT%d' % l] = self.sb(es, 'wsT', [128, 4, 128], BF16)
                P['bsb%d' % l] = self.sb(es, 'bsb', [128, 4, 128], F32)
                P['hcw%d' % l] = self.sb(es, 'hcw', [128, 12, 4], F32)
                P['fcw%d' % l] = self.sb(es, 'fcw', [128, 22, 4], F32)
                P['modT%d' % l] = self.sb(es, 'modT', [128, 48, NB + 1], F32)
            P['fgbc'] = self.sb(es, 'fgbc', [128, D], F32)
            P['KcT'] = self.sb(es, 'KcT', [128, 4, LC], BF16)
            P['Vc1'] = self.sb(es, 'Vc1', [128, 2, NH, 65], BF16)
            P['hT'] = self.sb(es, 'hT', [128, 8, L + 2], BF16)
            P['hTc'] = self.sb(es, 'hTc', [128, 8, LC + 2], BF16)
            self.P = P
            try:
                self.prologue(es)
                for b in range(NB):
                    for l in range(NL):
                        self.seq_pass(b, l, 'C')
                        self.seq_pass(b, l, 'L')
            except StopBuild:
                pass
            self.S.frozen = False
            self.S.barrier()
            self.S.emit()
        return nc

    def prologue(self, es0):
        P = self.P
        I = self.I
        R = self.R
        nc = self.nc
        NB = self.NB
        self.memset('pool', P['identf'][:], 0.0, ['identf'])
        self.S.op('pool', lambda e: e.affine_select(out=P['identf'][:], in_=P['identf'][:], pattern=[[-1, 128]],
                                                    compare_op=ALU.not_equal, fill=1.0, base=0, channel_multiplier=1),
                  ['identf'], ['identf'])
        self.cp('dve', P['identb'][:], P['identf'][:], ['identf'], ['identb'])
        self.memset('pool', P['onesf'][:], 1.0, ['onesf'])
        self.memset('pool', P['hT'][:], 0.0, [('hT', i) for i in range(L // 128)])
        self.memset('pool', P['hTc'][:], 0.0, [('hTc', i) for i in range(LC // 128)])
        self.memset('pool', P['Vc1'][:], 1.0, ['Vc1'])
        self.dma(P['fgbc'][:], I['final_g'][0:1, :].partition_broadcast(128), [], ['fgbc'], 'fgbc')
        self.S.phase = 'params'
        with ExitStack() as es:
            stage = self.sb(es, 'stage', [8, 6 * D], F32)
            s5 = self.sb(es, 's5', [8, D], F32)
            s5T = self.sb(es, 's5T', [128, 8, NB + 1], F32)
            adw = [self.sb(es, 'adw', [128, 8, 512], F32) for _ in range(2)]
            adb = self.sb(es, 'adb', [1, 6 * D], F32)
            sgw = self.sb(es, 'sgw', [128, 4, 128], F32)
            gst = self.sb(es, 'gst', [128, 8 * 14 * 64 // 2], F32)
            tblb = self.sb(es, 'tblb', [128, NH, 27, 64], BF16)
            m8 = self.sb(es, 'm8', [128, 64], F32)
            mn8 = self.sb(es, 'mn8', [128, 64], F32)
            nb1 = NB + 1
            self.dma(m8[:], I['m8'], [], ['m8'], 'm8')
            self.dma(mn8[:], I['mn8'], [], ['mn8'], 'mn8')

            def rows_to_cols(src_rows, Rr, N, dst, tag):
                nch = N // 128
                self.dma(stage[0:Rr, 0:N], src_rows, [], ['stage'], 'stage')
                pb = self.ps[0]
                for c in range(nch):
                    self.tr(pb[:, c * Rr:(c + 1) * Rr], stage[0:Rr, c * 128:(c + 1) * 128], P['identf'][0:Rr, 0:Rr],
                            ['stage', 'identf'], ['ps0'])
                self.cp('dve', dst, pb[:, 0:nch * Rr].rearrange("p (c r) -> p c r", r=Rr), ['ps0'], [tag])

            self.dma(s5[0:nb1, :], I['c5'], [], ['s5'], 's5')
            self.act(s5[0:nb1, :], s5[0:nb1, :], AF.Silu, ['s5'], ['s5'])
            pb = self.ps[1]
            for k in range(8):
                self.tr(pb[:, k * nb1:(k + 1) * nb1], s5[0:nb1, k * 128:(k + 1) * 128], P['identf'][0:nb1, 0:nb1],
                        ['s5', 'identf'], ['ps1'])
            self.cp('dve', s5T[:], pb[:, 0:8 * nb1].rearrange("p (c r) -> p c r", r=nb1), ['ps1'], ['s5T'])
            self.chk('p_s5')
            for l in range(self.nlayers):
                rows_to_cols(I['norm1_g'][l:l + 1, :], 1, D, P['g1T%d' % l][:].unsqueeze(2), 'g1T%d' % l)
                rows_to_cols(I['norm2_g'][l:l + 1, :], 1, D, P['g2T%d' % l][:].unsqueeze(2), 'g2T%d' % l)
                rows_to_cols(I['sgu_norm_g'][l:l + 1, :], 1, 512, P['gsnT%d' % l][:].unsqueeze(2), 'gsnT%d' % l)
                rows_to_cols(I['hy_conv'][l], 4, 1536, P['hcw%d' % l][:], 'hcw%d' % l)
                rows_to_cols(I['ffn_conv'][l], 4, DFF, P['fcw%d' % l][:], 'fcw%d' % l)
                self.chk('p_rows')
                self.dma(P['bsb%d' % l][:].rearrange("p g q -> p (g q)"), I['sgu_b'][l:l + 1, :].partition_broadcast(128), [],
                         ['bsb%d' % l], 'bsb%d' % l)
                self.dma(sgw[:], I['sgu_w'][l].rearrange("g q p -> q g p"), [], ['sgw'], 'sgw')
                pb2 = self.ps[2]
                for g in range(4):
                    self.tr(pb2[:, g * 128:(g + 1) * 128], sgw[:, g, :], P['identf'][:], ['sgw', 'identf'], ['ps2'])
                self.cp('dve', P['wsT%d' % l][:].rearrange("p g q -> p (g q)"), pb2[:, :], ['ps2'], ['wsT%d' % l])
                self.chk('p_sgw')
                self.dma(adb[:], I['ada_b'][l:l + 1, :], [], ['adb'], 'adb')
                for n in range(12):
                    a = adw[n % 2]
                    self.dma(a[:], I['ada_w'][l][:, n * 512:(n + 1) * 512].rearrange("(kc p) c -> p kc c", p=128), [],
                             [('adw', n % 2)], 'adw%d' % (n % 2))
                    pm = self.ps[3 + (n % 2)]
                    pk = 'ps%d' % (3 + (n % 2))
                    for k in range(8):
                        self.mm(pm[0:nb1, :], s5T[:, k, :], a[:, k, :], k == 0, False, ['s5T', ('adw', n % 2)], [pk])
                    self.mm(pm[0:nb1, :], P['onesf'][0:1, 0:nb1], adb[0:1, n * 512:(n + 1) * 512], False, True,
                            ['onesf', 'adb'], [pk])
                    self.cp('dve', stage[0:nb1, n * 512:(n + 1) * 512], pm[0:nb1, :], [pk], ['stage'])
                self.dma(R['modrow'][l], stage[0:nb1, :], ['stage'], [('modrow', l)], 'modrow_st')
                self.chk('p_mod')
                pb = self.ps[5]
                for c in range(48):
                    self.tr(pb[:, c * nb1:(c + 1) * nb1], stage[0:nb1, c * 128:(c + 1) * 128], P['identf'][0:nb1, 0:nb1],
                            ['stage', 'identf'], ['ps5'])
                self.cp('dve', P['modT%d' % l][:], pb[:, 0:48 * nb1].rearrange("p (c r) -> p c r", r=nb1), ['ps5'],
                        ['modT%d' % l])
                self.chk('p_modT')
                for hh in range(2):
                    self.dma(gst[:], I['rpbg'][l][:, hh * 3584:(hh + 1) * 3584], [], ['gst'], 'gst')
                    g3 = gst[:].rearrange("p (a q) -> p a q", q=64)
                    self.tt('dve', g3, g3, m8[:].unsqueeze(1).to_broadcast([128, 56, 64]), ALU.mult, ['gst', 'm8'], ['gst'])
                    for h4 in range(4):
                        self.tt('dve', tblb[:, hh * 4 + h4, 0:14, :], g3[:, h4 * 14:(h4 + 1) * 14, :],
                                mn8[:].unsqueeze(1).to_broadcast([128, 14, 64]), ALU.add, ['gst', 'mn8'], ['tblb'])
                self.cp('dve', tblb[:, :, 14, :], tblb[:, :, 2, :], ['tblb'], ['tblb'])
                self.cp('dve', tblb[:, :, 15, :], tblb[:, :, 10, :], ['tblb'], ['tblb'])
                self.memset('dve', tblb[0:64, :, 14, :], NEG8, ['tblb'], ['tblb'])
                self.memset('dve', tblb[64:128, :, 15, :], NEG8, ['tblb'], ['tblb'])
                self.memset('dve', tblb[:, :, 16, :], NEG8, ['tblb'], ['tblb'])
                for ci, src_slot in enumerate((3, 14, 5, 4, 7, 6, 9, 8, 16, 15)):
                    self.cp('dve', tblb[:, :, 17 + ci, :], tblb[:, :, src_slot, :], ['tblb'], ['tblb'])
                self.dma(R['tbl'][l], tblb[:].rearrange("p h o q -> p (h o q)"), ['tblb'], [('tbl', l)], 'tbl_st')
        self.S.barrier()
        self.chk('params')
        self.S.phase = 'filter'
        with ExitStack() as es:
            st = [self.sb(es, 'cst', [128, 2048], F32) for _ in range(2)]
            sbf = [self.sb(es, 'csb', [128, 2048], BF16) for _ in range(2)]
            cnt = [0]
            engs = ['pool', 'act', 'pool']

            def cast_units(src, dst, KC, N):
                for kc in range(KC):
                    for n0 in range(0, N, 2048):
                        yield (src, dst, kc, n0, min(2048, N - n0))

            def all_units():
                for l in range(self.nlayers if not getattr(self, 'skip_casts', False) else 0):
                    yield from cast_units(I['w_in'][l], R['wA'][l], 8, 8192)
                    for i in range(3):
                        yield from cast_units(I['w_branch'][l, i], R['wB'][l, i], 4, D)
                    yield from cast_units(I['w_out'][l], R['wO'][l], 8, D)
                    yield from cast_units(I['ffn_up'][l], R['wU'][l], 8, 2 * DFF)
                    yield from cast_units(I['ffn_down'][l], R['wD'][l], 22, D)
            gen = all_units()

            def pump(k):
                ph = self.S.phase
                self.S.phase = 'casts'
                for _ in range(k):
                    u = next(gen, None)
                    if u is None:
                        break
                    src, dst, kc, n0, w = u
                    i = cnt[0] % 2
                    eng = engs[cnt[0] % 3]
                    cnt[0] += 1
                    self.dma(st[i][:, 0:w], src[kc * 128:(kc + 1) * 128, n0:n0 + w], [], [('cst', i)], 'cst%d' % i)
                    self.cp(eng, sbf[i][:, 0:w], st[i][:, 0:w], [('cst', i)], [('csb', i)])
                    g0 = n0 // 512
                    ng = w // 512
                    self.dma(dst[g0:g0 + ng, :, kc, :].rearrange("g p c -> p g c"),
                             sbf[i][:, 0:w].rearrange("p (g c) -> p g c", g=ng), [('csb', i)], [('wscr',)], 'csb%d' % i)
                self.S.phase = ph
            self.pump = pump
            for l in range(self.nlayers):
                self.hy_filter(l, 'L')
                if l == 0:
                    self.hy_filter(l, 'C')
            pump(10000)
            self.pump = lambda k: None
        self.S.barrier()
        self.chk('filter')

    def range_reduce_sin(self, u, tmp, keys_u, keys_tmp):
        for _ in range(2):
            self.ts('dve', tmp, u, math.pi, 2 * math.pi, ALU.is_gt, ALU.mult, keys_u, keys_tmp)
            self.tt('dve', u, u, tmp, ALU.subtract, keys_u + keys_tmp, keys_u)
            self.ts('dve', tmp, u, -math.pi, 2 * math.pi, ALU.is_lt, ALU.mult, keys_u, keys_tmp)
            self.tt('dve', u, u, tmp, ALU.add, keys_u + keys_tmp, keys_u)
        self.act(u, u, AF.Sin, keys_u, keys_u)

    def hy_filter(self, l, kind):
        P = self.P
        I = self.I
        R = self.R
        Ls = L if kind == 'L' else LC
        nt = Ls // 128
        TW = min(512, Ls)
        spec = R['spec' + kind][l]
        with ExitStack() as es:
            zT = self.sb(es, 'zT', [33, Ls], F32)
            w1 = self.sb(es, 'w1', [33, 64], F32)
            w2 = self.sb(es, 'w2', [64, 64], F32)
            w3 = self.sb(es, 'w3', [64, 2048], F32)
            p3 = self.sb(es, 'p3', [64, 4], F32)
            fb = self.sb(es, 'fb', [64, 2], F32)
            st3 = self.sb(es, 'st3', [4, 64], F32)
            h1T = self.sb(es, 'h1T', [64, Ls], F32)
            h2T = self.sb(es, 'h2T', [64, Ls], F32)
            tmp = self.sb(es, 'tmp', [64, TW], F32)
            he = self.sb(es, 'he', [128, nt, 512], BF16)
            ho = self.sb(es, 'ho', [128, nt, 512], BF16)
            dec = [self.sb(es, 'dec', [128, 512], F32) for _ in range(2)]
            hd = [self.sb(es, 'hd', [128, 512], F32) for _ in range(2)]
            ab = [self.sb(es, 'ab', [128, 512], F32) for _ in range(2)]
            rn = self.sb(es, 'rn', [128, 512], F32)
            skb = self.sb(es, 'skb', [128, 512], F32)
            cfb = [self.sb(es, 'cfb', [128, nt, 128], BF16) for _ in range(2)]
            sfb = [self.sb(es, 'sfb', [128, nt, 128], BF16) for _ in range(2)]
            ko = [self.sb(es, 'ko', [128, 2, 512], F32) for _ in range(2)]
            self.dma(zT[:], I['zT' + kind], [], [('hsrc', 0)], 'zT')
            self.dma(w1[:], I['hy_w1'][l], [], ['w1'], 'w1')
            self.dma(w2[:], I['hy_w2'][l], [], ['w2'], 'w2')
            self.dma(w3[:], I['hy_w3'][l], [], ['w3'], 'w3')
            self.dma(st3[0:3, :], I['hy_p3'][l], [], ['st3'], 'st3')
            pb = self.ps[0]
            self.tr(pb[0:64, 0:3], st3[0:3, :], P['identf'][0:3, 0:3], ['st3', 'identf'], ['ps0'])
            self.cp('dve', p3[:, 0:3], pb[0:64, 0:3], ['ps0'], ['p3'])
            self.tt('dve', fb[:, 0:1], p3[:, 0:1], p3[:, 2:3], ALU.mult, ['p3'], ['fb'])
            self.tt('dve', fb[:, 1:2], p3[:, 1:2], p3[:, 2:3], ALU.mult, ['p3', 'fb'], ['fb'])
            for li, (wsrc, src, dst, kk) in enumerate(((w1, zT, h1T, 33), (w2, h1T, h2T, 64))):
                for t0 in range(0, Ls, TW):
                    pi = 1 + ((t0 // TW) % 2)
                    pk = 'ps%d' % pi
                    self.mm(self.ps[pi][0:64, 0:TW], wsrc[0:kk, :], src[0:kk, t0:t0 + TW], True, True,
                            ['w%d' % (li + 1), ('hsrc', li)], [pk])
                    self.ts('dve', dst[:, t0:t0 + TW], self.ps[pi][0:64, 0:TW], p3[:, 2:3], fb[:, li:li + 1], ALU.mult, ALU.add,
                            [pk, 'p3', 'fb'], [('hsrc', li + 1)])
                    self.range_reduce_sin(dst[:, t0:t0 + TW], tmp[:, 0:TW], [('hsrc', li + 1)], ['rrtmp'])
            for o in range(2):
                pn = self.ps[3]
                first = True
                for i in range(nt):
                    self.pump(2)
                    self.dma(dec[i % 2][:], I['decay' + kind][:, i, :], [], [('dec', i % 2)], 'dec%d' % (i % 2))
                    for dr in range(2):
                        n = o * 2 + dr
                        pi = 4 + dr
                        pk = 'ps%d' % pi
                        self.mm(self.ps[pi][:, :], h2T[:, i * 128:(i + 1) * 128], w3[:, n * 512:(n + 1) * 512], True, True,
                                [('hsrc', 2), 'w3'], [pk])
                        self.tt('dve', hd[dr][:], self.ps[pi][:, :], dec[i % 2][:], ALU.mult, [pk, ('dec', i % 2)], [('hd', dr)])
                        if dr == 1 and i == 0:
                            self.memset('dve', hd[dr][0:1, :], 0.0, [('hd', dr)], [('hd', dr)])
                        self.act(ab[dr][:], hd[dr][:], AF.Abs, [('hd', dr)], [('ab', dr)])
                        last = (i == nt - 1 and dr == 1)
                        self.mm(pn[:, :], P['onesf'][:], ab[dr][:], first, last, ['onesf', ('ab', dr)], ['ps3'])
                        first = False
                    self.tt('dve', he[:, i, :], hd[0][:], hd[1][:], ALU.add, [('hd', 0), ('hd', 1)], [('he', i)])
                    self.tt('pool', ho[:, i, :], hd[0][:], hd[1][:], ALU.subtract, [('hd', 0), ('hd', 1)], [('ho', i)])
                self.recip(rn[:], pn[:, :], ['ps3'], ['rn'])
                self.ts('dve', rn[:], rn[:], 1.0 / Ls, None, ALU.mult, None, ['rn'], ['rn'])
                self.dma(skb[:], I['hy_skip'][l, o:o + 1, :].partition_broadcast(128), [], ['skb'], 'skb')
                self.ts('dve', skb[:], skb[:], 1.0 / Ls, None, ALU.mult, None, ['skb'], ['skb'])
                hek = [('he', i) for i in range(nt)]
                hok = [('ho', i) for i in range(nt)]
                for m in range(nt):
                    self.pump(2)
                    s = m % 2
                    self.dma(cfb[s][:], I['cf' + kind][m], [], [('cfb', s)], 'cfb%d' % s)
                    self.dma(sfb[s][:], I['sf' + kind][m], [], [('sfb', s)], 'sfb%d' % s)
                    pr, ps_ = self.ps[6], self.ps[7]
                    for tt_ in range(nt):
                        self.mm(pr[:, :], cfb[s][:, tt_, :], he[:, tt_, :], tt_ == 0, tt_ == nt - 1, [('cfb', s)] + hek, ['ps6'])
                    for tt_ in range(nt):
                        self.mm(ps_[:, :], sfb[s][:, tt_, :], ho[:, tt_, :], tt_ == 0, tt_ == nt - 1, [('sfb', s)] + hok, ['ps7'])
                    self.tt('dve', ko[s][:, 0, :], pr[:, :], rn[:], ALU.mult, ['ps6', 'rn'], [('ko', s)])
                    self.tt('dve', ko[s][:, 0, :], ko[s][:, 0, :], skb[:], ALU.add, [('ko', s), 'skb'], [('ko', s)])
                    self.tt('dve', ko[s][:, 1, :], ps_[:, :], rn[:], ALU.mult, ['ps7', 'rn', ('ko', s)], [('ko', s)])
                    self.dma(spec[o, m], ko[s][:].rearrange("p a c -> p (a c)"), [('ko', s)], [('spec', kind, l)], 'ko%d' % s)
        self.S.barrier()

    def norm_tile(self, es_bufs, x_sb, xk, A, B, Akeys, hT, hkey, i):
        P = self.P
        junks, xns, _unused, pbi = es_bufs
        par = i % 2
        junk, xn = junks[par], xns[par]
        ss = P['ss'][:, 4 * par:4 * par + 4]
        sk = ('ss', par)
        pb = self.ps[pbi]
        pk = 'ps%d' % pbi
        self.memset('pool', ss[:, 0:1], 0.0, [sk])
        self.act(junk[:], x_sb, AF.Square, [xk, sk], [('njunk', par), sk], accum=ss[:, 0:1])
        self.ts('dve', ss[:, 1:2], ss[:, 0:1], 1.0 / D, EPS, ALU.mult, ALU.add, [sk], [sk])
        self.act(ss[:, 2:3], ss[:, 1:2], AF.Sqrt, [sk], [sk])
        self.recip(ss[:, 3:4], ss[:, 2:3], [sk], [sk])
        self.act(xn[:], x_sb, AF.Copy, [xk, sk], [('nxn', par)], scale=ss[:, 3:4])
        pbb = pb[:].bitcast(BF16)
        for c in range(8):
            self.tr(pbb[:, c * 128:(c + 1) * 128], xn[:, c * 128:(c + 1) * 128], P['identb'][:], [('nxn', par), 'identb'], [pk])
        for c in range(8):
            self.ts('dve', hT[:, c, 1 + 128 * i:1 + 128 * (i + 1)], pbb[:, c * 128:(c + 1) * 128], A[:, c:c + 1], B[:, c:c + 1],
                    ALU.mult, ALU.add, [pk] + Akeys, [(hkey, i, c)])

    def seq_pass(self, b, l, kind):
        P = self.P
        I = self.I
        R = self.R
        NB = self.NB
        isL = (kind == 'L')
        Ls = L if isL else LC
        nt = Ls // 128
        TW = min(512, Ls)
        nT = Ls // TW
        spt = TW // 128
        last_layer = (l == self.nlayers - 1)
        full = isL or (not last_layer)
        bcol = b if isL else NB
        hT = P['hT'] if isL else P['hTc']
        hkey = 'hT' if isL else 'hTc'
        modT = P['modT%d' % l]
        wA = R['wA'][l]
        if isL:
            xsrc = I['x'][b] if l == 0 else R['xs'][b]
            xsk = ('xin',) if l == 0 else ('xs', b)
            xdst = R['xs'][b]
            xdk = ('xs', b)
        else:
            xsrc = I['ctx'][b] if l == 0 else R['xcs'][b]
            xsk = ('xin',) if l == 0 else ('xcs', b)
            xdst = R['xcs'][b]
            xdk = ('xcs', b)

        def hkeys(T, halo=False, k=0):
            ks = [(hkey, T * spt + s, k) for s in range(spt)]
            if halo:
                if T * spt - 1 >= 0:
                    ks.append((hkey, T * spt - 1, k))
                if (T + 1) * spt < nt:
                    ks.append((hkey, (T + 1) * spt, k))
            return ks

        def hcols(T):
            return slice(1 + T * TW, 1 + (T + 1) * TW)

        def halo_ap(kc, T):
            base = hT[:, kc, T * TW:T * TW + 1]
            return bass.AP(base.tensor, base.offset, [list(base.ap[0]), [TW + 1, 2]])

        with ExitStack() as esP:
            AB = self.sb(esP, 'AB', [128, 4, 8], F32)
            PF = [self.sb(esP, 'PF', [128, 8, 512], BF16) for _ in range(2)]
            self.PF = PF

            def pf_fill(i, src):
                self.dma(PF[i][:], src, [('wscr',)], [('PF', i)], 'PF%d' % i)
            self.pf_fill = pf_fill
            A1 = AB[:, 0, :]
            A2 = AB[:, 1, :]
            B1 = modT[:, 0:8, bcol]
            B2 = modT[:, 24:32, bcol]
            self.stt('dve', A1, modT[:, 8:16, bcol], 1.0, P['g1T%d' % l][:], ALU.add, ALU.mult, ['modT%d' % l, 'g1T%d' % l], ['AB'])
            self.stt('dve', A2, modT[:, 32:40, bcol], 1.0, P['g2T%d' % l][:], ALU.add, ALU.mult, ['modT%d' % l, 'g2T%d' % l, 'AB'],
                     ['AB'])
            ABk = ['AB', 'modT%d' % l]
            self.S.phase = kind + '_N1'
            with ExitStack() as es:
                xt = [self.sb(es, 'xt', [128, D], F32) for _ in range(3)]
                junk = [self.sb(es, 'junk', [128, D], BF16) for _ in range(2)]
                xn = [self.sb(es, 'xn', [128, D], BF16) for _ in range(2)]
                for i in range(nt):
                    s = i % 3
                    self.dma(xt[s][:], xsrc[i * 128:(i + 1) * 128, :], [xsk + (i,)], [('xt', s)], 'xt%d' % s)
                    self.norm_tile((junk, xn, None, i % 2), xt[s][:], ('xt', s), A1, B1, ABk, hT, hkey, i)
                if isL and b == 0 and l == 0:
                    self.dump('hT', hT[:], [(hkey, i, c) for i in range(nt) for c in range(8)], [128, 8, L + 2], BF16)
                if isL:
                    pf_fill(0, wA[2])
                    pf_fill(1, wA[14])
                elif full:
                    pf_fill(0, wA[2])
                    pf_fill(1, wA[3])
                else:
                    pf_fill(0, wA[3])
                    pf_fill(1, wA[4])
                if full:
                    self.S.phase = kind + '_sgu'
                    self.phase_sgu(b, l, kind, hT, hkeys, hcols)
            self.S.barrier()
            if isL:
                self.chk('sgu')
            self.S.phase = kind + '_attn'
            self.phase_attn(b, l, kind, hT, hkeys, hcols, full)
            self.S.barrier()
            if isL:
                self.chk('attn')
            if not full:
                return
            with ExitStack() as esH:
                hyT = self.sb(esH, 'hyT', [128, 4, Ls], BF16)
                self.S.phase = kind + '_hyena'
                self.phase_hyena(b, l, kind, hT, hkeys, hcols, halo_ap, hyT)
                self.S.barrier()
                if isL:
                    self.chk('hyena')
                if isL and b == 0 and l == 0:
                    self.dump('hyT', hyT[:], [], [128, 4, L], BF16)
                mT = self.sb(esH, 'mT', [128, 8, Ls], BF16)
                self.S.phase = kind + '_merge1'
                self.phase_merge1(b, l, kind, hT, hkeys, hcols, hyT, mT)
                self.S.barrier()
                if isL:
                    self.chk('merge1')
                if isL and b == 0 and l == 0:
                    self.dump('mT', mT[:], [], [128, 8, L], BF16)
                self.S.phase = kind + '_merge2'
                self.phase_merge2(b, l, kind, hT, hkey, mT, xsrc, xsk, xdst, xdk, A2, B2, ABk, bcol)
                self.S.barrier()
                if isL:
                    self.chk('merge2')
            if isL and b == 0 and l == 0:
                self.dump('h2T', hT[:], [], [128, 8, L + 2], BF16)
            self.S.phase = kind + '_ffn'
            self.phase_ffn(b, l, kind, hT, hkeys, hcols, halo_ap, xdst, xdk, bcol, last_layer and isL)
            self.S.barrier()

    def phase_sgu(self, b, l, kind, hT, hkeys, hcols):
        P = self.P
        R = self.R
        isL = kind == 'L'
        Ls = L if isL else LC
        nt = Ls // 128
        TW = min(512, Ls)
        nT = Ls // TW
        spt = TW // 128
        wA = R['wA'][l]
        with ExitStack() as es:
            aT = self.sb(es, 'aT', [128, 4, Ls], BF16)
            wu = self.sb(es, 'wu', [128, 8, 512], BF16)
            wv = self.sb(es, 'wv', [128, 8, 512], BF16)
            vg = [self.sb(es, 'vg', [128, 512], F32) for _ in range(2)]
            vjs = [self.sb(es, 'vj', [128, 512], BF16) for _ in range(2)]
            vs = [self.sb(es, 'vs', [128, 512], BF16) for _ in range(2)]
            t1 = [self.sb(es, 't1', [128, 4, 128], F32) for _ in range(2)]
            svs = self.sb(es, 'sv', [128, 8], F32)
            self.dma(wu[:], wA[0], [('wscr',)], ['wu'], 'wu')
            self.dma(wv[:], wA[1], [('wscr',)], ['wv'], 'wv')
            cnt = 0
            for T in range(nT):
                for fc in range(4):
                    pi = cnt % 3
                    cnt += 1
                    pk = 'ps%d' % pi
                    for k in range(8):
                        self.mm(self.ps[pi][:, 0:TW], wu[:, k, fc * 128:(fc + 1) * 128], hT[:, k, hcols(T)], k == 0, k == 7,
                                ['wu'] + hkeys(T, k=k), [pk])
                    self.act(aT[:, fc, T * TW:(T + 1) * TW], self.ps[pi][:, 0:TW], AF.Gelu, [pk], [('aT', T, fc)])
                for s in range(spt):
                    i = T * spt + s
                    pi = 3 + (i % 2)
                    pk = 'ps%d' % pi
                    q = i % 2
                    for k in range(8):
                        self.mm(self.ps[pi][:, :], hT[:, k, 1 + i * 128:1 + (i + 1) * 128], wv[:, k, :], k == 0, k == 7,
                                ['wv', (hkeys(T, k=k)[s])], [pk])
                    self.act(vg[q][:], self.ps[pi][:, :], AF.Gelu, [pk], [('vg', q)])
                    sv = svs[:, 4 * q:4 * q + 4]
                    svk = ('sv', q)
                    vj = vjs[q]
                    self.memset('pool', sv[:, 0:1], 0.0, [svk])
                    self.act(vj[:], vg[q][:], AF.Square, [('vg', q), svk], [('vj', q), svk], accum=sv[:, 0:1])
                    self.ts('dve', sv[:, 1:2], sv[:, 0:1], 1.0 / 512, EPS, ALU.mult, ALU.add, [svk], [svk])
                    self.act(sv[:, 2:3], sv[:, 1:2], AF.Sqrt, [svk], [svk])
                    self.recip(sv[:, 3:4], sv[:, 2:3], [svk], [svk])
                    self.ts('dve', vs[q][:], vg[q][:], sv[:, 3:4], None, ALU.mult, None, [('vg', q), svk], [('vs', q)])
                    pj = 5 + (i % 2)
                    pjk = 'ps%d' % pj
                    for g in range(4):
                        self.mm(self.ps[pj][:, g * 128:(g + 1) * 128], vs[q][:, g * 128:(g + 1) * 128], P['wsT%d' % l][:, g, :],
                                True, True, [('vs', q), 'wsT%d' % l], [pjk])
                    self.tt('dve', t1[q][:], self.ps[pj][:, :].rearrange("p (g q) -> p g q", g=4),
                            P['gsnT%d' % l][:].unsqueeze(2).to_broadcast([128, 4, 128]), ALU.mult, [pjk, 'gsnT%d' % l], [('t1', q)])
                    self.tt('pool', t1[q][:], t1[q][:], P['bsb%d' % l][:], ALU.add, [('t1', q), 'bsb%d' % l], [('t1', q)])
                    akeys = [('aT', T, fc) for fc in range(4)]
                    self.tt('dve', aT[:, :, i * 128:(i + 1) * 128], aT[:, :, i * 128:(i + 1) * 128], t1[q][:], ALU.mult,
                            [('t1', q)] + akeys, [('aT2', i)])
            allk = [('aT', T, fc) for T in range(nT) for fc in range(4)] + [('aT2', i) for i in range(nt)]
            self.dma(R['sa' + kind], aT[:], allk, [('sa', kind)], 'sa_st')
            if isL and b == 0 and l == 0:
                self.dump('aT', aT[:], allk, [128, 4, L], BF16)

    def phase_attn(self, b, l, kind, hT, hkeys, hcols, full):
        P = self.P
        I = self.I
        R = self.R
        isL = kind == 'L'
        Ls = L if isL else LC
        nt = Ls // 128
        TW = min(512, Ls)
        nT = Ls // TW
        spt = TW // 128
        wA = R['wA'][l]
        with ExitStack() as es:
            es1 = es.enter_context(ExitStack())
            if isL:
                QT = self.sb(es, 'QTz', [128, NH, Ls], BF16)
                KT = self.sb(es, 'KT', [128, 4, Ls], BF16)
                V1 = self.sb(es, 'V1', [128, nt, NH, 65], BF16)
                tbl = self.sb(es, 'tbl', [128, NH, 27, 64], BF16)
                cosT = self.sb(es1, 'cosT', [128, L], F32)
                sinT = self.sb(es1, 'sinT', [128, L], F32)
                self.dma(cosT[:], I['cosT'], [], ['cosT'], 'cosT')
                self.dma(sinT[:], I['sinT'], [], ['sinT'], 'sinT')
                self.dma(tbl[:].rearrange("p h o q -> p (h o q)"), R['tbl'][l], [('tbl', l)], ['tbl'], 'tbl')
                self.memset('pool', V1[:], 1.0, ['V1'])
                self.memset('dve', QT[:], 0.0, ['QTz0'])
            else:
                KT = P['KcT']
                V1 = P['Vc1']
                if full:
                    QT = self.sb(es, 'QTc', [128, 4, Ls], BF16)
            wq = [self.sb(es1, 'wq', [128, 8, 512], BF16) for _ in range(2)]
            r1 = [self.sb(es1, 'r1', [128, 512], F32) for _ in range(2)]
            r2 = [self.sb(es1, 'r2', [128, 512], F32) for _ in range(2)]
            wcnt = [0]

            pfn = [0]

            def wload(g):
                if pfn[0] < 2:
                    i_ = pfn[0]
                    pfn[0] += 1
                    return self.PF[i_], ('PF', i_)
                s = wcnt[0] % 2
                wcnt[0] += 1
                self.dma(wq[s][:], wA[g], [('wscr',)], [('wq', s)], 'wq%d' % s)
                return wq[s], ('wq', s)

            pcnt = [0]

            def proj_fm(wb, wk, T, fc, pi):
                pk = 'ps%d' % pi
                for k in range(8):
                    self.mm(self.ps[pi][:, 0:TW], wb[:, k, fc * 128:(fc + 1) * 128], hT[:, k, hcols(T)], k == 0, k == 7,
                            [wk] + hkeys(T, k=k), [pk])
                return self.ps[pi][:, 0:TW], pk

            kname = 'KT' if isL else 'KcT'
            vname = 'V1' if isL else 'Vc1'
            for name, g, gsw, dstT in (('q', 2, 14, 'QT'), ('k', 3, 15, kname)):
                if name == 'q' and not full:
                    continue
                dst = QT if name == 'q' else KT
                wb, wk = wload(g)
                if isL:
                    wb2, wk2 = wload(gsw)
                for T in range(nT):
                    for fc in range(4):
                        pa, pak = proj_fm(wb, wk, T, fc, (pcnt[0] % 2) * 2)
                        if isL:
                            pb_, pbk = proj_fm(wb2, wk2, T, fc, (pcnt[0] % 2) * 2 + 1)
                            q = pcnt[0] % 2
                            self.tt('dve', r1[q][:], pa, cosT[:, T * TW:(T + 1) * TW], ALU.mult, [pak, 'cosT'], [('r1', q)])
                            self.tt('dve', r2[q][:], pb_, sinT[:, T * TW:(T + 1) * TW], ALU.mult, [pbk, 'sinT'], [('r2', q)])
                            if name == 'q':
                                for hp in range(2):
                                    self.tt('pool', dst[hp * 64:(hp + 1) * 64, 2 * fc + hp, T * TW:(T + 1) * TW],
                                            r1[q][hp * 64:(hp + 1) * 64, :], r2[q][hp * 64:(hp + 1) * 64, :], ALU.add,
                                            [('r1', q), ('r2', q), 'QTz0'], [(dstT, T, fc, hp)])
                            else:
                                self.tt('pool', dst[:, fc, T * TW:(T + 1) * TW], r1[q][:], r2[q][:], ALU.add, [('r1', q), ('r2', q)],
                                        [(dstT, T, fc)])
                        else:
                            self.cp('act', dst[:, fc, T * TW:(T + 1) * TW], pa, [pak], [(dstT, T, fc)])
                        pcnt[0] += 1
            wb, wk = wload(4)
            for i in range(nt):
                pi = 4 + (i % 2)
                pk = 'ps%d' % pi
                for k in range(8):
                    self.mm(self.ps[pi][:, :], hT[:, k, 1 + i * 128:1 + (i + 1) * 128], wb[:, k, :], k == 0, k == 7,
                            [wk, hkeys(i // spt, k=k)[i % spt]], [pk])
                self.cp('act', V1[:, i, :, 0:64], self.ps[pi][:, :].rearrange("p (h d) -> p h d", h=NH), [pk, vname], [(vname, i)])
            if not full:
                return
            if isL:
                QTk = [('QT', T, fc, hp) for T in range(nT) for fc in range(4) for hp in range(2)]
            else:
                QTk = [('QT', T, fc) for T in range(nT) for fc in range(4)]
            KTk = [(kname, T, fc) for T in range(nT) for fc in range(4)]
            if isL and b == 0 and l == 0:
                self.dump('QT', QT[:], QTk, [128, NH, L], BF16)
                self.dump('KT', KT[:], KTk, [128, 4, L], BF16)
            self.S.barrier()
            es1.close()
            self.pf_fill(0, wA[5])
            self.pf_fill(1, wA[6])
            self.S.phase = kind + '_attn2'
            attnT = self.sb(es, 'attnT', [128, 4, Ls], BF16)
            Pb = [self.sb(es, 'Pb', [128, 896], BF16) for _ in range(3)]
            atok = [self.sb(es, 'atok', [128, 512], BF16) for _ in range(2)]
            rc = [self.sb(es, 'rc', [128, 8], F32) for _ in range(2)]
            KcT = P['KcT']
            Vc1 = P['Vc1']
            nqb = Ls // 128
            sc = 0
            pc_ = 0
            for pr in range(nqb):
                r = 2 * pr
                tiles = []
                interior = False
                if isL:
                    if 4 <= r <= 26:
                        interior = True
                        for t in range(5):
                            tiles.append(('loc', (r - 4) // 2 + t, None))
                    else:
                        j0 = 0 if r < 4 else 12
                        for t in range(4):
                            j = j0 + t
                            tiles.append(('loc', j, (2 * j - r + 7, 2 * j - (r + 1) + 7)))
                tiles.append(('ctx', 0, None))
                tiles.append(('ctx', 1, None))
                ntl = len(tiles)
                rp = pr % 2
                OA, OB = self.ps[4 + 2 * rp], self.ps[5 + 2 * rp]
                OAk, OBk = 'ps%d' % (4 + 2 * rp), 'ps%d' % (5 + 2 * rp)
                for h in range(NH):
                    c = h // 2
                    p0 = (h % 2) * 64
                    st_ = sc % 2
                    sc += 1
                    SB_ = (self.ps[2 * st_], self.ps[2 * st_ + 1])
                    SK_ = ('ps%d' % (2 * st_), 'ps%d' % (2 * st_ + 1))
                    Tq = (r * 64) // TW
                    if isL:
                        qap = QT[:, h, r * 64:r * 64 + 128]
                        qk = [('QT', Tq, c, h % 2)]
                        ksl = slice(0, 128)
                    else:
                        qap = QT[p0:p0 + 64, c, r * 64:r * 64 + 128]
                        qk = [('QT', Tq, c)]
                        ksl = slice(p0, p0 + 64)
                    if interior:
                        self.mm(SB_[0][:, 0:512], P['identb'][:], tbl[:, h, 17:25, :].rearrange("p a q -> p (a q)"), True, False,
                                ['identb', 'tbl'], [SK_[0]])
                        self.mm(SB_[1][:, 0:128], P['identb'][:], tbl[:, h, 25:27, :].rearrange("p a q -> p (a q)"), True, False,
                                ['identb', 'tbl'], [SK_[1]])
                    for ti, (ty, j, bspec) in enumerate(tiles):
                        Sb, Sk = SB_[ti // 4], SK_[ti // 4]
                        co = (ti % 4) * 128
                        if ty == 'loc':
                            kap = KT[ksl, c, j * 128:(j + 1) * 128]
                            kk = [('KT', (j * 128) // TW, c)]
                            if interior:
                                self.mm(Sb[:, co:co + 128], kap, qap, False, (ti == 3 or ti == 4), kk + qk, [Sk])
                            else:
                                self.mm(Sb[:, co:co + 128], kap, qap, True, False, kk + qk, [Sk])
                                for a_ in range(2):
                                    self.mm(Sb[:, co + a_ * 64:co + (a_ + 1) * 64], P['identb'][:], tbl[:, h, bspec[a_], :], False, a_ == 1,
                                            ['identb', 'tbl'], [Sk])
                        else:
                            kap = KcT[ksl, c, j * 128:(j + 1) * 128]
                            kk = [('KcT', 0, c)]
                            self.mm(Sb[:, co:co + 128], kap, qap, True, True, kk + qk, [Sk])
                    pq = pc_ % 3
                    pc_ += 1
                    na = min(ntl, 4) * 128
                    self.act(Pb[pq][:, 0:na], SB_[0][:, 0:na], AF.Exp, [SK_[0]], [('Pb', pq, 0)], scale=0.125)
                    pkeys = [('Pb', pq, 0)]
                    if ntl > 4:
                        nb_ = (ntl - 4) * 128
                        self.act(Pb[pq][:, 512:512 + nb_], SB_[1][:, 0:nb_], AF.Exp, [SK_[1]], [('Pb', pq, 1)], scale=0.125)
                        pkeys.append(('Pb', pq, 1))
                    O = OA if h < 4 else OB
                    Ok = OAk if h < 4 else OBk
                    hh = h % 4
                    for ti, (ty, j, bspec) in enumerate(tiles):
                        if ty == 'loc':
                            vap = V1[:, j, h, :]
                            vk = [('V1', j)]
                        else:
                            vap = Vc1[:, j, h, :]
                            vk = [('Vc1', j)]
                        self.mm(O[:, hh * 65:(hh + 1) * 65], Pb[pq][:, ti * 128:(ti + 1) * 128], vap, ti == 0, ti == ntl - 1,
                                [pkeys[ti // 4]] + vk, [Ok])
                q = pr % 2
                for half, (O, Ok) in enumerate(((OA, OAk), (OB, OBk))):
                    o3 = O[:, 0:260].rearrange("p (h e) -> p h e", h=4)
                    self.recip(rc[q][:, half * 4:(half + 1) * 4].unsqueeze(2), o3[:, :, 64:65], [Ok], [('rc', q, half)])
                    self.tt('dve', atok[q][:, half * 256:(half + 1) * 256].rearrange("p (h d) -> p h d", h=4), o3[:, :, 0:64],
                            rc[q][:, half * 4:(half + 1) * 4].unsqueeze(2).to_broadcast([128, 4, 64]), ALU.mult,
                            [Ok, ('rc', q, half)], [('atok', q, half)])
                st_ = sc % 2
                sc += 1
                Tb = self.ps[2 * st_][:].bitcast(BF16)
                Tk = 'ps%d' % (2 * st_)
                for c in range(4):
                    self.tr(Tb[:, c * 128:(c + 1) * 128], atok[q][:, c * 128:(c + 1) * 128], P['identb'][:],
                            [('atok', q, 0), ('atok', q, 1), 'identb'], [Tk])
                self.cp('act', attnT[:, :, pr * 128:(pr + 1) * 128], Tb[:, 0:512].rearrange("p (c t) -> p c t", c=4), [Tk],
                        [('attnT', pr)])
            allk = [('attnT', r) for r in range(nqb)]
            self.dma(R['satt' + kind], attnT[:], allk, [('satt', kind)], 'satt_st')
            if isL and b == 0 and l == 0:
                self.dump('attnT', attnT[:], allk, [128, 4, L], BF16)

    def phase_hyena(self, b, l, kind, hT, hkeys, hcols, halo_ap, hyT):
        P = self.P
        I = self.I
        R = self.R
        isL = kind == 'L'
        Ls = L if isL else LC
        nt = Ls // 128
        TW = min(512, Ls)
        nT = Ls // TW
        spt = TW // 128
        wA = R['wA'][l]
        hcw = P['hcw%d' % l]
        spec = R['spec' + kind][l]
        with ExitStack() as es:
            ztok = self.sb(es, 'ztok', [128, nt, 512], BF16)
            x1tok = self.sb(es, 'x1tok', [128, nt, 512], BF16)
            Yr = self.sb(es, 'Yr', [128, nt, 512], BF16)
            Ys = self.sb(es, 'Ys', [128, nt, 512], BF16)
            cb = [self.sb(es, 'cb', [128, nt, 128], BF16) for _ in range(2)]
            sbf_ = [self.sb(es, 'sbf', [128, nt, 128], BF16) for _ in range(2)]
            kk_ = [self.sb(es, 'kk', [128, 2, 512], F32) for _ in range(2)]
            with ExitStack() as es2:
                wh = [self.sb(es2, 'wh', [128, 8, 512], BF16) for _ in range(2)]
                pbuf = [self.sb(es2, 'pbuf', [128, 516], F32) for _ in range(2)]
                cacc = [self.sb(es2, 'cacc', [128, 512], F32) for _ in range(2)]
                cfm = [self.sb(es2, 'cfm', [128, 512], BF16) for _ in range(2)]
                cnt = 0
                for gi, g in enumerate((5, 6, 7)):
                    s = gi % 2
                    if gi < 2:
                        whb, whk = self.PF[gi], ('PF', gi)
                    else:
                        self.dma(wh[0][:], wA[g], [('wscr',)], [('wh', 0)], 'wh0')
                        whb, whk = wh[0], ('wh', 0)
                    for T in range(nT):
                        for fc in range(4):
                            ch = gi * 4 + fc
                            q = cnt % 2
                            cnt += 1
                            pi = q * 2
                            pk, pk2 = 'ps%d' % pi, 'ps%d' % (pi + 1)
                            for k in range(8):
                                self.mm(self.ps[pi][:, 0:TW], whb[:, k, fc * 128:(fc + 1) * 128], hT[:, k, hcols(T)], k == 0, k == 7,
                                        [whk] + hkeys(T, k=k), [pk])
                            for k in range(8):
                                self.mm(self.ps[pi + 1][:, 0:2], whb[:, k, fc * 128:(fc + 1) * 128], halo_ap(k, T), k == 0, k == 7,
                                        [whk] + hkeys(T, True, k), [pk2])
                            self.cp('act', pbuf[q][:, 1:1 + TW], self.ps[pi][:, 0:TW], [pk], [('pbuf', q)])
                            hdst = bass.AP(pbuf[q][:, 0:1].tensor, pbuf[q][:, 0:1].offset,
                                           [list(pbuf[q][:, 0:1].ap[0]), [TW + 1, 2]])
                            self.cp('act', hdst, self.ps[pi + 1][:, 0:2], [pk2, ('pbuf', q)], [('pbuf', q)])
                            self.ts('dve', cacc[q][:, 0:TW], pbuf[q][:, 1:1 + TW], hcw[:, ch, 1:2], hcw[:, ch, 3:4], ALU.mult, ALU.add,
                                    [('pbuf', q), 'hcw%d' % l], [('cacc', q)])
                            self.stt('dve', cacc[q][:, 0:TW], pbuf[q][:, 0:TW], hcw[:, ch, 0:1], cacc[q][:, 0:TW], ALU.mult, ALU.add,
                                     [('pbuf', q), 'hcw%d' % l, ('cacc', q)], [('cacc', q)])
                            if gi == 2:
                                self.stt('dve', hyT[:, fc, T * TW:(T + 1) * TW], pbuf[q][:, 2:2 + TW], hcw[:, ch, 2:3], cacc[q][:, 0:TW],
                                         ALU.mult, ALU.add, [('pbuf', q), 'hcw%d' % l, ('cacc', q)], [('x2T', T, fc)])
                            else:
                                self.stt('dve', cfm[q][:, 0:TW], pbuf[q][:, 2:2 + TW], hcw[:, ch, 2:3], cacc[q][:, 0:TW],
                                         ALU.mult, ALU.add, [('pbuf', q), 'hcw%d' % l, ('cacc', q)], [('cfm', q)])
                                pt = 4 + q
                                ptk = 'ps%d' % pt
                                Tb = self.ps[pt][:].bitcast(BF16)
                                for s4 in range(spt):
                                    self.tr(Tb[:, s4 * 128:(s4 + 1) * 128], cfm[q][:, s4 * 128:(s4 + 1) * 128], P['identb'][:],
                                            [('cfm', q), 'identb'], [ptk])
                                dstt = ztok if gi == 0 else x1tok
                                dk = 'ztok' if gi == 0 else 'x1tok'
                                self.cp('act', dstt[:, T * spt:(T + 1) * spt, fc * 128:(fc + 1) * 128],
                                        Tb[:, 0:spt * 128].rearrange("p (s c) -> p s c", s=spt), [ptk],
                                        [(dk, T * spt + s4, fc) for s4 in range(spt)])
            self.dma(cb[0][:], I['cf' + kind][0], [], [('cb', 0)], 'cb0')
            self.dma(sbf_[0][:], I['sf' + kind][0], [], [('sbf', 0)], 'sbf0')
            self.dma(kk_[0][:].rearrange("p a c -> p (a c)"), spec[0, 0], [('spec', kind, l)], [('kk', 0)], 'kk0')
            self.S.barrier()
            self.pf_fill(0, wA[8])
            self.pf_fill(1, wA[10])
            self.S.phase = kind + '_hyena2'
            zk = [('ztok', i, fc) for i in range(nt) for fc in range(4)]
            x1k = [('x1tok', i, fc) for i in range(nt) for fc in range(4)]
            if isL and b == 0 and l == 0:
                self.dump('ztok', ztok[:], zk, [128, nt, 512], BF16)
                self.dump('x1tok', x1tok[:], x1k, [128, nt, 512], BF16)
                self.dump('x2T', hyT[:], [('x2T', T, fc) for T in range(nT) for fc in range(4)], [128, 4, L], BF16)
            with ExitStack() as es3:
                zs = [self.sb(es3, 'zs', [128, 512], F32) for _ in range(2)]
                ta = [self.sb(es3, 'ta', [128, 512], F32) for _ in range(2)]
                tb_ = [self.sb(es3, 'tb', [128, 512], F32) for _ in range(2)]
                tc_ = [self.sb(es3, 'tc', [128, 512], F32) for _ in range(2)]
                td = [self.sb(es3, 'td', [128, 512], F32) for _ in range(2)]
                ytok = [self.sb(es3, 'ytok', [128, 512], BF16) for _ in range(2)]
                cntl = 0
                for o in range(2):
                    zin_keys = zk if o == 0 else [('z1', i) for i in range(nt)]
                    Yk = []
                    for m in range(nt):
                        s = cntl % 2
                        cntl += 1
                        if not (o == 0 and m == 0):
                            self.dma(cb[s][:], I['cf' + kind][m], [], [('cb', s)], 'cb%d' % s)
                            self.dma(sbf_[s][:], I['sf' + kind][m], [], [('sbf', s)], 'sbf%d' % s)
                            self.dma(kk_[s][:].rearrange("p a c -> p (a c)"), spec[o, m], [('spec', kind, l)], [('kk', s)], 'kk%d' % s)
                        pr, psn = self.ps[s * 2], self.ps[s * 2 + 1]
                        prk, psk = 'ps%d' % (s * 2), 'ps%d' % (s * 2 + 1)
                        for t_ in range(nt):
                            self.mm(pr[:, :], cb[s][:, t_, :], ztok[:, t_, :], t_ == 0, t_ == nt - 1, [('cb', s)] + zin_keys, [prk])
                        for t_ in range(nt):
                            self.mm(psn[:, :], sbf_[s][:, t_, :], ztok[:, t_, :], t_ == 0, t_ == nt - 1, [('sbf', s)] + zin_keys, [psk])
                        self.cp('act', zs[s][:], psn[:, :], [psk], [('zs', s)])
                        self.tt('dve', ta[s][:], pr[:, :], kk_[s][:, 0, :], ALU.mult, [prk, ('kk', s)], [('ta', s)])
                        self.tt('pool', tb_[s][:], zs[s][:], kk_[s][:, 1, :], ALU.mult, [('zs', s), ('kk', s)], [('tb', s)])
                        self.tt('dve', Yr[:, m, :], ta[s][:], tb_[s][:], ALU.subtract, [('ta', s), ('tb', s)], [('Yr', o, m)])
                        self.tt('dve', tc_[s][:], pr[:, :], kk_[s][:, 1, :], ALU.mult, [prk, ('kk', s)], [('tc', s)])
                        self.tt('pool', td[s][:], zs[s][:], kk_[s][:, 0, :], ALU.mult, [('zs', s), ('kk', s)], [('td', s)])
                        self.tt('pool', Ys[:, m, :], tc_[s][:], td[s][:], ALU.add, [('tc', s), ('td', s)], [('Ys', o, m)])
                        Yk.append(('Yr', o, m))
                        Yk.append(('Ys', o, m))
                    for m in range(nt):
                        s = cntl % 2
                        cntl += 1
                        self.dma(cb[s][:], I['ci' + kind][m], [], [('cb', s)], 'cb%d' % s)
                        self.dma(sbf_[s][:], I['si' + kind][m], [], [('sbf', s)], 'sbf%d' % s)
                        py = self.ps[4 + s]
                        pyk = 'ps%d' % (4 + s)
                        for t_ in range(nt):
                            self.mm(py[:, :], cb[s][:, t_, :], Yr[:, t_, :], t_ == 0, False, [('cb', s)] + Yk, [pyk])
                        for t_ in range(nt):
                            self.mm(py[:, :], sbf_[s][:, t_, :], Ys[:, t_, :], False, t_ == nt - 1, [('sbf', s)] + Yk, [pyk])
                        if o == 0:
                            self.tt('dve', ztok[:, m, :], py[:, :], x1tok[:, m, :], ALU.mult, [pyk] + x1k + zk, [('z1', m)] + [('ztok', m, fc) for fc in range(4)])
                        else:
                            self.cp('act', ytok[s][:], py[:, :], [pyk], [('ytok', s)])
                            pt = 6 + s
                            ptk = 'ps%d' % pt
                            Tb = self.ps[pt][:].bitcast(BF16)
                            for cc in range(4):
                                self.tr(Tb[:, cc * 128:(cc + 1) * 128], ytok[s][:, cc * 128:(cc + 1) * 128], P['identb'][:],
                                        [('ytok', s), 'identb'], [ptk])
                            T_ = (m * 128) // TW
                            self.tt('dve', hyT[:, :, m * 128:(m + 1) * 128], Tb[:, 0:512].rearrange("p (c t) -> p c t", c=4),
                                    hyT[:, :, m * 128:(m + 1) * 128], ALU.mult, [ptk] + [('x2T', T_, fc) for fc in range(4)],
                                    [('hyT', m)] + [('x2T', T_, fc) for fc in range(4)])
                    if o == 0 and isL and b == 0 and l == 0:
                        self.dump('z1tok', ztok[:], [('z1', i) for i in range(nt)], [128, nt, 512], BF16)

    def phase_merge1(self, b, l, kind, hT, hkeys, hcols, hyT, mT):
        P = self.P
        R = self.R
        isL = kind == 'L'
        Ls = L if isL else LC
        TW = min(512, Ls)
        nT = Ls // TW
        wA = R['wA'][l]
        wB = R['wB'][l]
        with ExitStack() as es:
            aT = self.sb(es, 'aTm', [128, 4, Ls], BF16)
            attnT = self.sb(es, 'attnTm', [128, 4, Ls], BF16)
            wg = [self.sb(es, 'wg', [128, 8, 512], BF16) for _ in range(3)]
            wb = [self.sb(es, 'wbr', [128, 4, 512], BF16) for _ in range(3)]
            sg = [self.sb(es, 'sg', [128, 512], BF16) for _ in range(3)]
            m1 = [self.sb(es, 'm1', [128, 512], F32) for _ in range(2)]
            m2 = [self.sb(es, 'm2', [128, 512], F32) for _ in range(2)]
            for T in range(nT):
                self.dma(aT[:, :, T * TW:(T + 1) * TW], R['sa' + kind][:, :, T * TW:(T + 1) * TW], [('sa', kind)], [('aTm', T)],
                         'aTm%d' % T)
                self.dma(attnT[:, :, T * TW:(T + 1) * TW], R['satt' + kind][:, :, T * TW:(T + 1) * TW], [('satt', kind)],
                         [('attnTm', T)], 'attnTm%d' % T)
            srcs = ((aT, 'aTm'), (attnT, 'attnTm'), (hyT, None))
            cnt = 0
            for fq in range(2):
                wgb = []
                for gi in range(3):
                    if fq == 0 and gi < 2:
                        wgb.append((self.PF[gi], ('PF', gi)))
                    else:
                        self.dma(wg[gi][:], wA[8 + 2 * gi + fq], [('wscr',)], [('wg', gi)], 'wg%d' % gi)
                        wgb.append((wg[gi], ('wg', gi)))
                    self.dma(wb[gi][:], wB[gi, fq], [('wscr',)], [('wbr', gi)], 'wbr%d' % gi)
                if fq == 1:
                    self.pf_fill(0, R['wO'][l][0])
                    self.pf_fill(1, R['wO'][l][1])
                for f4 in range(4):
                    fc = fq * 4 + f4
                    for T in range(nT):
                        q = cnt % 2
                        cnt += 1
                        for gi in range(3):
                            pg = gi
                            pgk = 'ps%d' % pg
                            for k in range(8):
                                self.mm(self.ps[pg][:, 0:TW], wgb[gi][0][:, k, f4 * 128:(f4 + 1) * 128], hT[:, k, hcols(T)], k == 0, k == 7,
                                        [wgb[gi][1]] + hkeys(T, k=k), [pgk])
                            self.act(sg[gi][:, 0:TW], self.ps[pg][:, 0:TW], AF.Sigmoid, [pgk], [('sg', gi)])
                        for gi in range(3):
                            pbn = 3 + gi
                            pbk = 'ps%d' % pbn
                            src, sk = srcs[gi]
                            if sk is None:
                                skeys = [('hyT', i) for i in range(Ls // 128)]
                            else:
                                skeys = [(sk, T)]
                            for k in range(4):
                                self.mm(self.ps[pbn][:, 0:TW], wb[gi][:, k, f4 * 128:(f4 + 1) * 128], src[:, k, T * TW:(T + 1) * TW],
                                        k == 0, k == 3, [('wbr', gi)] + skeys, [pbk])
                        self.tt('dve', m1[q][:, 0:TW], self.ps[3][:, 0:TW], sg[0][:, 0:TW], ALU.mult, ['ps3', ('sg', 0)], [('m1', q)])
                        self.tt('dve', m2[q][:, 0:TW], self.ps[4][:, 0:TW], sg[1][:, 0:TW], ALU.mult, ['ps4', ('sg', 1)], [('m2', q)])
                        self.tt('pool', m1[q][:, 0:TW], m1[q][:, 0:TW], m2[q][:, 0:TW], ALU.add, [('m1', q), ('m2', q)], [('m1', q)])
                        self.tt('dve', m2[q][:, 0:TW], self.ps[5][:, 0:TW], sg[2][:, 0:TW], ALU.mult, ['ps5', ('sg', 2), ('m2', q)],
                                [('m2', q)])
                        self.tt('pool', mT[:, fc, T * TW:(T + 1) * TW], m1[q][:, 0:TW], m2[q][:, 0:TW], ALU.add, [('m1', q), ('m2', q)],
                                [('mT', T, fc)])

    def phase_merge2(self, b, l, kind, hT, hkey, mT, xsrc, xsk, xdst, xdk, A2, B2, ABk, bcol):
        P = self.P
        R = self.R
        isL = kind == 'L'
        Ls = L if isL else LC
        nt = Ls // 128
        TW = min(512, Ls)
        wO = R['wO'][l]
        with ExitStack() as es:
            g1bc = self.sb(es, 'g1bc', [128, D], F32)
            xt = [self.sb(es, 'xtm', [128, D], F32) for _ in range(3)]
            tmp = [self.sb(es, 'tmpm', [128, 512], F32) for _ in range(2)]
            junk = [self.sb(es, 'junk2', [128, D], BF16) for _ in range(2)]
            xn = [self.sb(es, 'xn2', [128, D], BF16) for _ in range(2)]
            tmpf = None
            self.dma(g1bc[:], R['modrow'][l][bcol:bcol + 1, 2 * D:3 * D].partition_broadcast(128), [('modrow', l)], ['g1bc'], 'g1bc')
            for i in range(nt):
                s = i % 3
                T = (i * 128) // TW
                self.dma(xt[s][:], xsrc[i * 128:(i + 1) * 128, :], [xsk + (i,)], [('xtm', s)], 'xtm%d' % s)
                for half in range(2):
                    pi = 2 + half + 2 * (i % 2)
                    pk = 'ps%d' % pi
                    for k in range(8):
                        self.mm(self.ps[pi][:, :], mT[:, k, i * 128:(i + 1) * 128], self.PF[half][:, k, :], k == 0, k == 7,
                                [('PF', half)] + [('mT', T, k)], [pk])
                    self.tt('dve', tmp[half][:], self.ps[pi][:, :], g1bc[:, half * 512:(half + 1) * 512], ALU.mult, [pk, 'g1bc'],
                            [('tmpm', half)])
                    self.tt('pool', xt[s][:, half * 512:(half + 1) * 512], xt[s][:, half * 512:(half + 1) * 512], tmp[half][:], ALU.add,
                            [('tmpm', half), ('xtm', s)], [('xtm', s)])
                self.dma(xdst[i * 128:(i + 1) * 128, :], xt[s][:], [('xtm', s)], [xdk + (i,)], 'xtm_st%d' % s)
                self.norm_tile((junk, xn, tmpf, i % 2), xt[s][:], ('xtm', s), A2, B2, ABk, hT, hkey, i)
            self.pf_fill(0, R['wU'][l][0])
            self.pf_fill(1, R['wU'][l][5])

    def phase_ffn(self, b, l, kind, hT, hkeys, hcols, halo_ap, xdst, xdk, bcol, final):
        P = self.P
        R = self.R
        isL = kind == 'L'
        Ls = L if isL else LC
        nt = Ls // 128
        TW = min(512, Ls)
        nT = Ls // TW
        spt = TW // 128
        wU = R['wU'][l]
        wD = R['wD'][l]
        fcw = P['fcw%d' % l]
        with ExitStack() as es:
            wd = self.sb(es, 'wd', [128, 2, 22, 512], BF16)
            g2bc = self.sb(es, 'g2bc', [128, D], F32)
            G = self.sb(es, 'G', [128, 22, TW], BF16)
            wa = [self.sb(es, 'wa', [128, 8, 512], BF16) for _ in range(2)]
            wbb = [self.sb(es, 'wbb', [128, 8, 512], BF16) for _ in range(2)]
            pbuf = [self.sb(es, 'pbuf2', [128, 516], F32) for _ in range(2)]
            cacc = [self.sb(es, 'cacc2', [128, 512], F32) for _ in range(2)]
            ga = [self.sb(es, 'ga', [128, 512], F32) for _ in range(2)]
            xt = [self.sb(es, 'xtf', [128, D], F32) for _ in range(2)]
            tmp = [self.sb(es, 'tmpf3', [128, 512], F32) for _ in range(2)]
            sf_ = self.sb(es, 'sfin', [128, 8], F32)
            junk = self.sb(es, 'junk3', [128, D], BF16)
            for half in range(2):
                self.dma(wd[:, half], wD[half], [('wscr',)], [('wd', half)], 'wd%d' % half)
            self.dma(g2bc[:], R['modrow'][l][bcol:bcol + 1, 5 * D:6 * D].partition_broadcast(128), [('modrow', l)], ['g2bc'], 'g2bc')
            cur = {'a': (None, None), 'b': (None, None)}
            lc = {'a': 0, 'b': 0}

            pfused = {'a': False, 'b': False}

            def getw(which, g):
                bufs = wa if which == 'a' else wbb
                if not pfused[which]:
                    pi_ = 0 if which == 'a' else 1
                    if cur[which][0] is None:
                        cur[which] = (g, 'pf')
                    if cur[which] == (g, 'pf'):
                        return self.PF[pi_], ('PF', pi_)
                    pfused[which] = True
                if cur[which][0] != g or cur[which][1] == 'pf':
                    s = lc[which] % 2
                    lc[which] += 1
                    self.dma(bufs[s][:], wU[g], [('wscr',)], [('w' + which, s)], 'wu%s%d' % (which, s))
                    cur[which] = (g, s)
                s = cur[which][1]
                return bufs[s], ('w' + which, s)
            cnt = 0
            xcnt = 0
            for T in range(nT):
                for fc in range(22):
                    q = cnt % 2
                    cnt += 1
                    ba, bak = getw('a', fc // 4)
                    bb_, bbk = getw('b', (22 + fc) // 4)
                    ca = (fc % 4) * 128
                    cbo = ((22 + fc) % 4) * 128
                    pa, ph, pb_ = self.ps[q * 3], self.ps[q * 3 + 1], self.ps[q * 3 + 2]
                    pak, phk, pbk = 'ps%d' % (q * 3), 'ps%d' % (q * 3 + 1), 'ps%d' % (q * 3 + 2)
                    for k in range(8):
                        self.mm(pa[:, 0:TW], ba[:, k, ca:ca + 128], hT[:, k, hcols(T)], k == 0, k == 7, [bak] + hkeys(T, k=k), [pak])
                    for k in range(8):
                        self.mm(ph[:, 0:2], ba[:, k, ca:ca + 128], halo_ap(k, T), k == 0, k == 7, [bak] + hkeys(T, True, k), [phk])
                    for k in range(8):
                        self.mm(pb_[:, 0:TW], bb_[:, k, cbo:cbo + 128], hT[:, k, hcols(T)], k == 0, k == 7, [bbk] + hkeys(T, k=k), [pbk])
                    self.cp('act', pbuf[q][:, 1:1 + TW], pa[:, 0:TW], [pak], [('pbuf2', q)])
                    hdst = bass.AP(pbuf[q][:, 0:1].tensor, pbuf[q][:, 0:1].offset, [list(pbuf[q][:, 0:1].ap[0]), [TW + 1, 2]])
                    self.cp('act', hdst, ph[:, 0:2], [phk, ('pbuf2', q)], [('pbuf2', q)])
                    self.ts('dve', cacc[q][:, 0:TW], pbuf[q][:, 1:1 + TW], fcw[:, fc, 1:2], fcw[:, fc, 3:4], ALU.mult, ALU.add,
                            [('pbuf2', q), 'fcw%d' % l], [('cacc2', q)])
                    self.stt('dve', cacc[q][:, 0:TW], pbuf[q][:, 0:TW], fcw[:, fc, 0:1], cacc[q][:, 0:TW], ALU.mult, ALU.add,
                             [('pbuf2', q), 'fcw%d' % l, ('cacc2', q)], [('cacc2', q)])
                    self.stt('dve', cacc[q][:, 0:TW], pbuf[q][:, 2:2 + TW], fcw[:, fc, 2:3], cacc[q][:, 0:TW], ALU.mult, ALU.add,
                             [('pbuf2', q), 'fcw%d' % l, ('cacc2', q)], [('cacc2', q)])
                    self.act(ga[q][:, 0:TW], cacc[q][:, 0:TW], AF.Gelu, [('cacc2', q)], [('ga', q)])
                    self.tt('dve', G[:, fc, :], ga[q][:, 0:TW], pb_[:, 0:TW], ALU.mult, [('ga', q), pbk], [('G', fc)])
                Gk = [('G', fc) for fc in range(22)]
                for s4 in range(spt):
                    i = T * spt + s4
                    s = xcnt % 2
                    xcnt += 1
                    self.dma(xt[s][:], xdst[i * 128:(i + 1) * 128, :], [xdk + (i,)], [('xtf', s)], 'xtf%d' % s)
                    for half in range(2):
                        pi = 6 + half
                        pk = 'ps%d' % pi
                        for k in range(22):
                            self.mm(self.ps[pi][:, :], G[:, k, s4 * 128:(s4 + 1) * 128], wd[:, half, k, :], k == 0, k == 21,
                                    [('wd', half)] + Gk, [pk])
                        self.tt('dve', tmp[half][:], self.ps[pi][:, :], g2bc[:, half * 512:(half + 1) * 512], ALU.mult, [pk, 'g2bc'],
                                [('tmpf3', half)])
                        self.tt('pool', xt[s][:, half * 512:(half + 1) * 512], xt[s][:, half * 512:(half + 1) * 512], tmp[half][:],
                                ALU.add, [('tmpf3', half), ('xtf', s)], [('xtf', s)])
                    if final:
                        self.memset('pool', sf_[:, 0:1], 0.0, ['sfin'])
                        self.act(junk[:], xt[s][:], AF.Square, [('xtf', s), 'sfin'], ['junk3', 'sfin'], accum=sf_[:, 0:1])
                        self.ts('dve', sf_[:, 1:2], sf_[:, 0:1], 1.0 / D, EPS, ALU.mult, ALU.add, ['sfin'], ['sfin'])
                        self.act(sf_[:, 2:3], sf_[:, 1:2], AF.Sqrt, ['sfin'], ['sfin'])
                        self.recip(sf_[:, 3:4], sf_[:, 2:3], ['sfin'], ['sfin'])
                        self.stt('dve', xt[s][:], xt[s][:], sf_[:, 3:4], P['fgbc'][:], ALU.mult, ALU.mult, [('xtf', s), 'sfin', 'fgbc'],
                                 [('xtf', s)])
                        self.dma(self.out[b, i * 128:(i + 1) * 128, :], xt[s][:], [('xtf', s)], [('out', b, i)], 'xtf_st%d' % s)
                    else:
                        self.dma(xdst[i * 128:(i + 1) * 128, :], xt[s][:], [('xtf', s)], [xdk + (i,)], 'xtf_st%d' % s)


def _prep_inputs(inp, NB, core):
    C = _consts()
    f32 = np.float32
    b0 = core * NB
    m = {}
    m['x'] = np.ascontiguousarray(inp['x'][b0:b0 + NB])
    m['ctx'] = np.ascontiguousarray(inp['ctx'][b0:b0 + NB])
    m['c5'] = np.ascontiguousarray(np.concatenate([inp['c'][b0:b0 + NB], inp['c_ctx'][None, :]], axis=0))
    return m


def _shared_inputs(inp):
    C = _consts()
    perm = _swap_perm()
    w_in = inp['w_in']
    w_ext = np.concatenate([w_in, w_in[:, :, 1024 + perm], w_in[:, :, 1536 + perm]], axis=2)
    m = {}
    m['ada_w'] = inp['ada_w']
    m['ada_b'] = inp['ada_b']
    m['norm1_g'] = inp['norm1_g']
    m['norm2_g'] = inp['norm2_g']
    m['w_in'] = np.ascontiguousarray(w_ext)
    m['sgu_norm_g'] = inp['sgu_norm_g']
    m['sgu_w'] = inp['sgu_w']
    m['sgu_b'] = np.ascontiguousarray(inp['sgu_b'].reshape(2, 512))
    m['rpbg'] = np.ascontiguousarray(_rpb_gather(inp['na_rpb']).reshape(2, 128, NH * 14 * 64))
    m['hy_conv'] = np.ascontiguousarray(np.concatenate([inp['hy_conv_w'], inp['hy_conv_b'][:, None, :]], axis=1))
    m['hy_p3'] = np.ascontiguousarray(np.stack([inp['hy_filt_b1'], inp['hy_filt_b2'], inp['hy_sin_freq']], axis=1))
    m['hy_w1'] = inp['hy_filt_w1']
    m['hy_w2'] = inp['hy_filt_w2']
    m['hy_w3'] = inp['hy_filt_w3']
    m['hy_skip'] = inp['hy_skip']
    m['w_branch'] = inp['w_branch']
    m['w_out'] = inp['w_out']
    m['ffn_up'] = inp['ffn_w_up']
    m['ffn_conv'] = np.ascontiguousarray(np.concatenate([inp['ffn_conv_w'], inp['ffn_conv_b'][:, None, :]], axis=1))
    m['ffn_down'] = inp['ffn_w_down']
    m['final_g'] = np.ascontiguousarray(inp['final_norm_g'][None, :])
    m['cosT'] = C['cosT']
    m['sinT'] = C['sinT']
    m['m8'] = C['m8']
    m['mn8'] = C['mn8']
    for kind, key in (('L', 'hyL'), ('C', 'hyC')):
        h = C[key]
        m['zT' + kind] = h['zT']
        m['decay' + kind] = h['decay']
        for nm in ('cf', 'sf', 'ci', 'si'):
            m[nm + kind] = h[nm]
    return {k: np.ascontiguousarray(v) for k, v in m.items()}


def run(inputs, NB=4, nlayers=2, debug=(), ncores=NCORES, stop=None):
    inp = {k: np.asarray(v) for k, v in inputs.items()}
    prog = Prog(NB=NB, nlayers=nlayers, debug=debug, stop=stop)
    nc = prog.build()
    shared = _shared_inputs(inp)
    in_maps = []
    for c in range(ncores):
        m = dict(shared)
        m.update(_prep_inputs(inp, NB, c))
        in_maps.append(m)
    res = run_bass_kernel_spmd(nc, in_maps, core_ids=list(range(ncores)))
    return prog, res


def kernel(**inputs):
    prog, res = run(inputs, NB=4, nlayers=2)
    out = np.concatenate([np.asarray(r['out']) for r in res.results], axis=0)
    return out.astype(np.float32)
```
